# Optimizing a Trainium2 kernel written in Bass

```python
import math
import jax, jax.numpy as jnp
from jax import lax
import numpy as np

D_MODEL = 1024
BATCH = 8
SEQ = 8192
DEPTH = 4

CTX_LEN = 256
GRID_W = 64
D_HYENA = D_MODEL // 2
D_RNN = D_MODEL // 2
D_MIX = D_HYENA + D_RNN
RNN_HEADS = 8
RNN_BLOCK = D_RNN // RNN_HEADS
D_IN = 4 * D_HYENA + 2 * D_RNN
HYENA_SHORT = 3
RNN_CONV = 4
FILTER_EMB = 33
FILTER_BANDS = (FILTER_EMB - 1) // 2
FILTER_HIDDEN = 64
FILTER_TARGET = 1e-2
FAST_DECAY = 0.3
SLOW_DECAY = 1.5
RG_C = 8.0
EPS = 1e-6

kernel_name = "hyena_rglru_hybrid_dit_block"


def rms_norm(x, g):
    xf = x.astype(jnp.float32)
    y = xf * lax.rsqrt(jnp.mean(xf * xf, axis=-1, keepdims=True) + EPS)
    return (y * g.astype(jnp.float32)).astype(x.dtype)


def grid_sincos(rows, dim):
    r, col = jnp.meshgrid(jnp.arange(rows, dtype=jnp.float32),
                          jnp.arange(GRID_W, dtype=jnp.float32), indexing='ij')
    quarter = dim // 4
    omega = 1.0 / (10000.0 ** (jnp.arange(quarter, dtype=jnp.float32) / quarter))

    def emb(pos):
        ang = pos.reshape(-1, 1) * omega[None, :]
        return jnp.concatenate([jnp.sin(ang), jnp.cos(ang)], axis=-1)

    return jnp.concatenate([emb(r), emb(col)], axis=-1)


def dwconv(u, w, pad):
    return lax.conv_general_dilated(u, w[:, None, :], window_strides=(1,), padding=[pad],
                                    dimension_numbers=('NWC', 'WIO', 'NWC'),
                                    feature_group_count=u.shape[-1])


def hyena_filter(L, f_w1, f_b1, f_w2, f_b2, f_w3, f_b3, f_w4, f_freq):
    t = jnp.linspace(0.0, 1.0, L, dtype=jnp.float32)[:, None]
    w = 2.0 * math.pi * jnp.arange(L, dtype=jnp.float32)[:, None] / L
    f = jnp.linspace(1e-4, FILTER_BANDS - 1, FILTER_BANDS, dtype=jnp.float32)[None, :]
    z = jnp.concatenate([t, jnp.cos(f * w), -jnp.sin(f * w)], axis=-1)
    fr = f_freq.astype(jnp.float32)
    h = jnp.sin(fr[0] * (z @ f_w1.astype(jnp.float32) + f_b1.astype(jnp.float32)))
    h = jnp.sin(fr[1] * (h @ f_w2.astype(jnp.float32) + f_b2.astype(jnp.float32)))
    h = jnp.sin(fr[2] * (h @ f_w3.astype(jnp.float32) + f_b3.astype(jnp.float32)))
    h = h @ f_w4.astype(jnp.float32)
    max_decay = math.log(FILTER_TARGET) / FAST_DECAY
    min_decay = math.log(FILTER_TARGET) / SLOW_DECAY
    deltas = jnp.abs(jnp.linspace(min_decay, max_decay, D_HYENA, dtype=jnp.float32))
    deltas = jnp.concatenate([deltas, deltas])
    h = h * jnp.exp(-t * deltas[None, :])
    k_f, k_b = h[:, :D_HYENA], h[:, D_HYENA:]
    l1 = jnp.sum(jnp.abs(k_f), axis=0) + jnp.sum(jnp.abs(k_b[1:]), axis=0)
    k_f = k_f / l1
    k_b = k_b / l1
    return jnp.concatenate([k_f, jnp.zeros((1, D_HYENA), jnp.float32), k_b[1:][::-1]], axis=0)


def long_conv(u, kernel, bias):
    L = u.shape[1]
    uf = jnp.fft.rfft(u.astype(jnp.float32), n=2 * L, axis=1)
    kf = jnp.fft.rfft(kernel, n=2 * L, axis=0)
    y = jnp.fft.irfft(uf * kf[None], n=2 * L, axis=1)[:, :L]
    return (y + u.astype(jnp.float32) * bias.astype(jnp.float32)).astype(u.dtype)


def rglru_coeffs(xc, wa, ba, wx, bx, lam):
    B, L, _ = xc.shape
    xh = xc.reshape(B, L, RNN_HEADS, RNN_BLOCK)
    r = jax.nn.sigmoid((jnp.einsum('blhi,hij->blhj', xh, wa).reshape(B, L, D_RNN) + ba).astype(jnp.float32))
    i = jax.nn.sigmoid((jnp.einsum('blhi,hij->blhj', xh, wx).reshape(B, L, D_RNN) + bx).astype(jnp.float32))
    log_a = -RG_C * r * jax.nn.softplus(-lam.astype(jnp.float32))
    a = jnp.exp(log_a)
    b = jnp.sqrt(-jnp.expm1(2.0 * log_a)) * (i * xc.astype(jnp.float32))
    return a, b


def linear_scan(a, b, h0, reverse):
    idx = -1 if reverse else 0
    b = b.at[:, idx].add(a[:, idx] * h0)

    def combine(e1, e2):
        a1, b1 = e1
        a2, b2 = e2
        return a1 * a2, a2 * b1 + b2

    _, h = lax.associative_scan(combine, (a, b), reverse=reverse, axis=1)
    return h


def rglru_bidir(xr, conv_w, conv_b, wa, ba, wx, bx, lam, h0_f, h0_b):
    xc = dwconv(xr, conv_w, (2, 1)) + conv_b
    a_f, b_f = rglru_coeffs(xc, wa[0], ba[0], wx[0], bx[0], lam[0])
    a_b, b_b = rglru_coeffs(xc, wa[1], ba[1], wx[1], bx[1], lam[1])
    hf = linear_scan(a_f, b_f, h0_f, False)
    hb = linear_scan(a_b, b_b, h0_b, True)
    return hf, hb


def mixer_out(p, y_rnn, kernel, hy_conv_w, hy_conv_b, hy_bias, gn_h, gn_r, w_out):
    u = dwconv(p[..., :3 * D_HYENA], hy_conv_w, (1, 1)) + hy_conv_b
    x0, x1, v = jnp.split(u, 3, axis=-1)
    y_h = long_conv(v * x1, kernel, hy_bias) * x0
    z_h = p[..., 3 * D_HYENA:4 * D_HYENA]
    z_r = p[..., 4 * D_HYENA + D_RNN:]
    y = jnp.concatenate([rms_norm(y_h, gn_h) * jax.nn.silu(z_h),
                         rms_norm(y_rnn.astype(p.dtype), gn_r) * jax.nn.silu(z_r)], axis=-1)
    return y @ w_out


def setup_inputs(seed: int = 0) -> dict:
    key = jax.random.key(seed)
    ks = jax.random.split(key, 32)
    nrm = lambda k, s: jax.random.normal(k, s, jnp.float32)
    u = jax.random.uniform(ks[27], (DEPTH, 2, D_RNN), jnp.float32, 0.81, 0.998)
    a0 = jnp.sqrt(u)
    rg_lam = -jnp.log(jnp.expm1(-jnp.log(a0) / RG_C))
    return {
        "x": nrm(ks[0], (BATCH, SEQ, D_MODEL)),
        "c": nrm(ks[1], (BATCH, D_MODEL)),
        "ctx": nrm(ks[2], (BATCH, CTX_LEN, D_MODEL)),
        "c_ctx": nrm(ks[3], (D_MODEL,)),
        "w_mod": nrm(ks[4], (DEPTH, D_MODEL, 3 * D_MODEL)) * 0.5 * D_MODEL ** -0.5,
        "b_mod": nrm(ks[5], (DEPTH, 3 * D_MODEL)) * 0.01,
        "norm_g": 1.0 + 0.05 * nrm(ks[6], (DEPTH, D_MODEL)),
        "w_in": nrm(ks[7], (DEPTH, D_MODEL, D_IN)) * D_MODEL ** -0.5,
        "hy_conv_w": nrm(ks[8], (DEPTH, HYENA_SHORT, 3 * D_HYENA)) * HYENA_SHORT ** -0.5,
        "hy_conv_b": nrm(ks[9], (DEPTH, 3 * D_HYENA)) * 0.01,
        "f_w1": nrm(ks[10], (DEPTH, FILTER_EMB, FILTER_HIDDEN)) * FILTER_EMB ** -0.5,
        "f_b1": nrm(ks[11], (DEPTH, FILTER_HIDDEN)) * 0.1,
        "f_w2": nrm(ks[12], (DEPTH, FILTER_HIDDEN, FILTER_HIDDEN)) * FILTER_HIDDEN ** -0.5,
        "f_b2": nrm(ks[13], (DEPTH, FILTER_HIDDEN)) * 0.1,
        "f_w3": nrm(ks[14], (DEPTH, FILTER_HIDDEN, FILTER_HIDDEN)) * FILTER_HIDDEN ** -0.5,
        "f_b3": nrm(ks[15], (DEPTH, FILTER_HIDDEN)) * 0.1,
        "f_w4": nrm(ks[16], (DEPTH, FILTER_HIDDEN, 2 * D_HYENA)) * FILTER_HIDDEN ** -0.5,
        "f_freq": 1.0 + 0.1 * nrm(ks[17], (DEPTH, 3, FILTER_HIDDEN)),
        "hy_bias": nrm(ks[18], (DEPTH, D_HYENA)) * 0.1,
        "rg_conv_w": nrm(ks[19], (DEPTH, RNN_CONV, D_RNN)) * RNN_CONV ** -0.5,
        "rg_conv_b": nrm(ks[20], (DEPTH, D_RNN)) * 0.01,
        "rg_wa": nrm(ks[21], (DEPTH, 2, RNN_HEADS, RNN_BLOCK, RNN_BLOCK)) * RNN_BLOCK ** -0.5,
        "rg_ba": nrm(ks[22], (DEPTH, 2, D_RNN)) * 0.01,
        "rg_wx": nrm(ks[23], (DEPTH, 2, RNN_HEADS, RNN_BLOCK, RNN_BLOCK)) * RNN_BLOCK ** -0.5,
        "rg_bx": nrm(ks[24], (DEPTH, 2, D_RNN)) * 0.01,
        "rg_lam": rg_lam,
        "br_norm_h": 1.0 + 0.05 * nrm(ks[25], (DEPTH, D_HYENA)),
        "br_norm_r": 1.0 + 0.05 * nrm(ks[26], (DEPTH, D_RNN)),
        "w_out": nrm(ks[28], (DEPTH, D_MIX, D_MODEL)) * D_MIX ** -0.5,
        "final_g": 1.0 + 0.05 * nrm(ks[29], (D_MODEL,)),
    }


def reference(x, c, ctx, c_ctx, w_mod, b_mod, norm_g, w_in, hy_conv_w, hy_conv_b,
              f_w1, f_b1, f_w2, f_b2, f_w3, f_b3, f_w4, f_freq, hy_bias,
              rg_conv_w, rg_conv_b, rg_wa, rg_ba, rg_wx, rg_bx, rg_lam,
              br_norm_h, br_norm_r, w_out, final_g):
    B, N, D = x.shape
    L_ctx = ctx.shape[1]
    rows = N // GRID_W
    xl = x + grid_sincos(rows, D).astype(x.dtype)[None]
    xc = ctx
    s_lat = jax.nn.silu(c)
    s_ctx = jax.nn.silu(c_ctx)
    r0, r1 = 4 * D_HYENA, 4 * D_HYENA + D_RNN
    for l in range(DEPTH):
        last = l == DEPTH - 1
        mod_l = s_lat @ w_mod[l] + b_mod[l]
        mod_c = s_ctx @ w_mod[l] + b_mod[l]
        sh_l, sc_l, g_l = jnp.split(mod_l[:, None, :], 3, axis=-1)
        sh_c, sc_c, g_c = jnp.split(mod_c, 3)
        pl = (rms_norm(xl, norm_g[l]) * (1.0 + sc_l) + sh_l) @ w_in[l]
        pc = (rms_norm(xc, norm_g[l]) * (1.0 + sc_c) + sh_c) @ w_in[l]
        rnn_p = (rg_conv_w[l], rg_conv_b[l], rg_wa[l], rg_ba[l], rg_wx[l], rg_bx[l], rg_lam[l])
        zero = jnp.zeros((B, D_RNN), jnp.float32)
        hf_c, hb_c = rglru_bidir(pc[..., r0:r1], *rnn_p, zero, zero)
        hf_l, hb_l = rglru_bidir(pl[..., r0:r1], *rnn_p, hf_c[:, -1], hb_c[:, 0])
        filt = (f_w1[l], f_b1[l], f_w2[l], f_b2[l], f_w3[l], f_b3[l], f_w4[l], f_freq[l])
        shared = (hy_conv_w[l], hy_conv_b[l], hy_bias[l], br_norm_h[l], br_norm_r[l], w_out[l])
        y_l = mixer_out(pl, hf_l + hb_l, hyena_filter(N, *filt), *shared)
        if not last:
            y_c = mixer_out(pc, hf_c + hb_c, hyena_filter(L_ctx, *filt), *shared)
            xc = xc + g_c * y_c
        xl = xl + g_l * y_l
    return rms_norm(xl, final_g)
```

```python
import math
from contextlib import ExitStack
import numpy as np
import ml_dtypes
import concourse.bass as bass
import concourse.mybir as mybir
from concourse.bass_utils import run_bass_kernel_spmd

F32 = mybir.dt.float32
BF16 = mybir.dt.bfloat16
ALU = mybir.AluOpType
AF = mybir.ActivationFunctionType
AX = mybir.AxisListType

D = 1024
DIN = 3072
NL = 8192
NCX = 256
DH = 512
DEPTH = 4
EPS = 1e-6
MFFT = 16384
MAGIC = 12582912.0
TWO_PI = 2.0 * math.pi


class Tl:
    __slots__ = ("t", "w", "r")

    def __init__(self, t):
        self.t = t
        self.w = None
        self.r = {}

    def __getitem__(self, k):
        return self.t[k]


class KB:
    def __init__(self, nc):
        self.nc = nc
        self.eng = {"pe": nc.tensor, "act": nc.scalar, "dve": nc.vector, "pool": nc.gpsimd, "sp": nc.sync}
        self.psem = {e: nc.alloc_semaphore(f"prog_{e}") for e in ("pe", "act", "dve", "pool")}
        self.pcnt = {e: 0 for e in self.psem}
        self.seen = {e: {} for e in self.eng}
        self.dq = {}
        for q, n in (("sp", 12), ("act", 6), ("pool", 6), ("dve", 2)):
            self.dq[q] = dict(sems=[nc.alloc_semaphore(f"dq_{q}{i}") for i in range(n)], cnt=[0] * n, idx=0)
        self.nins = 0

    def _wait(self, e, tok):
        key, val, sem = tok
        if self.seen[e].get(key, 0) >= val:
            return
        self.eng[e].wait_ge(sem, val)
        self.seen[e][key] = val
        self.nins += 1

    def _deps(self, e, R, W, dma=False):
        me = None if dma else "c:" + e
        for b in list(R) + list(W):
            t = b.w
            if t is not None and not (e == "pe" and t[0] == me):
                self._wait(e, t)
        for b in W:
            for t in b.r.values():
                if t[0] != me:
                    self._wait(e, t)

    def op(self, e, fn, R=(), W=()):
        self._deps(e, R, W)
        ins = fn()
        self.pcnt[e] += 1
        ins.then_inc(self.psem[e], 1)
        self.nins += 1
        tok = ("c:" + e, self.pcnt[e], self.psem[e])
        for b in W:
            b.w = tok
            b.r = {}
        for b in R:
            if b.w is not tok:
                b.r[tok[0]] = tok
        return tok

    def dma(self, q, out, in_, R=(), W=()):
        d = self.dq[q]
        e = q
        self._deps(e, R, W, True)
        i = d["idx"]
        d["idx"] = (i + 1) % len(d["sems"])
        sem = d["sems"][i]
        key = f"d:{q}:{i}"
        if d["cnt"][i] > 0:
            self._wait(e, (key, d["cnt"][i], sem))
        ins = self.eng[e].dma_start(out=out, in_=in_)
        d["cnt"][i] += 16
        ins.then_inc(sem, 16)
        self.nins += 1
        tok = (key, d["cnt"][i], sem)
        for b in W:
            b.w = tok
            b.r = {}
        for b in R:
            b.r[key] = tok
        return tok

    def barrier(self):
        toks = [("c:" + e, self.pcnt[e], self.psem[e]) for e in self.psem if self.pcnt[e] > 0]
        for q, d in self.dq.items():
            for i, sem in enumerate(d["sems"]):
                if d["cnt"][i] > 0:
                    toks.append((f"d:{q}:{i}", d["cnt"][i], sem))
        for e in self.eng:
            for t in toks:
                if t[0] != "c:" + e:
                    self._wait(e, t)

    def sbt(self, es, name, shape, dt):
        self.uid = getattr(self, "uid", 0) + 1
        return es.enter_context(self.nc.sbuf_tensor(f"s{self.uid}_{name}", list(shape), dt))

    def sb(self, es, name, shape, dt):
        return Tl(self.sbt(es, name, shape, dt))

    def ps(self, es, name, shape, dt):
        self.uid = getattr(self, "uid", 0) + 1
        return Tl(es.enter_context(self.nc.psum_tensor(f"p{self.uid}_{name}", list(shape), dt)))


_CONST_CACHE = {}


def _bf(a):
    return np.ascontiguousarray(a.astype(np.float32)).astype(ml_dtypes.bfloat16)


def host_consts():
    if _CONST_CACHE:
        return _CONST_CACHE
    c = {}
    rows, gw = NL // 64, 64
    r, col = np.meshgrid(np.arange(rows, dtype=np.float32), np.arange(gw, dtype=np.float32), indexing="ij")
    quarter = D // 4
    omega = (1.0 / (10000.0 ** (np.arange(quarter, dtype=np.float32) / np.float32(quarter)))).astype(np.float32)

    def emb(pos):
        ang = pos.reshape(-1, 1).astype(np.float32) * omega[None, :]
        return np.concatenate([np.sin(ang), np.cos(ang)], axis=-1)

    c["pos"] = np.concatenate([emb(r), emb(col)], axis=-1).astype(np.float32)
    c["ident_bf"] = _bf(np.eye(128))
    c["ident_f"] = np.eye(128, dtype=np.float32)
    c["ones_bf"] = _bf(np.ones((128, 128)))
    a = np.arange(128, dtype=np.float64)[:, None, None]
    b = np.arange(128, dtype=np.float64)[None, :, None]
    k1 = np.arange(64, dtype=np.float64)[None, None, :]
    th = TWO_PI * (128 * a + b) * (k1 + 0.5) / MFFT
    c["FA"] = _bf(np.concatenate([np.cos(th), -np.sin(th)], axis=-1).reshape(128, 128 * 128))
    a64 = np.arange(64, dtype=np.float64)[None, None, :]
    bb = np.arange(128, dtype=np.float64)[None, :, None]
    kk = np.arange(64, dtype=np.float64)[:, None, None]
    th2 = TWO_PI * (128 * a64 + bb) * (kk + 0.5) / MFFT
    c["GA"] = _bf(np.concatenate([(2.0 / MFFT) * np.cos(th2), -(2.0 / MFFT) * np.sin(th2)], axis=0).reshape(128, 128 * 64))
    ph = TWO_PI * np.outer(np.arange(128.0), np.arange(128.0)) / 128.0
    c["F128"] = _bf(np.stack([np.cos(ph), np.sin(ph), -np.sin(ph), -np.cos(ph)], axis=1).reshape(128, 4 * 128))

    def feats(L):
        t = np.linspace(0.0, 1.0, L, dtype=np.float32)[:, None]
        w = (2.0 * math.pi * np.arange(L, dtype=np.float32)[:, None] / L).astype(np.float32)
        f = np.linspace(1e-4, 15, 16, dtype=np.float32)[None, :]
        z = np.concatenate([t, np.cos(f * w), -np.sin(f * w)], axis=-1).astype(np.float32)
        return np.ascontiguousarray(z.T), t[:, 0]

    c["zT_l"], tl = feats(NL)
    c["zT_c"], tcx = feats(NCX)
    c["tg_l"] = np.ascontiguousarray(np.broadcast_to(tl[None, :], (128, NL))).astype(np.float32)
    c["tg_c"] = np.ascontiguousarray(np.broadcast_to(tcx[None, :], (128, NCX))).astype(np.float32)
    max_decay = math.log(1e-2) / 0.3
    min_decay = math.log(1e-2) / 1.5
    deltas = np.abs(np.linspace(min_decay, max_decay, DH, dtype=np.float32))
    c["ndel"] = np.ascontiguousarray((-deltas).reshape(4, 128).T).astype(np.float32)
    n = np.arange(512, dtype=np.float64)[:, None]
    k = np.arange(256, dtype=np.float64)[None, :]
    th = TWO_PI * n * (k + 0.5) / 512.0
    Fc = np.concatenate([np.cos(th), -np.sin(th)], axis=1)
    c["Fc"] = _bf(Fc.reshape(4, 128, 512).transpose(1, 0, 2).reshape(128, 4 * 512))
    t = np.arange(256, dtype=np.float64)[None, :]
    kk_ = np.arange(256, dtype=np.float64)[:, None]
    th2 = TWO_PI * t * (kk_ + 0.5) / 512.0
    Gc = np.concatenate([(2.0 / 512) * np.cos(th2), -(2.0 / 512) * np.sin(th2)], axis=0)
    c["Gc"] = _bf(Gc.reshape(4, 128, 256).transpose(1, 0, 2).reshape(128, 4 * 256))
    _CONST_CACHE.update(c)
    return c


PP_HCW = 0
PP_HCB = 36
PP_HBIAS = 48
PP_GNH = 52
PP_GNR = 56
PP_RCW = 60
PP_RCB = 76
PP_RB = 80
PP_LAM = 96
NPP = 104


class NS:
    pass


def _chunked(v, nchunk):
    return np.ascontiguousarray(v.reshape(nchunk, 128).T)


def pack_inputs(inp, b):
    C = host_consts()
    f32 = np.float32
    m = {}
    m["x"] = np.ascontiguousarray(inp["x"][b], dtype=f32)
    m["ctx"] = np.ascontiguousarray(inp["ctx"][b], dtype=f32)
    cs = np.stack([inp["c"][b], inp["c_ctx"]], axis=-1).astype(f32)
    m["cs"] = np.ascontiguousarray(cs.reshape(8, 128, 2).transpose(1, 0, 2))
    for k in ("w_mod", "b_mod", "norm_g", "w_in", "w_out", "final_g", "f_w1", "f_w2", "f_w3", "f_w4"):
        m[k] = np.ascontiguousarray(inp[k], dtype=f32)
    pp = np.zeros((DEPTH, 128, NPP), f32)
    bd = np.zeros((DEPTH, 2, 2, 4, 128, 128), f32)
    fp = np.zeros((DEPTH, 64, 6), f32)
    for l in range(DEPTH):
        hcw = inp["hy_conv_w"][l]
        pp[l, :, PP_HCW:PP_HCW + 36] = hcw.T.reshape(12, 128, 3).transpose(1, 0, 2).reshape(128, 36)
        pp[l, :, PP_HCB:PP_HCB + 12] = _chunked(inp["hy_conv_b"][l], 12)
        pp[l, :, PP_HBIAS:PP_HBIAS + 4] = _chunked(inp["hy_bias"][l], 4)
        pp[l, :, PP_GNH:PP_GNH + 4] = _chunked(inp["br_norm_h"][l], 4)
        pp[l, :, PP_GNR:PP_GNR + 4] = _chunked(inp["br_norm_r"][l], 4)
        rcw = inp["rg_conv_w"][l]
        pp[l, :, PP_RCW:PP_RCW + 16] = rcw.T.reshape(4, 128, 4).transpose(1, 0, 2).reshape(128, 16)
        pp[l, :, PP_RCB:PP_RCB + 4] = _chunked(inp["rg_conv_b"][l], 4)
        for d in range(2):
            for g, (wk, bk) in enumerate((("rg_wa", "rg_ba"), ("rg_wx", "rg_bx"))):
                pp[l, :, PP_RB + (d * 2 + g) * 4: PP_RB + (d * 2 + g) * 4 + 4] = _chunked(inp[bk][l, d], 4)
                w = inp[wk][l, d]
                for ch in range(4):
                    for hh in range(2):
                        bd[l, d, g, ch, hh * 64:(hh + 1) * 64, hh * 64:(hh + 1) * 64] = w[ch * 2 + hh]
            pp[l, :, PP_LAM + d * 4: PP_LAM + d * 4 + 4] = _chunked(inp["rg_lam"][l, d], 4)
        fp[l, :, 0] = inp["f_b1"][l]
        fp[l, :, 1] = inp["f_b2"][l]
        fp[l, :, 2] = inp["f_b3"][l]
        fp[l, :, 3:6] = inp["f_freq"][l].T
    m["pp"] = np.ascontiguousarray(pp.transpose(1, 0, 2))
    m["bd"] = np.ascontiguousarray(bd.transpose(0, 1, 2, 4, 3, 5))
    m["fp"] = np.ascontiguousarray(fp.transpose(1, 0, 2))
    for k in ("pos", "ident_bf", "ident_f", "ones_bf", "FA", "GA", "F128", "Fc", "Gc", "zT_l", "zT_c", "tg_l", "tg_c", "ndel"):
        m[k] = C[k]
    return m


def build(dbg=(), phases=None, depth=DEPTH):
    nc = bass.Bass("TRN2", target_bir_lowering=False)
    kb = KB(nc)
    G = NS()
    G.nc = nc
    G.kb = kb

    def inp(name, shape, dt=F32):
        return nc.dram_tensor(name, list(shape), dt, kind="ExternalInput").ap()

    def scr(name, shape, dt):
        kind = "ExternalOutput" if name in dbg else "Internal"
        return nc.dram_tensor(name, list(shape), dt, kind=kind).ap()

    G.x = inp("x", [NL, D]); G.ctx = inp("ctx", [NCX, D]); G.cs = inp("cs", [128, 8, 2]); G.pos = inp("pos", [NL, D])
    G.w_mod = inp("w_mod", [DEPTH, D, 3 * D]); G.b_mod = inp("b_mod", [DEPTH, 3 * D]); G.norm_g = inp("norm_g", [DEPTH, D])
    G.w_in = inp("w_in", [DEPTH, D, DIN]); G.w_out = inp("w_out", [DEPTH, D, D]); G.final_g = inp("final_g", [D])
    G.f_w1 = inp("f_w1", [DEPTH, 33, 64]); G.f_w2 = inp("f_w2", [DEPTH, 64, 64]); G.f_w3 = inp("f_w3", [DEPTH, 64, 64])
    G.f_w4 = inp("f_w4", [DEPTH, 64, 1024])
    G.pp_d = inp("pp", [128, DEPTH, NPP]); G.bd = inp("bd", [DEPTH, 2, 2, 128, 4, 128]); G.fp_d = inp("fp", [64, DEPTH, 6])
    G.ident_d = inp("ident_bf", [128, 128], BF16); G.ones_d = inp("ones_bf", [128, 128], BF16)
    G.FA_d = inp("FA", [128, 128 * 128], BF16); G.GA_d = inp("GA", [128, 128 * 64], BF16); G.F128_d = inp("F128", [128, 512], BF16)
    G.zT = {NL: inp("zT_l", [33, NL]), NCX: inp("zT_c", [33, NCX])}
    G.tg = {NL: inp("tg_l", [128, NL]), NCX: inp("tg_c", [128, NCX])}
    G.ndel_d = inp("ndel", [128, 4])
    G.identf_d = inp("ident_f", [128, 128]); G.Fc_d = inp("Fc", [128, 2048], BF16); G.Gc_d = inp("Gc", [128, 1024], BF16)
    G.out = nc.dram_tensor("out", [NL, D], F32, kind="ExternalOutput").ap()

    G.modd = scr("modd", [DEPTH, 2, 3, D], F32)
    G.lat = NS(); G.cx = NS()
    for s, N, nm, idx in ((G.lat, NL, "l", 0), (G.cx, NCX, "c", 1)):
        s.N = N; s.idx = idx; s.nm = nm
        s.X = scr("X" + nm, [N, D], F32)
        s.P = scr("P" + nm, [DIN, N], BF16)
        s.HB = scr("HB" + nm, [DH, N], F32)
        s.YR = scr("YR" + nm, [DH, N], F32)
        s.YH = scr("YH" + nm, [DH, N], F32)
    G.KH = scr("KH", [4, 128, 16, 2, 512], F32)
    G.dbgc = scr("DBGC", [4, 128, 3, 512], F32) if "DBGC" in dbg else None
    G.UT = scr("UT", [DH, NL], BF16)

    with ExitStack() as es0:
        G.ident = kb.sb(es0, "ident", [128, 128], BF16)
        G.ones = kb.sb(es0, "ones", [128, 128], BF16)
        G.pp = kb.sb(es0, "pp", [128, DEPTH, NPP], F32)
        G.hl = kb.sb(es0, "hl", [128, DEPTH, 8], F32)
        G.hrb = kb.sb(es0, "hrb", [128, DEPTH, 16], F32)
        G.h0 = kb.sb(es0, "h0", [128, 4, 2], F32)
        G.rl1 = kb.sb(es0, "rl1", [128, 4], F32)
        G.ndel = kb.sb(es0, "ndel", [128, 4], F32)
        G.fpp = kb.sb(es0, "fpp", [64, DEPTH, 6], F32)
        G.frb = kb.sb(es0, "frb", [64, DEPTH, 3], F32)
        kb.dma("sp", G.ident[:], G.ident_d, W=[G.ident])
        kb.dma("sp", G.ones[:], G.ones_d, W=[G.ones])
        kb.dma("sp", G.pp[:], G.pp_d, W=[G.pp])
        kb.dma("sp", G.ndel[:], G.ndel_d, W=[G.ndel])
        kb.dma("sp", G.fpp[:], G.fp_d, W=[G.fpp])

        P = phases
        ph_prologue(G)
        kb.barrier()
        esW = None; preW = None
        for l in range(depth):
            last = l == DEPTH - 1
            if P is None or "inproj" in P:
                ph_inproj(G, l, preW)
                kb.barrier()
                if esW is not None:
                    esW.close(); esW = None; preW = None
            if P is None or "rglru" in P:
                ph_rglru(G, l, G.cx)
                kb.barrier()
                ph_rglru(G, l, G.lat)
                kb.barrier()
            if P is None or "hyena" in P or "hy_f" in P:
                ph_filter_lat(G, l)
                kb.barrier()
            if P is None or "hyena" in P or "hy_l" in P:
                ph_hyena_lat(G, l)
                kb.barrier()
            if (P is None or "hyena" in P or "hy_c" in P) and not last:
                ph_hyena_ctx(G, l)
                kb.barrier()
            if P is None or "out" in P:
                if not last:
                    ph_out(G, l, G.cx, False)
                    kb.barrier()
                if P is None and not last and l + 1 < depth:
                    esW = ExitStack()
                    preW = load_w_in(G, l + 1, esW)
                ph_out(G, l, G.lat, last)
                kb.barrier()
        kb.barrier()
    return nc


def ph_prologue(G):
    nc, kb = G.nc, G.kb
    with ExitStack() as es:
        t8 = kb.sb(es, "t8", [128, DEPTH, 8], F32)
        kb.op("act", lambda: nc.scalar.activation(out=t8[:], in_=G.pp[:, :, PP_LAM:PP_LAM + 8], func=AF.Exp, scale=-1.0), R=[G.pp], W=[t8])
        t8b = kb.sb(es, "t8b", [128, DEPTH, 8], F32)
        t8c = kb.sb(es, "t8c", [128, DEPTH, 8], F32)
        kb.op("dve", lambda: nc.vector.tensor_scalar(out=t8b[:], in0=t8[:], scalar1=2.0, scalar2=None, op0=ALU.add), R=[t8], W=[t8b])
        kb.op("dve", lambda: nc.vector.reciprocal(out=t8b[:], in_=t8b[:]), R=[t8b], W=[t8b])
        kb.op("dve", lambda: nc.vector.tensor_tensor(out=t8[:], in0=t8[:], in1=t8b[:], op=ALU.mult), R=[t8, t8b], W=[t8])
        kb.op("dve", lambda: nc.vector.tensor_tensor(out=t8b[:], in0=t8[:], in1=t8[:], op=ALU.mult), R=[t8], W=[t8b])
        kb.op("dve", lambda: nc.vector.tensor_scalar(out=t8c[:], in0=t8b[:], scalar1=0.2, scalar2=1.0 / 3.0, op0=ALU.mult, op1=ALU.add), R=[t8b], W=[t8c])
        kb.op("dve", lambda: nc.vector.tensor_tensor(out=t8c[:], in0=t8c[:], in1=t8b[:], op=ALU.mult), R=[t8c, t8b], W=[t8c])
        kb.op("dve", lambda: nc.vector.scalar_tensor_tensor(out=G.hl[:], in0=t8c[:], scalar=1.0, in1=t8[:], op0=ALU.add, op1=ALU.mult), R=[t8c, t8], W=[G.hl])
        kb.op("dve", lambda: nc.vector.tensor_scalar(out=G.hl[:], in0=G.hl[:], scalar1=-8.0, scalar2=None, op0=ALU.mult), R=[G.hl], W=[G.hl])
        kb.op("dve", lambda: nc.vector.tensor_scalar(out=G.hrb[:], in0=G.pp[:, :, PP_RB:PP_RB + 16], scalar1=0.5, scalar2=None, op0=ALU.mult), R=[G.pp], W=[G.hrb])
        kb.op("dve", lambda: nc.vector.tensor_tensor(out=G.frb[:], in0=G.fpp[:, :, 0:3], in1=G.fpp[:, :, 3:6], op=ALU.mult), R=[G.fpp], W=[G.frb])
        cs = kb.sb(es, "cs", [128, 8, 2], F32)
        S = kb.sb(es, "S", [128, 8, 2], BF16)
        kb.dma("sp", cs[:], G.cs, W=[cs])
        kb.op("act", lambda: nc.scalar.activation(out=S[:], in_=cs[:], func=AF.Silu), R=[cs], W=[S])
        wm = [kb.sb(es, f"wm{i}", [128, 8, 512], BF16) for i in range(2)]
        bm = kb.sb(es, "bm", [2, 3 * D], F32)
        ng = kb.sb(es, "ng", [2, D], F32)
        mods = kb.sb(es, "mods", [2, 3 * D], F32)
        pp_ = [kb.ps(es, f"pp{i}", [128, 512], F32) for i in range(2)]
        modd_t = Tl(G.modd)
        it = 0
        for l in range(DEPTH):
            kb.dma("sp", bm[:], G.b_mod[l].partition_broadcast(2), W=[bm])
            kb.dma("sp", ng[:], G.norm_g[l].partition_broadcast(2), W=[ng])
            for n in range(6):
                w = wm[it % 2]; p = pp_[it % 2]; it += 1
                kb.dma("pool", w[:], G.w_mod[l][:, n * 512:(n + 1) * 512].rearrange("(k p) n -> p k n", p=128), W=[w])
                for kc in range(8):
                    kb.op("pe", lambda: nc.tensor.matmul(p[0:2, :], lhsT=S[:, kc, :], rhs=w[:, kc, :], start=(kc == 0), stop=(kc == 7)),
                          R=[S, w], W=[p])
                kb.op("dve", lambda: nc.vector.tensor_tensor(out=mods[:, n * 512:(n + 1) * 512], in0=p[0:2, :], in1=bm[:, n * 512:(n + 1) * 512], op=ALU.add),
                      R=[p, bm], W=[mods])
            kb.op("dve", lambda: nc.vector.scalar_tensor_tensor(out=mods[:, D:2 * D], in0=mods[:, D:2 * D], scalar=1.0, in1=ng[:], op0=ALU.add, op1=ALU.mult),
                  R=[mods, ng], W=[mods])
            kb.dma("sp", G.modd[l].rearrange("s r d -> s (r d)"), mods[:], R=[mods], W=[modd_t])


def load_w_in(G, l, es):
    kb = G.kb
    W = [kb.sb(es, f"W{i}", [128, DIN], BF16) for i in range(8)]
    for i in range(8):
        kb.dma("pool", W[i][:], G.w_in[l, i * 128:(i + 1) * 128, :], W=[W[i]])
    return W


def ph_inproj(G, l, W=None):
    nc, kb = G.nc, G.kb
    with ExitStack() as es:
        if W is None:
            W = load_w_in(G, l, es)
        gsc = {}; sh = {}
        for s in (G.cx, G.lat):
            gsc[s.idx] = kb.sb(es, f"gsc{s.idx}", [128, D], F32)
            sh[s.idx] = kb.sb(es, f"sh{s.idx}", [128, D], F32)
            kb.dma("sp", sh[s.idx][:], G.modd[l, s.idx, 0].partition_broadcast(128), W=[sh[s.idx]])
            kb.dma("sp", gsc[s.idx][:], G.modd[l, s.idx, 1].partition_broadcast(128), W=[gsc[s.idx]])
        xin = [kb.sb(es, f"xin{i}", [128, 4, D], F32) for i in range(2)]
        ptile = [kb.sb(es, f"ptile{i}", [128, 4, D], F32) for i in range(2)] if l == 0 else None
        junk = kb.sb(es, "junk", [128, D], BF16)
        ss = [kb.sb(es, f"ss{i}", [128, 4], F32) for i in range(2)]
        tmp = [kb.sb(es, f"tmp{i}", [128, D], F32) for i in range(2)]
        xs_t = [kb.sbt(es, f"xs{i}", [128, 4, D], BF16) for i in range(2)]
        xs = [[Tl(xs_t[i][:, k, :]) for k in range(4)] for i in range(2)]
        xT_t = [kb.sbt(es, f"xT{i}", [128, 8, 512], BF16) for i in range(2)]
        xT = [[Tl(xT_t[i][:, dc, :]) for dc in range(8)] for i in range(2)]
        po_t = [kb.sbt(es, f"po{i}", [128, 4, 512], BF16) for i in range(3)]
        po = [[Tl(po_t[i][:, j, :]) for j in range(4)] for i in range(3)]
        psT = [kb.ps(es, f"psT{i}", [128, 512], BF16) for i in range(2)]
        psm = [kb.ps(es, f"psm{i}", [128, 512], F32) for i in range(4)]
        tiles = [(s, it) for s in (G.cx, G.lat) for it in range(s.N // min(512, s.N))]
        st = dict(ev=0, gi=0, tk=0)

        def stage1(idx):
            s, it = tiles[idx]
            TT = min(512, s.N); nb = TT // 128; t0 = it * TT
            xi = xin[idx % 2]; sq = ss[idx % 2]; xsi = xs[idx % 2]; xTi = xT[idx % 2]
            g_, h_ = gsc[s.idx], sh[s.idx]
            src = s.X if l > 0 else (G.x if s.idx == 0 else G.ctx)
            kb.dma("sp", xi[:, 0:nb, :], src[t0:t0 + TT, :].rearrange("(k p) d -> p k d", p=128), W=[xi])
            if l == 0:
                if s.idx == 0:
                    pz = ptile[idx % 2]
                    kb.dma("sp", pz[:, 0:nb, :], G.pos[t0:t0 + TT, :].rearrange("(k p) d -> p k d", p=128), W=[pz])
                    kb.op("pool", lambda: nc.gpsimd.tensor_tensor(out=xi[:, 0:nb, :], in0=xi[:, 0:nb, :], in1=pz[:, 0:nb, :], op=ALU.add), R=[xi, pz], W=[xi])
                kb.dma("sp", s.X[t0:t0 + TT, :].rearrange("(k p) d -> p k d", p=128), xi[:, 0:nb, :], R=[xi])
            kb.op("pool", lambda: nc.gpsimd.memset(sq[:], 0.0), W=[sq])
            for k in range(nb):
                kb.op("act", lambda: nc.scalar.activation(out=junk[:], in_=xi[:, k, :], func=AF.Square, accum_out=sq[:, k:k + 1]),
                      R=[xi], W=[junk, sq])
            kb.op("act", lambda: nc.scalar.activation(out=sq[:], in_=sq[:], func=AF.Sqrt, bias=EPS, scale=1.0 / D), R=[sq], W=[sq])
            kb.op("dve", lambda: nc.vector.reciprocal(out=sq[:], in_=sq[:]), R=[sq], W=[sq])
            for k in range(nb):
                tm = tmp[st["tk"] % 2]; st["tk"] += 1
                kb.op("dve", lambda: nc.vector.scalar_tensor_tensor(out=tm[:], in0=xi[:, k, :], scalar=sq[:, k:k + 1], in1=g_[:], op0=ALU.mult, op1=ALU.mult),
                      R=[xi, sq, g_], W=[tm])
                kb.op("pool", lambda: nc.gpsimd.tensor_tensor(out=xsi[k][:], in0=tm[:], in1=h_[:], op=ALU.add), R=[tm, h_], W=[xsi[k]])
            for dc in range(8):
                pt = psT[dc % 2]
                for k in range(nb):
                    kb.op("pe", lambda: nc.tensor.transpose(out=pt[:, k * 128:(k + 1) * 128], in_=xsi[k][:, dc * 128:(dc + 1) * 128], identity=G.ident[:]),
                          R=[xsi[k], G.ident], W=[pt])
                if dc % 2:
                    kb.op("act", lambda: nc.scalar.copy(out=xTi[dc][:, 0:TT], in_=pt[:, 0:TT]), R=[pt], W=[xTi[dc]])
                else:
                    kb.op("dve", lambda: nc.vector.tensor_copy(out=xTi[dc][:, 0:TT], in_=pt[:, 0:TT]), R=[pt], W=[xTi[dc]])

        def stage2(idx):
            s, it = tiles[idx]
            TT = min(512, s.N); t0 = it * TT
            xTi = xT[idx % 2]
            for cc in range(24):
                pm = psm[cc % 4]
                for dc in range(8):
                    kb.op("pe", lambda: nc.tensor.matmul(pm[:, 0:TT], lhsT=W[dc][:, cc * 128:(cc + 1) * 128], rhs=xTi[dc][:, 0:TT], start=(dc == 0), stop=(dc == 7)),
                          R=[W[dc], xTi[dc]], W=[pm])
                pg = po[st["gi"] % 3]; j = cc % 4
                if st["ev"] % 2:
                    kb.op("act", lambda: nc.scalar.copy(out=pg[j][:, 0:TT], in_=pm[:, 0:TT]), R=[pm], W=[pg[j]])
                else:
                    kb.op("dve", lambda: nc.vector.tensor_copy(out=pg[j][:, 0:TT], in_=pm[:, 0:TT]), R=[pm], W=[pg[j]])
                st["ev"] += 1
                if j == 3:
                    c0 = (cc - 3) * 128
                    kb.dma("act", s.P[c0:c0 + 512, t0:t0 + TT].rearrange("(k p) t -> p k t", p=128), po_t[st["gi"] % 3][:, :, 0:TT], R=pg)
                    st["gi"] += 1

        stage1(0)
        for i in range(len(tiles)):
            if i + 1 < len(tiles):
                stage1(i + 1)
            stage2(i)


def ph_rglru(G, l, s):
    nc, kb = G.nc, G.kb
    N = s.N; T2 = min(1024, N); nt = N // T2
    NS = 4; LA = 3
    SUB = min(512, T2); nsub = T2 // SUB
    is_ctx = s.idx == 1
    with ExitStack() as es:
        bdt = [[kb.sb(es, f"bd{d}{g}", [128, 4, 128], BF16) for g in range(2)] for d in range(2)]
        for d in range(2):
            for g in range(2):
                kb.dma("pool", bdt[d][g][:], G.bd[l, d, g], W=[bdt[d][g]])
        dg_t = [kb.sbt(es, f"rdg{i}", [128, 4, 128], BF16) for i in range(2)]
        dg = [Tl(t) for t in dg_t]
        pin = [kb.sb(es, f"pin{i}", [128, T2 + 3], BF16) for i in range(2)]
        xcf_t = [kb.sbt(es, f"xcf{i}", [128, N], F32) for i in range(2)]
        xcb_t = [kb.sbt(es, f"xcbf{i}", [128, N], BF16) for i in range(2)]
        xcf = [[Tl(xcf_t[j][:, i * T2:(i + 1) * T2]) for i in range(nt)] for j in range(2)]
        xcb = [[Tl(xcb_t[j][:, i * T2:(i + 1) * T2]) for i in range(nt)] for j in range(2)]
        thr = [kb.sb(es, f"thr{i}", [128, T2], F32) for i in range(NS)]
        thi = [kb.sb(es, f"thi{i}", [128, T2], F32) for i in range(NS)]
        sq = [kb.sb(es, f"sq{i}", [128, T2], F32) for i in range(NS)]
        hh = [kb.sb(es, f"hh{i}", [128, T2], F32) for i in range(NS)]
        hbl = [kb.sb(es, f"hbl{i}", [128, T2], F32) for i in range(NS)]
        st = kb.sb(es, "st", [128, 1], F32)
        psg = [kb.ps(es, f"psg{i}", [128, 512], F32) for i in range(6)]
        hbt = {}
        cnt = dict(pi=0, pn=0)
        items = []
        for ch in range(4):
            for d in (1, 0):
                order = list(range(nt - 1, -1, -1)) if d == 1 else list(range(nt))
                for j, ti in enumerate(order):
                    items.append((ch, d, ti, j == 0))
        conv_done = set()

        def conv(ch):
            cb = G.pp[:, l, PP_RCB + ch: PP_RCB + ch + 1]
            dgt = dg_t[ch % 2]; dgl = dg[ch % 2]
            for k in range(4):
                col = PP_RCW + ch * 4 + k
                kb.op("dve", lambda: nc.vector.tensor_scalar(out=dgt[:, k, :], in0=G.ident[:], scalar1=G.pp[:, l, col:col + 1], scalar2=None, op0=ALU.mult), R=[G.ident, G.pp], W=[dgl])
            for ti in range(nt):
                t0 = ti * T2
                p = pin[cnt["pn"] % 2]; cnt["pn"] += 1
                lo = max(t0 - 2, 0); hi = min(t0 + T2 + 1, N)
                if t0 == 0:
                    kb.op("pool", lambda: nc.gpsimd.memset(p[:, 0:2], 0.0), W=[p])
                if t0 + T2 == N:
                    kb.op("pool", lambda: nc.gpsimd.memset(p[:, T2 + 2:T2 + 3], 0.0), W=[p])
                kb.dma("sp", p[:, lo - (t0 - 2): hi - (t0 - 2)], s.P[2048 + ch * 128: 2048 + (ch + 1) * 128, lo:hi], W=[p])
                xf = xcf[ch % 2][ti]; xb = xcb[ch % 2][ti]
                for sb_ in range(nsub):
                    pc = psg[cnt["pi"] % 6]; cnt["pi"] += 1
                    for k in range(4):
                        kb.op("pe", lambda: nc.tensor.matmul(pc[:, 0:SUB], lhsT=dgt[:, k, :], rhs=p[:, sb_ * SUB + k: sb_ * SUB + k + SUB], start=(k == 0), stop=(k == 3)), R=[dgl, p], W=[pc])
                    sl = slice(sb_ * SUB, (sb_ + 1) * SUB)
                    kb.op("act", lambda: nc.scalar.activation(out=xf[:, sl], in_=pc[:, 0:SUB], func=AF.Identity, bias=cb, scale=1.0), R=[pc, G.pp], W=[xf])
                    kb.op("act", lambda: nc.scalar.activation(out=xb[:, sl], in_=pc[:, 0:SUB], func=AF.Identity, bias=cb, scale=1.0), R=[pc, G.pp], W=[xb])

        def stageA(i):
            ch, d, ti, first = items[i]
            if ch not in conv_done:
                conv(ch); conv_done.add(ch)
            slot = i % NS
            t0 = ti * T2
            x = xcf[ch % 2][ti]; xb = xcb[ch % 2][ti]; tr = thr[slot]; tq = thi[slot]; s2 = sq[slot]; hb = hbl[slot]
            hl = G.hl[:, l, d * 4 + ch: d * 4 + ch + 1]
            hba = G.hrb[:, l, (d * 2 + 0) * 4 + ch: (d * 2 + 0) * 4 + ch + 1]
            hbx = G.hrb[:, l, (d * 2 + 1) * 4 + ch: (d * 2 + 1) * 4 + ch + 1]
            for sb_ in range(nsub):
                sl = slice(sb_ * SUB, (sb_ + 1) * SUB)
                pr = psg[cnt["pi"] % 6]; pq = psg[(cnt["pi"] + 1) % 6]; cnt["pi"] += 2
                kb.op("pe", lambda: nc.tensor.matmul(pr[:, 0:SUB], lhsT=bdt[d][0][:, ch, :], rhs=xb[:, sl], start=True, stop=True), R=[bdt[d][0], xb], W=[pr])
                kb.op("pe", lambda: nc.tensor.matmul(pq[:, 0:SUB], lhsT=bdt[d][1][:, ch, :], rhs=xb[:, sl], start=True, stop=True), R=[bdt[d][1], xb], W=[pq])
                kb.op("act", lambda: nc.scalar.activation(out=tr[:, sl], in_=pr[:, 0:SUB], func=AF.Tanh, bias=hba, scale=0.5), R=[pr, G.hrb], W=[tr])
                kb.op("act", lambda: nc.scalar.activation(out=tq[:, sl], in_=pq[:, 0:SUB], func=AF.Tanh, bias=hbx, scale=0.5), R=[pq, G.hrb], W=[tq])
            kb.op("act", lambda: nc.scalar.activation(out=tr[:], in_=tr[:], func=AF.Exp, bias=hl, scale=hl), R=[tr, G.hl], W=[tr])
            kb.op("pool", lambda: nc.gpsimd.tensor_tensor(out=s2[:], in0=tr[:], in1=tr[:], op=ALU.mult), R=[tr], W=[s2])

        def stageA2(i):
            ch, d, ti, first = items[i]
            slot = i % NS
            x = xcf[ch % 2][ti]; tq = thi[slot]; s2 = sq[slot]
            kb.op("act", lambda: nc.scalar.activation(out=s2[:], in_=s2[:], func=AF.Sqrt, bias=1.0, scale=-1.0), R=[s2], W=[s2])
            kb.op("dve", lambda: nc.vector.scalar_tensor_tensor(out=tq[:], in0=tq[:], scalar=1.0, in1=x[:], op0=ALU.add, op1=ALU.mult), R=[tq, x], W=[tq])
            kb.op("dve", lambda: nc.vector.scalar_tensor_tensor(out=tq[:], in0=tq[:], scalar=0.5, in1=s2[:], op0=ALU.mult, op1=ALU.mult), R=[tq, s2], W=[tq])

        def stageB(i):
            ch, d, ti, first = items[i]
            slot = i % NS
            t0 = ti * T2
            tr = thr[slot]; tq = thi[slot]; h = hh[slot]; hb = hbl[slot]
            if d == 0:
                kb.dma("sp", hb[:], s.HB[ch * 128:(ch + 1) * 128, t0:t0 + T2], R=[hbt[(ch, ti)]], W=[hb])
            if first:
                init = 0.0 if is_ctx else G.h0[:, ch, (1 if d == 1 else 0):(2 if d == 1 else 1)]
                rinit = [] if is_ctx else [G.h0]
            else:
                init = st[:, 0:1]; rinit = [st]
            if d == 1:
                kb.op("dve", lambda: nc.vector.tensor_tensor_scan(out=h[:, ::-1], data0=tr[:, ::-1], data1=tq[:, ::-1], initial=init, op0=ALU.mult, op1=ALU.add),
                      R=[tr, tq] + rinit, W=[h])
                kb.op("dve", lambda: nc.vector.tensor_copy(out=st[:], in_=h[:, 0:1]), R=[h], W=[st])
                hbt[(ch, ti)] = Tl(None)
                kb.dma("sp", s.HB[ch * 128:(ch + 1) * 128, t0:t0 + T2], h[:], R=[h], W=[hbt[(ch, ti)]])
                if is_ctx and ti == 0:
                    kb.op("dve", lambda: nc.vector.tensor_copy(out=G.h0[:, ch, 1:2], in_=h[:, 0:1]), R=[h], W=[G.h0])
            else:
                kb.op("dve", lambda: nc.vector.tensor_tensor_scan(out=h[:], data0=tr[:], data1=tq[:], initial=init, op0=ALU.mult, op1=ALU.add),
                      R=[tr, tq] + rinit, W=[h])
                kb.op("dve", lambda: nc.vector.tensor_copy(out=st[:], in_=h[:, T2 - 1:T2]), R=[h], W=[st])
                if is_ctx and ti == nt - 1:
                    kb.op("dve", lambda: nc.vector.tensor_copy(out=G.h0[:, ch, 0:1], in_=h[:, T2 - 1:T2]), R=[h], W=[G.h0])
                kb.op("pool", lambda: nc.gpsimd.tensor_tensor(out=hb[:], in0=hb[:], in1=h[:], op=ALU.add), R=[hb, h], W=[hb])
                kb.dma("sp", s.YR[ch * 128:(ch + 1) * 128, t0:t0 + T2], hb[:], R=[hb])

        import os
        if os.environ.get("RG_NOLOOK"):
            for i in range(len(items)):
                stageA(i)
                stageA2(i)
                stageB(i)
        else:
            n = len(items)
            groups = [list(range(g, min(g + 2, n))) for g in range(0, n, 2)]
            for i in groups[0]:
                stageA(i)
            for gi, grp in enumerate(groups):
                if gi + 1 < len(groups):
                    for i in groups[gi + 1]:
                        stageA(i)
                for i in grp:
                    stageA2(i)
                for i in grp:
                    stageB(i)


def filter_mlp(G, l, L, es, h3T):
    nc, kb = G.nc, G.kb
    SUB = min(512, L)
    w1 = kb.sb(es, "fw1", [33, 64], F32); w2 = kb.sb(es, "fw2", [64, 64], F32); w3 = kb.sb(es, "fw3", [64, 64], F32)
    kb.dma("sp", w1[:], G.f_w1[l], W=[w1]); kb.dma("sp", w2[:], G.f_w2[l], W=[w2]); kb.dma("sp", w3[:], G.f_w3[l], W=[w3])
    zt = [kb.sb(es, f"zt{i}", [33, SUB], F32) for i in range(2)]
    pre = [kb.sb(es, f"fpre{i}", [64, SUB], F32) for i in range(2)]
    t1 = [kb.sb(es, f"ft1{i}", [64, SUB], F32) for i in range(2)]
    hA = [kb.sb(es, f"fhA{i}", [64, SUB], F32) for i in range(2)]
    pf = [kb.ps(es, f"pf{i}", [128, 512], F32) for i in range(2)]
    cnt = 0
    for j in range(L // SUB):
        z = zt[j % 2]
        kb.dma("sp", z[:], G.zT[L][:, j * SUB:(j + 1) * SUB], W=[z])
        src = z; wts = (w1, w2, w3); kdim = (33, 64, 64)
        for li in range(3):
            p = pf[cnt % 2]; pr = pre[cnt % 2]; tt = t1[cnt % 2]; cnt += 1
            kb.op("pe", lambda: nc.tensor.matmul(p[0:64, 0:SUB], lhsT=wts[li][0:kdim[li], :], rhs=src[0:kdim[li], 0:SUB], start=True, stop=True), R=[wts[li], src], W=[p])
            fr = G.fpp[:, l, 3 + li:4 + li]; frb = G.frb[:, l, li:li + 1]
            kb.op("dve", lambda: nc.vector.tensor_scalar(out=pr[:], in0=p[0:64, 0:SUB], scalar1=fr, scalar2=frb, op0=ALU.mult, op1=ALU.add), R=[p, G.fpp, G.frb], W=[pr])
            kb.op("dve", lambda: nc.vector.tensor_scalar(out=tt[:], in0=pr[:], scalar1=1.0 / TWO_PI, scalar2=MAGIC, op0=ALU.mult, op1=ALU.add), R=[pr], W=[tt])
            kb.op("dve", lambda: nc.vector.tensor_scalar(out=tt[:], in0=tt[:], scalar1=-MAGIC, scalar2=-TWO_PI, op0=ALU.add, op1=ALU.mult), R=[tt], W=[tt])
            kb.op("pool", lambda: nc.gpsimd.tensor_tensor(out=pr[:], in0=pr[:], in1=tt[:], op=ALU.add), R=[pr, tt], W=[pr])
            if li < 2:
                h = hA[li]
                kb.op("act", lambda: nc.scalar.activation(out=h[:], in_=pr[:], func=AF.Sin), R=[pr], W=[h])
                src = h
            else:
                kb.op("act", lambda: nc.scalar.activation(out=h3T[:, j * SUB:(j + 1) * SUB], in_=pr[:], func=AF.Sin), R=[pr], W=[h3T])


class FFT:
    def __init__(self, G, es, with_ga=False):
        nc, kb = G.nc, G.kb
        self.G = G
        self.FA_t = kb.sbt(es, "FA", [128, 16384], BF16)
        self.FA = Tl(self.FA_t)
        if with_ga:
            self.GA = kb.sb(es, "GA", [128, 8192], BF16)
            kb.dma("sp", self.GA[:], G.GA_d, W=[self.GA])
        self.F128 = kb.sb(es, "F128", [128, 512], BF16)
        kb.dma("sp", self.F128[:], G.F128_d, W=[self.F128])
        self.UA_t = kb.sbt(es, "UA", [128, 16384], BF16)
        self.C_t = kb.sbt(es, "C", [128, 16384], BF16)
        self.C2_t = kb.sbt(es, "C2", [128, 16384], BF16)
        self.psT = [kb.ps(es, f"fpsT{i}", [128, 512], BF16) for i in range(2)]
        self.psA = [kb.ps(es, f"fpsA{i}", [128, 512], F32) for i in range(2)]
        self.psU = [kb.ps(es, f"fpsU{i}", [128, 512], F32) for i in range(4)]
        self.ev = 0
        self.fa_is = None

    def load_FA(self, which):
        G = self.G
        if self.fa_is == which:
            return
        if which == "FA":
            G.kb.dma("sp", self.FA_t[:, 0:8192], G.FA_d[:, 0:8192], W=[self.FA])
            G.kb.dma("sp", self.FA_t[:, 8192:16384], G.FA_d[:, 8192:16384], W=[self.FA])
        self.fa_is = which

    def evac(self, out, in_, R, W):
        nc, kb = self.G.nc, self.G.kb
        self.ev += 1
        if self.ev % 2:
            kb.op("act", lambda: nc.scalar.copy(out=out, in_=in_), R=R, W=W)
        else:
            kb.op("dve", lambda: nc.vector.tensor_copy(out=out, in_=in_), R=R, W=W)

    def forward(self, SRC, SRC_t, nA, consume):
        G = self.G; nc, kb = G.nc, G.kb
        ident = G.ident
        UA = [Tl(self.UA_t[:, g * 512:(g + 1) * 512]) for g in range(32)]
        C = [Tl(self.C_t[:, g * 512:(g + 1) * 512]) for g in range(32)]
        C2 = [Tl(None) for g in range(32)]
        self.load_FA("FA")
        W = nA * 128
        for g in range(32):
            pt = self.psT[g % 2]
            for j in range(4):
                b = 4 * g + j
                kb.op("pe", lambda: nc.tensor.transpose(out=pt[0:nA, j * 128:(j + 1) * 128], in_=SRC_t[:, b:W:128], identity=ident[:]), R=[SRC, ident], W=[pt])
            self.evac(UA[g][0:nA, :], pt[0:nA, :], [pt], [UA[g]])
        for g in range(32):
            pa = self.psA[g % 2]
            for j in range(4):
                b = 4 * g + j
                kb.op("pe", lambda: nc.tensor.matmul(pa[:, j * 128:(j + 1) * 128], lhsT=self.FA_t[0:nA, b * 128:(b + 1) * 128], rhs=UA[g][0:nA, j * 128:(j + 1) * 128], start=True, stop=True),
                      R=[self.FA, UA[g]], W=[pa])
            self.evac(C[g][:, :], pa[:, :], [pa], [C[g]])
        for g in range(32):
            pt = self.psT[g % 2]
            for j in range(4):
                c = 4 * g + j
                kb.op("pe", lambda: nc.tensor.transpose(out=pt[:, j * 128:(j + 1) * 128], in_=self.C_t[:, c:16384:128], identity=ident[:]), R=C + [ident], W=[pt])
            self.evac(self.C2_t[:, g * 512:(g + 1) * 512], pt[:, :], [pt], [C2[g]])
        C2k = self.C2_t[:, :].rearrange("p (c k) -> p c k", k=128)
        Cc = self.F128[:, 0:128]; Ss = self.F128[:, 128:256]; nSs = self.F128[:, 256:384]
        for g in range(16):
            k0 = 4 * g
            Cr = C2k[:, :, k0:k0 + 4]; Ci = C2k[:, :, 64 + k0:64 + k0 + 4]
            pr = self.psU[(2 * g) % 4]; pi = self.psU[(2 * g + 1) % 4]
            kb.op("pe", lambda: nc.tensor.matmul(pr[:, :], lhsT=Cc, rhs=Cr, start=True, stop=False), R=C2 + [self.F128], W=[pr])
            kb.op("pe", lambda: nc.tensor.matmul(pr[:, :], lhsT=Ss, rhs=Ci, start=False, stop=True), R=C2 + [self.F128], W=[pr])
            kb.op("pe", lambda: nc.tensor.matmul(pi[:, :], lhsT=Cc, rhs=Ci, start=True, stop=False), R=C2 + [self.F128], W=[pi])
            kb.op("pe", lambda: nc.tensor.matmul(pi[:, :], lhsT=nSs, rhs=Cr, start=False, stop=True), R=C2 + [self.F128], W=[pi])
            consume(g, pr, pi)


def ph_filter_lat(G, l):
    nc, kb = G.nc, G.kb
    L = NL
    with ExitStack() as es:
        h3T = kb.sb(es, "h3T", [64, L], BF16)
        with ExitStack() as es2:
            filter_mlp(G, l, L, es2, h3T)
        kb.barrier()
        F = FFT(G, es)
        w4b = kb.sb(es, "w4b", [64, 1024], BF16)
        kb.dma("pool", w4b[:], G.f_w4[l], W=[w4b])
        KT_t = kb.sbt(es, "KT", [128, 2 * L], BF16)
        KT = Tl(KT_t)
        tg = [kb.sb(es, f"tg{i}", [128, 512], F32) for i in range(2)]
        dec = [kb.sb(es, f"dec{i}", [128, 512], F32) for i in range(2)]
        absb = kb.sb(es, "absb", [128, 4096], BF16)
        l1p = kb.sb(es, "l1p", [128, 4], F32)
        kst = [kb.sb(es, f"kst{i}", [128, 2, 512], F32) for i in range(2)]
        pk = F.psA
        for ch in range(4):
            kb.op("pool", lambda: nc.gpsimd.memset(KT_t[:, L:L + 1], 0.0), W=[KT])
            for j in range(L // 512):
                t0 = j * 512
                tgt = tg[j % 2]; dc = dec[j % 2]
                kb.dma("sp", tgt[:], G.tg[L][:, t0:t0 + 512], W=[tgt])
                kb.op("act", lambda: nc.scalar.activation(out=dc[:], in_=tgt[:], func=AF.Exp, scale=G.ndel[:, ch:ch + 1]), R=[tgt, G.ndel], W=[dc])
                for half in range(2):
                    p = pk[half]
                    c0 = half * 512 + ch * 128
                    kb.op("pe", lambda: nc.tensor.matmul(p[:, :], lhsT=w4b[:, c0:c0 + 128], rhs=h3T[:, t0:t0 + 512], start=True, stop=True), R=[w4b, h3T], W=[p])
                    if half == 0:
                        kb.op("dve", lambda: nc.vector.tensor_tensor(out=KT_t[:, t0:t0 + 512], in0=p[:, :], in1=dc[:], op=ALU.mult), R=[p, dc], W=[KT])
                    elif t0 == 0:
                        kb.op("dve", lambda: nc.vector.scalar_tensor_tensor(out=KT_t[:, 2 * L - 1:2 * L - 512:-1], in0=p[:, 1:512], scalar=-1.0, in1=dc[:, 1:512], op0=ALU.mult, op1=ALU.mult),
                              R=[p, dc], W=[KT])
                    else:
                        kb.op("dve", lambda: nc.vector.scalar_tensor_tensor(out=KT_t[:, 2 * L - t0:2 * L - t0 - 512:-1], in0=p[:, :], scalar=-1.0, in1=dc[:], op0=ALU.mult, op1=ALU.mult),
                              R=[p, dc], W=[KT])
            for q in range(4):
                kb.op("act", lambda: nc.scalar.activation(out=absb[:], in_=KT_t[:, q * 4096:(q + 1) * 4096], func=AF.Abs), R=[KT], W=[absb])
                kb.op("dve", lambda: nc.vector.reduce_sum(out=l1p[:, q:q + 1], in_=absb[:], axis=AX.X), R=[absb], W=[l1p])
            kb.op("dve", lambda: nc.vector.reduce_sum(out=G.rl1[:, ch:ch + 1], in_=l1p[:], axis=AX.X), R=[l1p], W=[G.rl1])
            kb.op("dve", lambda: nc.vector.reciprocal(out=G.rl1[:, ch:ch + 1], in_=G.rl1[:, ch:ch + 1]), R=[G.rl1], W=[G.rl1])

            def consume(g, pr, pi):
                ks = kst[g % 2]
                kb.op("act", lambda: nc.scalar.copy(out=ks[:, 0, :], in_=pr[:, :]), R=[pr], W=[ks])
                kb.op("dve", lambda: nc.vector.tensor_copy(out=ks[:, 1, :], in_=pi[:, :]), R=[pi], W=[ks])
                kb.dma("sp", G.KH[ch, :, g, :, :], ks[:], R=[ks])

            F.forward(KT, KT_t, 128, consume)


def conv3(G, l, col, p, out, T):
    nc, kb = G.nc, G.kb
    w = lambda k: G.pp[:, l, PP_HCW + col * 3 + k: PP_HCW + col * 3 + k + 1]
    b = G.pp[:, l, PP_HCB + col: PP_HCB + col + 1]
    kb.op("dve", lambda: nc.vector.tensor_scalar(out=out[:, 0:T], in0=p[:, 0:T], scalar1=w(0), scalar2=b, op0=ALU.mult, op1=ALU.add), R=[p, G.pp], W=[out])
    for k in (1, 2):
        kb.op("dve", lambda: nc.vector.scalar_tensor_tensor(out=out[:, 0:T], in0=p[:, k:k + T], scalar=w(k), in1=out[:, 0:T], op0=ALU.mult, op1=ALU.add), R=[p, out, G.pp], W=[out])


def load_halo1(G, s, p, row0, t0, T):
    nc, kb = G.nc, G.kb
    N = s.N
    lo = max(t0 - 1, 0); hi = min(t0 + T + 1, N)
    if t0 == 0:
        kb.op("pool", lambda: nc.gpsimd.memset(p[:, 0:1], 0.0), W=[p])
    if t0 + T == N:
        kb.op("pool", lambda: nc.gpsimd.memset(p[:, T + 1:T + 2], 0.0), W=[p])
    kb.dma("sp", p[:, lo - (t0 - 1): hi - (t0 - 1)], s.P[row0:row0 + 128, lo:hi], W=[p])


def ph_hyena_lat(G, l):
    nc, kb = G.nc, G.kb
    s = G.lat
    L = NL; T = 512
    with ExitStack() as es:
        F = FFT(G, es, with_ga=True)
        U_t = kb.sbt(es, "U", [128, L], BF16)
        U = Tl(U_t)
        pt_ = [kb.sb(es, f"hp{i}", [128, T + 2], BF16) for i in range(4)]
        cf = [kb.sb(es, f"hc{i}", [128, T], F32) for i in range(4)]
        Kt = [kb.sb(es, f"Kt{i}", [128, 2, 512], F32) for i in range(2)]
        tm = [kb.sb(es, f"sm{i}", [128, 4, 512], BF16) for i in range(2)]
        yo = [kb.sb(es, f"yo{i}", [128, T], F32) for i in range(2)]
        Cc = F.F128[:, 0:128]; Ss = F.F128[:, 128:256]; nSs = F.F128[:, 256:384]; nCc = F.F128[:, 384:512]
        YT = F.UA_t[:, :].bitcast(F32)
        YTv = YT.rearrange("p (a b) -> p b a", b=128)
        dg_t = kb.sbt(es, "dg", [128, 9, 128], BF16)
        dg = Tl(dg_t)

        def conv_pe(i, ch, p, ps):
            for k in range(3):
                kb.op("pe", lambda: nc.tensor.matmul(ps[:, 0:T], lhsT=dg_t[:, i * 3 + k, :], rhs=p[:, k:k + T], start=(k == 0), stop=(k == 2)), R=[dg, p], W=[ps])

        for ch in range(4):
            for i in range(3):
                for k in range(3):
                    col = PP_HCW + (4 * i + ch) * 3 + k
                    kb.op("dve", lambda: nc.vector.tensor_scalar(out=dg_t[:, i * 3 + k, :], in0=G.ident[:], scalar1=G.pp[:, l, col:col + 1], scalar2=None, op0=ALU.mult), R=[G.ident, G.pp], W=[dg])
            for j in range(L // T):
                t0 = j * T
                p1 = pt_[(2 * j) % 4]; p2 = pt_[(2 * j + 1) % 4]; c1 = cf[j % 4]
                pa = F.psU[(2 * j) % 4]; pb = F.psU[(2 * j + 1) % 4]
                load_halo1(G, s, p1, 512 + ch * 128, t0, T)
                load_halo1(G, s, p2, 1024 + ch * 128, t0, T)
                conv_pe(1, ch, p1, pa)
                conv_pe(2, ch, p2, pb)
                kb.op("act", lambda: nc.scalar.activation(out=c1[:], in_=pa[:, 0:T], func=AF.Identity, bias=G.pp[:, l, PP_HCB + 4 + ch:PP_HCB + 5 + ch], scale=1.0), R=[pa, G.pp], W=[c1])
                kb.op("dve", lambda: nc.vector.scalar_tensor_tensor(out=U_t[:, t0:t0 + T], in0=pb[:, 0:T], scalar=G.pp[:, l, PP_HCB + 8 + ch:PP_HCB + 9 + ch], in1=c1[:], op0=ALU.add, op1=ALU.mult),
                      R=[pb, G.pp, c1], W=[U])
            Dg = [Tl(None) for g in range(16)]

            def consume(g, pr, pi):
                kt = Kt[g % 2]; t = tm[g % 2]
                k0 = 4 * g
                kb.dma("sp", kt[:], G.KH[ch, :, g, :, :], W=[kt])
                kb.op("dve", lambda: nc.vector.tensor_tensor(out=t[:, 0, :], in0=pr[:, :], in1=kt[:, 0, :], op=ALU.mult), R=[pr, kt], W=[t])
                kb.op("dve", lambda: nc.vector.tensor_tensor(out=t[:, 1, :], in0=pi[:, :], in1=kt[:, 1, :], op=ALU.mult), R=[pi, kt], W=[t])
                kb.op("dve", lambda: nc.vector.tensor_tensor(out=t[:, 2, :], in0=pr[:, :], in1=kt[:, 1, :], op=ALU.mult), R=[pr, kt], W=[t])
                kb.op("dve", lambda: nc.vector.tensor_tensor(out=t[:, 3, :], in0=pi[:, :], in1=kt[:, 0, :], op=ALU.mult), R=[pi, kt], W=[t])
                pd0 = F.psA[0]; pd1 = F.psA[1]
                for j, w in enumerate((Cc, nCc, nSs, nSs)):
                    kb.op("pe", lambda: nc.tensor.matmul(pd0[:, :], lhsT=w, rhs=t[:, j, :], start=(j == 0), stop=(j == 3)), R=[t, F.F128], W=[pd0])
                for j, w in enumerate((Ss, nSs, Cc, Cc)):
                    kb.op("pe", lambda: nc.tensor.matmul(pd1[:, :], lhsT=w, rhs=t[:, j, :], start=(j == 0), stop=(j == 3)), R=[t, F.F128], W=[pd1])
                kb.op("act", lambda: nc.scalar.copy(out=F.C_t[:, k0 * 128:k0 * 128 + 512].rearrange("p (k c) -> p k c", c=128), in_=pd0[:, :].rearrange("p (c k) -> p k c", k=4)), R=[pd0], W=[Dg[g]])
                kb.op("act", lambda: nc.scalar.copy(out=F.C_t[:, (64 + k0) * 128:(64 + k0) * 128 + 512].rearrange("p (k c) -> p k c", c=128), in_=pd1[:, :].rearrange("p (c k) -> p k c", k=4)), R=[pd1], W=[Dg[g]])

            F.forward(U, U_t, 64, consume)
            D2 = [Tl(None) for g in range(32)]
            D2b = F.C2_t[:, :].rearrange("p (c b) -> p b c", b=128)
            for g in range(32):
                pt = F.psT[g % 2]
                for j in range(4):
                    c = 4 * g + j
                    kb.op("pe", lambda: nc.tensor.transpose(out=pt[:, j * 128:(j + 1) * 128], in_=F.C_t[:, c:16384:128], identity=G.ident[:]), R=Dg + [G.ident], W=[pt])
                F.evac(F.C2_t[:, g * 512:(g + 1) * 512], pt[:, :], [pt], [D2[g]])
            YTt = [Tl(None) for g in range(16)]
            for g in range(16):
                py = F.psU[g % 4]
                for j in range(8):
                    b = 8 * g + j
                    kb.op("pe", lambda: nc.tensor.matmul(py[:, j * 64:(j + 1) * 64], lhsT=D2b[:, b, :], rhs=F.GA[:, b * 64:(b + 1) * 64], start=True, stop=True),
                          R=D2 + [F.GA], W=[py])
                kb.op("act", lambda: nc.scalar.activation(out=YTv[:, 8 * g:8 * g + 8, :], in_=py[:, :].rearrange("p (b a) -> p b a", a=64), func=AF.Copy, scale=G.rl1[:, ch:ch + 1]),
                      R=[py, G.rl1], W=[YTt[g]])
            for j in range(L // T):
                t0 = j * T
                p0 = pt_[j % 4]; y = yo[j % 2]; pc = F.psA[j % 2]
                load_halo1(G, s, p0, ch * 128, t0, T)
                conv_pe(0, ch, p0, pc)
                kb.op("dve", lambda: nc.vector.scalar_tensor_tensor(out=y[:], in0=U_t[:, t0:t0 + T], scalar=G.pp[:, l, PP_HBIAS + ch:PP_HBIAS + ch + 1], in1=YT[:, t0:t0 + T], op0=ALU.mult, op1=ALU.add),
                      R=[U, G.pp] + YTt, W=[y])
                kb.op("dve", lambda: nc.vector.scalar_tensor_tensor(out=y[:], in0=pc[:, 0:T], scalar=G.pp[:, l, PP_HCB + ch:PP_HCB + ch + 1], in1=y[:], op0=ALU.add, op1=ALU.mult),
                      R=[pc, G.pp, y], W=[y])
                kb.dma("act", s.YH[ch * 128:(ch + 1) * 128, t0:t0 + T], y[:], R=[y])
            kb.barrier()


def ph_hyena_ctx(G, l):
    nc, kb = G.nc, G.kb
    s = G.cx; L = NCX
    with ExitStack() as es:
        h3T = kb.sb(es, "h3Tc", [64, L], BF16)
        with ExitStack() as es2:
            filter_mlp(G, l, L, es2, h3T)
        kb.barrier()
        w4b = kb.sb(es, "w4bc", [64, 1024], BF16)
        kb.dma("pool", w4b[:], G.f_w4[l], W=[w4b])
        tg = kb.sb(es, "tgc", [128, L], F32)
        kb.dma("sp", tg[:], G.tg[L], W=[tg])
        Fc = kb.sb(es, "Fc", [128, 4, 512], BF16); Gc = kb.sb(es, "Gc", [128, 4, 256], BF16); idf = kb.sb(es, "idf", [128, 128], F32)
        kb.dma("sp", Fc[:], G.Fc_d.rearrange("p (a b) -> p a b", a=4), W=[Fc])
        kb.dma("sp", Gc[:], G.Gc_d.rearrange("p (a b) -> p a b", a=4), W=[Gc])
        kb.dma("sp", idf[:], G.identf_d, W=[idf])
        dec = kb.sb(es, "decc", [128, L], F32)
        KT = [kb.sb(es, f"KTc{i}", [128, 2 * L], BF16) for i in range(4)]
        absb = kb.sb(es, "absc", [128, 2 * L], BF16)
        rl = kb.sb(es, "rlc", [128, 4], F32)
        pp_ = [kb.sb(es, f"cp{i}", [128, L + 2], BF16) for i in range(3)]
        x0c = [kb.sb(es, f"x0c{i}", [128, L], F32) for i in range(4)]
        cc_ = [kb.sb(es, f"cc{i}", [128, L], F32) for i in range(2)]
        uf = [kb.sb(es, f"uf{i}", [128, L], F32) for i in range(4)]
        ub = [kb.sb(es, f"ub{i}", [128, L], BF16) for i in range(4)]
        utok = kb.sb(es, "utok", [128, 2, 512], BF16)
        ktok = kb.sb(es, "ktok", [128, 4, 512], BF16)
        Kh = kb.sb(es, "Khc", [128, 4, 512], F32)
        Yh = kb.sb(es, "Yhc", [128, 4, 512], BF16)
        tm = [kb.sb(es, f"tmc{i}", [128, 512], F32) for i in range(4)]
        ytok = kb.sb(es, "ytok", [128, 2, 512], F32)
        yo = [kb.sb(es, f"yoc{i}", [128, L], F32) for i in range(2)]
        psb = [kb.ps(es, f"pcb{i}", [128, 512], BF16) for i in range(2)]
        psf = [kb.ps(es, f"pcf{i}", [128, 512], F32) for i in range(4)]
        for ch in range(4):
            kb.op("act", lambda: nc.scalar.activation(out=dec[:], in_=tg[:], func=AF.Exp, scale=G.ndel[:, ch:ch + 1]), R=[tg, G.ndel], W=[dec])
            K_ = KT[ch]
            kb.op("pool", lambda: nc.gpsimd.memset(K_[:, L:L + 1], 0.0), W=[K_])
            for half in range(2):
                p = psf[half]; c0 = half * 512 + ch * 128
                kb.op("pe", lambda: nc.tensor.matmul(p[:, 0:L], lhsT=w4b[:, c0:c0 + 128], rhs=h3T[:, 0:L], start=True, stop=True), R=[w4b, h3T], W=[p])
                if half == 0:
                    kb.op("dve", lambda: nc.vector.tensor_tensor(out=K_[:, 0:L], in0=p[:, 0:L], in1=dec[:], op=ALU.mult), R=[p, dec], W=[K_])
                else:
                    kb.op("dve", lambda: nc.vector.scalar_tensor_tensor(out=K_[:, 2 * L - 1:L:-1], in0=p[:, 1:L], scalar=-1.0, in1=dec[:, 1:L], op0=ALU.mult, op1=ALU.mult), R=[p, dec], W=[K_])
            kb.op("act", lambda: nc.scalar.activation(out=absb[:], in_=K_[:], func=AF.Abs), R=[K_], W=[absb])
            kb.op("dve", lambda: nc.vector.reduce_sum(out=rl[:, ch:ch + 1], in_=absb[:], axis=AX.X), R=[absb], W=[rl])
            load_halo1(G, s, pp_[0], ch * 128, 0, L)
            conv3(G, l, ch, pp_[0], x0c[ch], L)
            for i, row0 in ((1, 512 + ch * 128), (2, 1024 + ch * 128)):
                load_halo1(G, s, pp_[i], row0, 0, L)
                conv3(G, l, 4 * i + ch, pp_[i], cc_[i - 1], L)
            kb.op("dve", lambda: nc.vector.tensor_tensor(out=uf[ch][:], in0=cc_[0][:], in1=cc_[1][:], op=ALU.mult), R=[cc_[0], cc_[1]], W=[uf[ch]])
            kb.op("act", lambda: nc.scalar.copy(out=ub[ch][:], in_=uf[ch][:]), R=[uf[ch]], W=[ub[ch]])
            pt = psb[ch % 2]
            for tb in range(2):
                kb.op("pe", lambda: nc.tensor.transpose(out=pt[:, tb * 128:(tb + 1) * 128], in_=ub[ch][:, tb * 128:(tb + 1) * 128], identity=G.ident[:]), R=[ub[ch], G.ident], W=[pt])
            kb.op("act", lambda: nc.scalar.copy(out=utok[:, :, ch * 128:(ch + 1) * 128], in_=pt[:, 0:256].rearrange("p (a b) -> p a b", a=2)), R=[pt], W=[utok])
            pt2 = psb[(ch + 1) % 2]
            for nb_ in range(4):
                kb.op("pe", lambda: nc.tensor.transpose(out=pt2[:, nb_ * 128:(nb_ + 1) * 128], in_=K_[:, nb_ * 128:(nb_ + 1) * 128], identity=G.ident[:]), R=[K_, G.ident], W=[pt2])
            kb.op("dve", lambda: nc.vector.tensor_copy(out=ktok[:, :, ch * 128:(ch + 1) * 128], in_=pt2[:, :].rearrange("p (a b) -> p a b", a=4)), R=[pt2], W=[ktok])
        kb.op("dve", lambda: nc.vector.reciprocal(out=rl[:], in_=rl[:]), R=[rl], W=[rl])
        for m in range(4):
            p = psf[m]
            for nb_ in range(4):
                kb.op("pe", lambda: nc.tensor.matmul(p[:, :], lhsT=Fc[:, nb_, m * 128:(m + 1) * 128], rhs=ktok[:, nb_, :], start=(nb_ == 0), stop=(nb_ == 3)), R=[Fc, ktok], W=[p])
            if m % 2:
                kb.op("act", lambda: nc.scalar.copy(out=Kh[:, m, :], in_=p[:, :]), R=[p], W=[Kh])
            else:
                kb.op("dve", lambda: nc.vector.tensor_copy(out=Kh[:, m, :], in_=p[:, :]), R=[p], W=[Kh])
        for m in range(4):
            p = psf[m]
            for tb in range(2):
                kb.op("pe", lambda: nc.tensor.matmul(p[:, :], lhsT=Fc[:, tb, m * 128:(m + 1) * 128], rhs=utok[:, tb, :], start=(tb == 0), stop=(tb == 1)), R=[Fc, utok], W=[p])
        for q in range(2):
            pr = psf[q]; pi = psf[2 + q]
            t1, t2, t3, t4 = tm
            kb.op("dve", lambda: nc.vector.tensor_tensor(out=t1[:], in0=pr[:, :], in1=Kh[:, q, :], op=ALU.mult), R=[pr, Kh], W=[t1])
            kb.op("dve", lambda: nc.vector.tensor_tensor(out=t2[:], in0=pi[:, :], in1=Kh[:, 2 + q, :], op=ALU.mult), R=[pi, Kh], W=[t2])
            kb.op("pool", lambda: nc.gpsimd.tensor_tensor(out=Yh[:, q, :], in0=t1[:], in1=t2[:], op=ALU.subtract), R=[t1, t2], W=[Yh])
            kb.op("dve", lambda: nc.vector.tensor_tensor(out=t3[:], in0=pr[:, :], in1=Kh[:, 2 + q, :], op=ALU.mult), R=[pr, Kh], W=[t3])
            kb.op("dve", lambda: nc.vector.tensor_tensor(out=t4[:], in0=pi[:, :], in1=Kh[:, q, :], op=ALU.mult), R=[pi, Kh], W=[t4])
            kb.op("pool", lambda: nc.gpsimd.tensor_tensor(out=Yh[:, 2 + q, :], in0=t3[:], in1=t4[:], op=ALU.add), R=[t3, t4], W=[Yh])
        for tb in range(2):
            p = psf[tb]
            for kc in range(4):
                kb.op("pe", lambda: nc.tensor.matmul(p[:, :], lhsT=Gc[:, kc, tb * 128:(tb + 1) * 128], rhs=Yh[:, kc, :], start=(kc == 0), stop=(kc == 3)), R=[Gc, Yh], W=[p])
            kb.op("act", lambda: nc.scalar.copy(out=ytok[:, tb, :], in_=p[:, :]), R=[p], W=[ytok])
        for ch in range(4):
            p = psf[2 + ch % 2]
            for tb in range(2):
                kb.op("pe", lambda: nc.tensor.matmul(p[:, tb * 128:(tb + 1) * 128], lhsT=ytok[:, tb, ch * 128:(ch + 1) * 128], rhs=idf[:], start=True, stop=True), R=[ytok, idf], W=[p])
            y = yo[ch % 2]
            kb.op("dve", lambda: nc.vector.tensor_scalar(out=y[:], in0=p[:, 0:L], scalar1=rl[:, ch:ch + 1], scalar2=None, op0=ALU.mult), R=[p, rl], W=[y])
            kb.op("dve", lambda: nc.vector.scalar_tensor_tensor(out=y[:], in0=uf[ch][:], scalar=G.pp[:, l, PP_HBIAS + ch:PP_HBIAS + ch + 1], in1=y[:], op0=ALU.mult, op1=ALU.add),
                  R=[uf[ch], G.pp, y], W=[y])
            kb.op("dve", lambda: nc.vector.tensor_tensor(out=y[:], in0=y[:], in1=x0c[ch][:], op=ALU.mult), R=[y, x0c[ch]], W=[y])
            kb.dma("act", s.YH[ch * 128:(ch + 1) * 128, :], y[:], R=[y])


def ph_out(G, l, s, last):
    nc, kb = G.nc, G.kb
    N = s.N; TT = min(512, N); nt = N // TT; nb = TT // 128
    with ExitStack() as es:
        gbc = kb.sb(es, "gbc", [128, D], F32)
        kb.dma("sp", gbc[:], G.modd[l, s.idx, 2].partition_broadcast(128), W=[gbc])
        Wo_t = kb.sbt(es, "Wo", [128, 8, D], BF16)
        Wo = [Tl(Wo_t[:, k, :]) for k in range(8)]
        wtmp = [kb.sb(es, f"wtmp{i}", [128, D], F32) for i in range(2)]
        for k in range(8):
            wt = wtmp[k % 2]
            kb.dma("sp", wt[:], G.w_out[l, k * 128:(k + 1) * 128, :], W=[wt])
            kb.op("pool", lambda: nc.gpsimd.tensor_tensor(out=Wo[k][:, :], in0=wt[:], in1=gbc[:], op=ALU.mult), R=[wt, gbc], W=[Wo[k]])
        if last:
            fg = kb.sb(es, "fg", [128, D], F32)
            kb.dma("sp", fg[:], G.final_g.partition_broadcast(128), W=[fg])
            junk = kb.sb(es, "ojunk", [128, D], BF16)
            fss = [kb.sb(es, f"fss{i}", [128, 4], F32) for i in range(2)]
            xo = [kb.sb(es, f"xo{i}", [128, D], F32) for i in range(2)]
        yin = [kb.sb(es, f"yin{i}", [128, 8, TT], F32) for i in range(2)]
        zin = [kb.sb(es, f"zin{i}", [128, 8, TT], BF16) for i in range(2)]
        sqb = [kb.sb(es, f"sqb{i}", [128, TT], BF16) for i in range(2)]
        rstd = [kb.sb(es, f"rstd{i}", [128, 2, TT], F32) for i in range(2)]
        sz = [kb.sb(es, f"sz{i}", [128, TT], F32) for i in range(2)]
        t1 = [kb.sb(es, f"ot1{i}", [128, TT], F32) for i in range(2)]
        ym_t = [kb.sbt(es, f"ym{i}", [128, 8, TT], BF16) for i in range(2)]
        ym = [[Tl(ym_t[i][:, k, :]) for k in range(8)] for i in range(2)]
        xin = [kb.sb(es, f"oxin{i}", [128, 4, D], F32) for i in range(2)]
        pms = [kb.ps(es, f"pms{i}", [128, 512], F32) for i in range(2)]
        po = [kb.ps(es, f"po{i}", [128, 512], F32) for i in range(4)]
        st = dict(qi=0, pi=0)

        def stage1(it):
            t0 = it * TT
            yi = yin[it % 2]; zi = zin[it % 2]; rs = rstd[it % 2]; ymi = ym[it % 2]; xi = xin[it % 2]
            kb.dma("sp", yi[:, 0:4, :], s.YH[:, t0:t0 + TT].rearrange("(k p) t -> p k t", p=128), W=[yi])
            kb.dma("sp", yi[:, 4:8, :], s.YR[:, t0:t0 + TT].rearrange("(k p) t -> p k t", p=128), W=[yi])
            kb.dma("sp", zi[:, 0:4, :], s.P[1536:2048, t0:t0 + TT].rearrange("(k p) t -> p k t", p=128), W=[zi])
            kb.dma("sp", zi[:, 4:8, :], s.P[2560:3072, t0:t0 + TT].rearrange("(k p) t -> p k t", p=128), W=[zi])
            kb.dma("sp", xi[:, 0:nb, :], s.X[t0:t0 + TT, :].rearrange("(k p) d -> p k d", p=128), W=[xi])
            for grp in range(2):
                pm = pms[grp]
                for k in range(4):
                    sq = sqb[st["qi"] % 2]; st["qi"] += 1
                    kb.op("act", lambda: nc.scalar.activation(out=sq[:], in_=yi[:, grp * 4 + k, :], func=AF.Square), R=[yi], W=[sq])
                    kb.op("pe", lambda: nc.tensor.matmul(pm[:, 0:TT], lhsT=G.ones[:], rhs=sq[:], start=(k == 0), stop=(k == 3)), R=[G.ones, sq], W=[pm])
                kb.op("act", lambda: nc.scalar.activation(out=rs[:, grp, :], in_=pm[:, 0:TT], func=AF.Sqrt, bias=EPS, scale=1.0 / DH), R=[pm], W=[rs])
            kb.op("dve", lambda: nc.vector.reciprocal(out=rs[:], in_=rs[:]), R=[rs], W=[rs])
            for k8 in range(8):
                grp = k8 // 4
                z = sz[k8 % 2]; tt = t1[k8 % 2]
                gcol = (PP_GNH if grp == 0 else PP_GNR) + (k8 % 4)
                kb.op("act", lambda: nc.scalar.activation(out=z[:], in_=zi[:, k8, :], func=AF.Silu), R=[zi], W=[z])
                kb.op("dve", lambda: nc.vector.scalar_tensor_tensor(out=tt[:], in0=yi[:, k8, :], scalar=G.pp[:, l, gcol:gcol + 1], in1=rs[:, grp, :], op0=ALU.mult, op1=ALU.mult),
                      R=[yi, G.pp, rs], W=[tt])
                kb.op("pool", lambda: nc.gpsimd.tensor_tensor(out=ymi[k8][:, :], in0=tt[:], in1=z[:], op=ALU.mult), R=[tt, z], W=[ymi[k8]])

        def stage2(it):
            t0 = it * TT
            ymi = ym[it % 2]; xi = xin[it % 2]
            for tb in range(nb):
                for dn in range(2):
                    p = po[st["pi"] % 4]; st["pi"] += 1
                    for k8 in range(8):
                        kb.op("pe", lambda: nc.tensor.matmul(p[:, :], lhsT=ymi[k8][:, tb * 128:(tb + 1) * 128], rhs=Wo[k8][:, dn * 512:(dn + 1) * 512], start=(k8 == 0), stop=(k8 == 7)),
                              R=[ymi[k8], Wo[k8]], W=[p])
                    kb.op("dve", lambda: nc.vector.tensor_tensor(out=xi[:, tb, dn * 512:(dn + 1) * 512], in0=p[:, :], in1=xi[:, tb, dn * 512:(dn + 1) * 512], op=ALU.add), R=[p, xi], W=[xi])
            if not last:
                kb.dma("act", s.X[t0:t0 + TT, :].rearrange("(k p) d -> p k d", p=128), xi[:, 0:nb, :], R=[xi])
            else:
                fs = fss[it % 2]
                kb.op("pool", lambda: nc.gpsimd.memset(fs[:], 0.0), W=[fs])
                for tb in range(nb):
                    kb.op("act", lambda: nc.scalar.activation(out=junk[:], in_=xi[:, tb, :], func=AF.Square, accum_out=fs[:, tb:tb + 1]), R=[xi], W=[junk, fs])
                kb.op("act", lambda: nc.scalar.activation(out=fs[:], in_=fs[:], func=AF.Sqrt, bias=EPS, scale=1.0 / D), R=[fs], W=[fs])
                kb.op("dve", lambda: nc.vector.reciprocal(out=fs[:], in_=fs[:]), R=[fs], W=[fs])
                for tb in range(nb):
                    o = xo[tb % 2]
                    kb.op("dve", lambda: nc.vector.scalar_tensor_tensor(out=o[:], in0=xi[:, tb, :], scalar=fs[:, tb:tb + 1], in1=fg[:], op0=ALU.mult, op1=ALU.mult), R=[xi, fs, fg], W=[o])
                    kb.dma("act", G.out[t0 + tb * 128:t0 + (tb + 1) * 128, :], o[:], R=[o])

        stage1(0)
        for it in range(nt):
            if it + 1 < nt:
                stage1(it + 1)
            stage2(it)


_NC_CACHE = {}


def kernel(**inputs):
    inp = {k: np.asarray(v) for k, v in inputs.items()}
    if "nc" not in _NC_CACHE:
        _NC_CACHE["nc"] = build()
    nc = _NC_CACHE["nc"]
    in_maps = [pack_inputs(inp, b) for b in range(8)]
    res = run_bass_kernel_spmd(nc, in_maps, core_ids=list(range(8)))
    out = np.stack([np.asarray(res.results[b]["out"], dtype=np.float32) for b in range(8)], axis=0)
    return out
```

```python
import math
from contextlib import ExitStack
import numpy as np
import ml_dtypes
import concourse.bass as bass
import concourse.mybir as mybir
from concourse.bass_utils import run_bass_kernel_spmd

F32 = mybir.dt.float32
BF16 = mybir.dt.bfloat16
ALU = mybir.AluOpType
AF = mybir.ActivationFunctionType
AX = mybir.AxisListType

D = 1024
DIN = 3072
NL = 8192
NCX = 256
DH = 512
DEPTH = 4
EPS = 1e-6
MFFT = 16384
MAGIC = 12582912.0
TWO_PI = 2.0 * math.pi


class Tl:
    __slots__ = ("t", "w", "r")

    def __init__(self, t):
        self.t = t
        self.w = None
        self.r = {}

    def __getitem__(self, k):
        return self.t[k]


class KB:
    def __init__(self, nc):
        self.nc = nc
        self.eng = {"pe": nc.tensor, "act": nc.scalar, "dve": nc.vector, "pool": nc.gpsimd, "sp": nc.sync}
        self.psem = {e: nc.alloc_semaphore(f"prog_{e}") for e in ("pe", "act", "dve", "pool")}
        self.pcnt = {e: 0 for e in self.psem}
        self.seen = {e: {} for e in self.eng}
        self.dq = {}
        for q, n in (("sp", 12), ("act", 6), ("pool", 6), ("dve", 2)):
            self.dq[q] = dict(sems=[nc.alloc_semaphore(f"dq_{q}{i}") for i in range(n)], cnt=[0] * n, idx=0)
        self.nins = 0

    def _wait(self, e, tok):
        key, val, sem = tok
        if self.seen[e].get(key, 0) >= val:
            return
        self.eng[e].wait_ge(sem, val)
        self.seen[e][key] = val
        self.nins += 1

    def _deps(self, e, R, W, dma=False):
        me = None if dma else "c:" + e
        for b in list(R) + list(W):
            t = b.w
            if t is not None and not (e == "pe" and t[0] == me):
                self._wait(e, t)
        for b in W:
            for t in b.r.values():
                if t[0] != me:
                    self._wait(e, t)

    def op(self, e, fn, R=(), W=()):
        self._deps(e, R, W)
        ins = fn()
        self.pcnt[e] += 1
        ins.then_inc(self.psem[e], 1)
        self.nins += 1
        tok = ("c:" + e, self.pcnt[e], self.psem[e])
        for b in W:
            b.w = tok
            b.r = {}
        for b in R:
            if b.w is not tok:
                b.r[tok[0]] = tok
        return tok

    def dma(self, q, out, in_, R=(), W=()):
        d = self.dq[q]
        e = q
        self._deps(e, R, W, True)
        i = d["idx"]
        d["idx"] = (i + 1) % len(d["sems"])
        sem = d["sems"][i]
        key = f"d:{q}:{i}"
        if d["cnt"][i] > 0:
            self._wait(e, (key, d["cnt"][i], sem))
        ins = self.eng[e].dma_start(out=out, in_=in_)
        d["cnt"][i] += 16
        ins.then_inc(sem, 16)
        self.nins += 1
        tok = (key, d["cnt"][i], sem)
        for b in W:
            b.w = tok
            b.r = {}
        for b in R:
            b.r[key] = tok
        return tok

    def barrier(self):
        toks = [("c:" + e, self.pcnt[e], self.psem[e]) for e in self.psem if self.pcnt[e] > 0]
        for q, d in self.dq.items():
            for i, sem in enumerate(d["sems"]):
                if d["cnt"][i] > 0:
                    toks.append((f"d:{q}:{i}", d["cnt"][i], sem))
        for e in self.eng:
            for t in toks:
                if t[0] != "c:" + e:
                    self._wait(e, t)

    def sbt(self, es, name, shape, dt):
        self.uid = getattr(self, "uid", 0) + 1
        return es.enter_context(self.nc.sbuf_tensor(f"s{self.uid}_{name}", list(shape), dt))

    def sb(self, es, name, shape, dt):
        return Tl(self.sbt(es, name, shape, dt))

    def ps(self, es, name, shape, dt):
        self.uid = getattr(self, "uid", 0) + 1
        return Tl(es.enter_context(self.nc.psum_tensor(f"p{self.uid}_{name}", list(shape), dt)))


_CONST_CACHE = {}


def _bf(a):
    return np.ascontiguousarray(a.astype(np.float32)).astype(ml_dtypes.bfloat16)


def host_consts():
    if _CONST_CACHE:
        return _CONST_CACHE
    c = {}
    rows, gw = NL // 64, 64
    r, col = np.meshgrid(np.arange(rows, dtype=np.float32), np.arange(gw, dtype=np.float32), indexing="ij")
    quarter = D // 4
    omega = (1.0 / (10000.0 ** (np.arange(quarter, dtype=np.float32) / np.float32(quarter)))).astype(np.float32)

    def emb(pos):
        ang = pos.reshape(-1, 1).astype(np.float32) * omega[None, :]
        return np.concatenate([np.sin(ang), np.cos(ang)], axis=-1)

    c["pos"] = np.concatenate([emb(r), emb(col)], axis=-1).astype(np.float32)
    c["ident_bf"] = _bf(np.eye(128))
    c["ident_f"] = np.eye(128, dtype=np.float32)
    c["ones_bf"] = _bf(np.ones((128, 128)))
    a = np.arange(128, dtype=np.float64)[:, None, None]
    b = np.arange(128, dtype=np.float64)[None, :, None]
    k1 = np.arange(64, dtype=np.float64)[None, None, :]
    th = TWO_PI * (128 * a + b) * (k1 + 0.5) / MFFT
    c["FA"] = _bf(np.concatenate([np.cos(th), -np.sin(th)], axis=-1).reshape(128, 128 * 128))
    a64 = np.arange(64, dtype=np.float64)[None, None, :]
    bb = np.arange(128, dtype=np.float64)[None, :, None]
    kk = np.arange(64, dtype=np.float64)[:, None, None]
    th2 = TWO_PI * (128 * a64 + bb) * (kk + 0.5) / MFFT
    c["GA"] = _bf(np.concatenate([(2.0 / MFFT) * np.cos(th2), -(2.0 / MFFT) * np.sin(th2)], axis=0).reshape(128, 128 * 64))
    ph = TWO_PI * np.outer(np.arange(128.0), np.arange(128.0)) / 128.0
    c["F128"] = _bf(np.stack([np.cos(ph), np.sin(ph), -np.sin(ph)], axis=1).reshape(128, 3 * 128))

    def feats(L):
        t = np.linspace(0.0, 1.0, L, dtype=np.float32)[:, None]
        w = (2.0 * math.pi * np.arange(L, dtype=np.float32)[:, None] / L).astype(np.float32)
        f = np.linspace(1e-4, 15, 16, dtype=np.float32)[None, :]
        z = np.concatenate([t, np.cos(f * w), -np.sin(f * w)], axis=-1).astype(np.float32)
        return np.ascontiguousarray(z.T), t[:, 0]

    c["zT_l"], tl = feats(NL)
    c["zT_c"], tcx = feats(NCX)
    c["tg_l"] = np.ascontiguousarray(np.broadcast_to(tl[None, :], (128, NL))).astype(np.float32)
    c["tg_c"] = np.ascontiguousarray(np.broadcast_to(tcx[None, :], (128, NCX))).astype(np.float32)
    max_decay = math.log(1e-2) / 0.3
    min_decay = math.log(1e-2) / 1.5
    deltas = np.abs(np.linspace(min_decay, max_decay, DH, dtype=np.float32))
    c["ndel"] = np.ascontiguousarray((-deltas).reshape(4, 128).T).astype(np.float32)
    n = np.arange(512, dtype=np.float64)[:, None]
    k = np.arange(256, dtype=np.float64)[None, :]
    th = TWO_PI * n * (k + 0.5) / 512.0
    Fc = np.concatenate([np.cos(th), -np.sin(th)], axis=1)
    c["Fc"] = _bf(Fc.reshape(4, 128, 512).transpose(1, 0, 2).reshape(128, 4 * 512))
    t = np.arange(256, dtype=np.float64)[None, :]
    kk_ = np.arange(256, dtype=np.float64)[:, None]
    th2 = TWO_PI * t * (kk_ + 0.5) / 512.0
    Gc = np.concatenate([(2.0 / 512) * np.cos(th2), -(2.0 / 512) * np.sin(th2)], axis=0)
    c["Gc"] = _bf(Gc.reshape(4, 128, 256).transpose(1, 0, 2).reshape(128, 4 * 256))
    _CONST_CACHE.update(c)
    return c


PP_HCW = 0
PP_HCB = 36
PP_HBIAS = 48
PP_GNH = 52
PP_GNR = 56
PP_RCW = 60
PP_RCB = 76
PP_RB = 80
PP_LAM = 96
NPP = 104


class NS:
    pass


def _chunked(v, nchunk):
    return np.ascontiguousarray(v.reshape(nchunk, 128).T)


def pack_inputs(inp, b):
    C = host_consts()
    f32 = np.float32
    m = {}
    m["x"] = np.ascontiguousarray(inp["x"][b], dtype=f32)
    m["ctx"] = np.ascontiguousarray(inp["ctx"][b], dtype=f32)
    cs = np.stack([inp["c"][b], inp["c_ctx"]], axis=-1).astype(f32)
    m["cs"] = np.ascontiguousarray(cs.reshape(8, 128, 2).transpose(1, 0, 2))
    for k in ("w_mod", "b_mod", "norm_g", "w_in", "w_out", "final_g", "f_w1", "f_w2", "f_w3", "f_w4"):
        m[k] = np.ascontiguousarray(inp[k], dtype=f32)
    pp = np.zeros((DEPTH, 128, NPP), f32)
    bd = np.zeros((DEPTH, 2, 2, 4, 128, 128), f32)
    fp = np.zeros((DEPTH, 64, 6), f32)
    for l in range(DEPTH):
        hcw = inp["hy_conv_w"][l]
        pp[l, :, PP_HCW:PP_HCW + 36] = hcw.T.reshape(12, 128, 3).transpose(1, 0, 2).reshape(128, 36)
        pp[l, :, PP_HCB:PP_HCB + 12] = _chunked(inp["hy_conv_b"][l], 12)
        pp[l, :, PP_HBIAS:PP_HBIAS + 4] = _chunked(inp["hy_bias"][l], 4)
        pp[l, :, PP_GNH:PP_GNH + 4] = _chunked(inp["br_norm_h"][l], 4)
        pp[l, :, PP_GNR:PP_GNR + 4] = _chunked(inp["br_norm_r"][l], 4)
        rcw = inp["rg_conv_w"][l]
        pp[l, :, PP_RCW:PP_RCW + 16] = rcw.T.reshape(4, 128, 4).transpose(1, 0, 2).reshape(128, 16)
        pp[l, :, PP_RCB:PP_RCB + 4] = _chunked(inp["rg_conv_b"][l], 4)
        for d in range(2):
            for g, (wk, bk) in enumerate((("rg_wa", "rg_ba"), ("rg_wx", "rg_bx"))):
                pp[l, :, PP_RB + (d * 2 + g) * 4: PP_RB + (d * 2 + g) * 4 + 4] = _chunked(inp[bk][l, d], 4)
                w = inp[wk][l, d]
                for ch in range(4):
                    for hh in range(2):
                        bd[l, d, g, ch, hh * 64:(hh + 1) * 64, hh * 64:(hh + 1) * 64] = w[ch * 2 + hh]
            pp[l, :, PP_LAM + d * 4: PP_LAM + d * 4 + 4] = _chunked(inp["rg_lam"][l, d], 4)
        fp[l, :, 0] = inp["f_b1"][l]
        fp[l, :, 1] = inp["f_b2"][l]
        fp[l, :, 2] = inp["f_b3"][l]
        fp[l, :, 3:6] = inp["f_freq"][l].T
    m["pp"] = np.ascontiguousarray(pp.transpose(1, 0, 2))
    m["bd"] = np.ascontiguousarray(bd.transpose(0, 1, 2, 4, 3, 5))
    m["fp"] = np.ascontiguousarray(fp.transpose(1, 0, 2))
    for k in ("pos", "ident_bf", "ident_f", "ones_bf", "FA", "GA", "F128", "Fc", "Gc", "zT_l", "zT_c", "tg_l", "tg_c", "ndel"):
        m[k] = C[k]
    return m


def build(dbg=(), phases=None, depth=DEPTH):
    nc = bass.Bass("TRN2", target_bir_lowering=False)
    kb = KB(nc)
    G = NS()
    G.nc = nc
    G.kb = kb

    def inp(name, shape, dt=F32):
        return nc.dram_tensor(name, list(shape), dt, kind="ExternalInput").ap()

    def scr(name, shape, dt):
        kind = "ExternalOutput" if name in dbg else "Internal"
        return nc.dram_tensor(name, list(shape), dt, kind=kind).ap()

    G.x = inp("x", [NL, D]); G.ctx = inp("ctx", [NCX, D]); G.cs = inp("cs", [128, 8, 2]); G.pos = inp("pos", [NL, D])
    G.w_mod = inp("w_mod", [DEPTH, D, 3 * D]); G.b_mod = inp("b_mod", [DEPTH, 3 * D]); G.norm_g = inp("norm_g", [DEPTH, D])
    G.w_in = inp("w_in", [DEPTH, D, DIN]); G.w_out = inp("w_out", [DEPTH, D, D]); G.final_g = inp("final_g", [D])
    G.f_w1 = inp("f_w1", [DEPTH, 33, 64]); G.f_w2 = inp("f_w2", [DEPTH, 64, 64]); G.f_w3 = inp("f_w3", [DEPTH, 64, 64])
    G.f_w4 = inp("f_w4", [DEPTH, 64, 1024])
    G.pp_d = inp("pp", [128, DEPTH, NPP]); G.bd = inp("bd", [DEPTH, 2, 2, 128, 4, 128]); G.fp_d = inp("fp", [64, DEPTH, 6])
    G.ident_d = inp("ident_bf", [128, 128], BF16); G.ones_d = inp("ones_bf", [128, 128], BF16)
    G.FA_d = inp("FA", [128, 128 * 128], BF16); G.GA_d = inp("GA", [128, 128 * 64], BF16); G.F128_d = inp("F128", [128, 384], BF16)
    G.zT = {NL: inp("zT_l", [33, NL]), NCX: inp("zT_c", [33, NCX])}
    G.tg = {NL: inp("tg_l", [128, NL]), NCX: inp("tg_c", [128, NCX])}
    G.ndel_d = inp("ndel", [128, 4])
    G.identf_d = inp("ident_f", [128, 128]); G.Fc_d = inp("Fc", [128, 2048], BF16); G.Gc_d = inp("Gc", [128, 1024], BF16)
    G.out = nc.dram_tensor("out", [NL, D], F32, kind="ExternalOutput").ap()

    G.modd = scr("modd", [DEPTH, 2, 3, D], F32)
    G.lat = NS(); G.cx = NS()
    for s, N, nm, idx in ((G.lat, NL, "l", 0), (G.cx, NCX, "c", 1)):
        s.N = N; s.idx = idx; s.nm = nm
        s.X = scr("X" + nm, [N, D], F32)
        s.P = scr("P" + nm, [DIN, N], BF16)
        s.HB = scr("HB" + nm, [DH, N], F32)
        s.YR = scr("YR" + nm, [DH, N], F32)
        s.YH = scr("YH" + nm, [DH, N], F32)
    G.KH = scr("KH", [4, 128, 16, 2, 512], F32)
    G.dbgc = scr("DBGC", [4, 128, 3, 512], F32) if "DBGC" in dbg else None
    G.UT = scr("UT", [DH, NL], BF16)

    with ExitStack() as es0:
        G.ident = kb.sb(es0, "ident", [128, 128], BF16)
        G.ones = kb.sb(es0, "ones", [128, 128], BF16)
        G.pp = kb.sb(es0, "pp", [128, DEPTH, NPP], F32)
        G.hl = kb.sb(es0, "hl", [128, DEPTH, 8], F32)
        G.hrb = kb.sb(es0, "hrb", [128, DEPTH, 16], F32)
        G.h0 = kb.sb(es0, "h0", [128, 4, 2], F32)
        G.rl1 = kb.sb(es0, "rl1", [128, 4], F32)
        G.ndel = kb.sb(es0, "ndel", [128, 4], F32)
        G.fpp = kb.sb(es0, "fpp", [64, DEPTH, 6], F32)
        G.frb = kb.sb(es0, "frb", [64, DEPTH, 3], F32)
        kb.dma("sp", G.ident[:], G.ident_d, W=[G.ident])
        kb.dma("sp", G.ones[:], G.ones_d, W=[G.ones])
        kb.dma("sp", G.pp[:], G.pp_d, W=[G.pp])
        kb.dma("sp", G.ndel[:], G.ndel_d, W=[G.ndel])
        kb.dma("sp", G.fpp[:], G.fp_d, W=[G.fpp])

        P = phases
        ph_prologue(G)
        kb.barrier()
        esW = None; preW = None
        for l in range(depth):
            last = l == DEPTH - 1
            if P is None or "inproj" in P:
                ph_inproj(G, l, preW)
                kb.barrier()
                if esW is not None:
                    esW.close(); esW = None; preW = None
            if P is None or "rglru" in P:
                ph_rglru(G, l, G.cx)
                kb.barrier()
                ph_rglru(G, l, G.lat)
                kb.barrier()
            if P is None or "hyena" in P or "hy_f" in P:
                ph_filter_lat(G, l)
                kb.barrier()
            if P is None or "hyena" in P or "hy_l" in P:
                ph_hyena_lat(G, l)
                kb.barrier()
            if (P is None or "hyena" in P or "hy_c" in P) and not last:
                ph_hyena_ctx(G, l)
                kb.barrier()
            if P is None or "out" in P:
                if not last:
                    ph_out(G, l, G.cx, False)
                    kb.barrier()
                if P is None and not last and l + 1 < depth:
                    esW = ExitStack()
                    preW = load_w_in(G, l + 1, esW)
                ph_out(G, l, G.lat, last)
                kb.barrier()
        kb.barrier()
    return nc


def ph_prologue(G):
    nc, kb = G.nc, G.kb
    with ExitStack() as es:
        t8 = kb.sb(es, "t8", [128, DEPTH, 8], F32)
        kb.op("act", lambda: nc.scalar.activation(out=t8[:], in_=G.pp[:, :, PP_LAM:PP_LAM + 8], func=AF.Exp, scale=-1.0), R=[G.pp], W=[t8])
        t8b = kb.sb(es, "t8b", [128, DEPTH, 8], F32)
        t8c = kb.sb(es, "t8c", [128, DEPTH, 8], F32)
        kb.op("dve", lambda: nc.vector.tensor_scalar(out=t8b[:], in0=t8[:], scalar1=2.0, scalar2=None, op0=ALU.add), R=[t8], W=[t8b])
        kb.op("dve", lambda: nc.vector.reciprocal(out=t8b[:], in_=t8b[:]), R=[t8b], W=[t8b])
        kb.op("dve", lambda: nc.vector.tensor_tensor(out=t8[:], in0=t8[:], in1=t8b[:], op=ALU.mult), R=[t8, t8b], W=[t8])
        kb.op("dve", lambda: nc.vector.tensor_tensor(out=t8b[:], in0=t8[:], in1=t8[:], op=ALU.mult), R=[t8], W=[t8b])
        kb.op("dve", lambda: nc.vector.tensor_scalar(out=t8c[:], in0=t8b[:], scalar1=0.2, scalar2=1.0 / 3.0, op0=ALU.mult, op1=ALU.add), R=[t8b], W=[t8c])
        kb.op("dve", lambda: nc.vector.tensor_tensor(out=t8c[:], in0=t8c[:], in1=t8b[:], op=ALU.mult), R=[t8c, t8b], W=[t8c])
        kb.op("dve", lambda: nc.vector.scalar_tensor_tensor(out=G.hl[:], in0=t8c[:], scalar=1.0, in1=t8[:], op0=ALU.add, op1=ALU.mult), R=[t8c, t8], W=[G.hl])
        kb.op("dve", lambda: nc.vector.tensor_scalar(out=G.hl[:], in0=G.hl[:], scalar1=-8.0, scalar2=None, op0=ALU.mult), R=[G.hl], W=[G.hl])
        kb.op("dve", lambda: nc.vector.tensor_scalar(out=G.hrb[:], in0=G.pp[:, :, PP_RB:PP_RB + 16], scalar1=0.5, scalar2=None, op0=ALU.mult), R=[G.pp], W=[G.hrb])
        kb.op("dve", lambda: nc.vector.tensor_tensor(out=G.frb[:], in0=G.fpp[:, :, 0:3], in1=G.fpp[:, :, 3:6], op=ALU.mult), R=[G.fpp], W=[G.frb])
        cs = kb.sb(es, "cs", [128, 8, 2], F32)
        S = kb.sb(es, "S", [128, 8, 2], BF16)
        kb.dma("sp", cs[:], G.cs, W=[cs])
        kb.op("act", lambda: nc.scalar.activation(out=S[:], in_=cs[:], func=AF.Silu), R=[cs], W=[S])
        wm = [kb.sb(es, f"wm{i}", [128, 8, 512], BF16) for i in range(2)]
        bm = kb.sb(es, "bm", [2, 3 * D], F32)
        ng = kb.sb(es, "ng", [2, D], F32)
        mods = kb.sb(es, "mods", [2, 3 * D], F32)
        pp_ = [kb.ps(es, f"pp{i}", [128, 512], F32) for i in range(2)]
        modd_t = Tl(G.modd)
        it = 0
        for l in range(DEPTH):
            kb.dma("sp", bm[:], G.b_mod[l].partition_broadcast(2), W=[bm])
            kb.dma("sp", ng[:], G.norm_g[l].partition_broadcast(2), W=[ng])
            for n in range(6):
                w = wm[it % 2]; p = pp_[it % 2]; it += 1
                kb.dma("pool", w[:], G.w_mod[l][:, n * 512:(n + 1) * 512].rearrange("(k p) n -> p k n", p=128), W=[w])
                for kc in range(8):
                    kb.op("pe", lambda: nc.tensor.matmul(p[0:2, :], lhsT=S[:, kc, :], rhs=w[:, kc, :], start=(kc == 0), stop=(kc == 7)),
                          R=[S, w], W=[p])
                kb.op("dve", lambda: nc.vector.tensor_tensor(out=mods[:, n * 512:(n + 1) * 512], in0=p[0:2, :], in1=bm[:, n * 512:(n + 1) * 512], op=ALU.add),
                      R=[p, bm], W=[mods])
            kb.op("dve", lambda: nc.vector.scalar_tensor_tensor(out=mods[:, D:2 * D], in0=mods[:, D:2 * D], scalar=1.0, in1=ng[:], op0=ALU.add, op1=ALU.mult),
                  R=[mods, ng], W=[mods])
            kb.dma("sp", G.modd[l].rearrange("s r d -> s (r d)"), mods[:], R=[mods], W=[modd_t])


def load_w_in(G, l, es):
    kb = G.kb
    W = [kb.sb(es, f"W{i}", [128, DIN], BF16) for i in range(8)]
    for i in range(8):
        kb.dma("pool", W[i][:], G.w_in[l, i * 128:(i + 1) * 128, :], W=[W[i]])
    return W


def ph_inproj(G, l, W=None):
    nc, kb = G.nc, G.kb
    with ExitStack() as es:
        if W is None:
            W = load_w_in(G, l, es)
        gsc = {}; sh = {}
        for s in (G.cx, G.lat):
            gsc[s.idx] = kb.sb(es, f"gsc{s.idx}", [128, D], F32)
            sh[s.idx] = kb.sb(es, f"sh{s.idx}", [128, D], F32)
            kb.dma("sp", sh[s.idx][:], G.modd[l, s.idx, 0].partition_broadcast(128), W=[sh[s.idx]])
            kb.dma("sp", gsc[s.idx][:], G.modd[l, s.idx, 1].partition_broadcast(128), W=[gsc[s.idx]])
        xin = [kb.sb(es, f"xin{i}", [128, 4, D], F32) for i in range(2)]
        ptile = [kb.sb(es, f"ptile{i}", [128, 4, D], F32) for i in range(2)] if l == 0 else None
        junk = kb.sb(es, "junk", [128, D], BF16)
        ss = [kb.sb(es, f"ss{i}", [128, 4], F32) for i in range(2)]
        tmp = [kb.sb(es, f"tmp{i}", [128, D], F32) for i in range(2)]
        xs_t = [kb.sbt(es, f"xs{i}", [128, 4, D], BF16) for i in range(2)]
        xs = [[Tl(xs_t[i][:, k, :]) for k in range(4)] for i in range(2)]
        xT_t = [kb.sbt(es, f"xT{i}", [128, 8, 512], BF16) for i in range(2)]
        xT = [[Tl(xT_t[i][:, dc, :]) for dc in range(8)] for i in range(2)]
        po_t = [kb.sbt(es, f"po{i}", [128, 4, 512], BF16) for i in range(3)]
        po = [[Tl(po_t[i][:, j, :]) for j in range(4)] for i in range(3)]
        psT = [kb.ps(es, f"psT{i}", [128, 512], BF16) for i in range(2)]
        psm = [kb.ps(es, f"psm{i}", [128, 512], F32) for i in range(4)]
        tiles = [(s, it) for s in (G.cx, G.lat) for it in range(s.N // min(512, s.N))]
        st = dict(ev=0, gi=0, tk=0)

        def stage1(idx):
            s, it = tiles[idx]
            TT = min(512, s.N); nb = TT // 128; t0 = it * TT
            xi = xin[idx % 2]; sq = ss[idx % 2]; xsi = xs[idx % 2]; xTi = xT[idx % 2]
            g_, h_ = gsc[s.idx], sh[s.idx]
            src = s.X if l > 0 else (G.x if s.idx == 0 else G.ctx)
            kb.dma("sp", xi[:, 0:nb, :], src[t0:t0 + TT, :].rearrange("(k p) d -> p k d", p=128), W=[xi])
            if l == 0:
                if s.idx == 0:
                    pz = ptile[idx % 2]
                    kb.dma("sp", pz[:, 0:nb, :], G.pos[t0:t0 + TT, :].rearrange("(k p) d -> p k d", p=128), W=[pz])
                    kb.op("pool", lambda: nc.gpsimd.tensor_tensor(out=xi[:, 0:nb, :], in0=xi[:, 0:nb, :], in1=pz[:, 0:nb, :], op=ALU.add), R=[xi, pz], W=[xi])
                kb.dma("sp", s.X[t0:t0 + TT, :].rearrange("(k p) d -> p k d", p=128), xi[:, 0:nb, :], R=[xi])
            kb.op("pool", lambda: nc.gpsimd.memset(sq[:], 0.0), W=[sq])
            for k in range(nb):
                kb.op("act", lambda: nc.scalar.activation(out=junk[:], in_=xi[:, k, :], func=AF.Square, accum_out=sq[:, k:k + 1]),
                      R=[xi], W=[junk, sq])
            kb.op("act", lambda: nc.scalar.activation(out=sq[:], in_=sq[:], func=AF.Sqrt, bias=EPS, scale=1.0 / D), R=[sq], W=[sq])
            kb.op("dve", lambda: nc.vector.reciprocal(out=sq[:], in_=sq[:]), R=[sq], W=[sq])
            for k in range(nb):
                tm = tmp[st["tk"] % 2]; st["tk"] += 1
                kb.op("dve", lambda: nc.vector.scalar_tensor_tensor(out=tm[:], in0=xi[:, k, :], scalar=sq[:, k:k + 1], in1=g_[:], op0=ALU.mult, op1=ALU.mult),
                      R=[xi, sq, g_], W=[tm])
                kb.op("pool", lambda: nc.gpsimd.tensor_tensor(out=xsi[k][:], in0=tm[:], in1=h_[:], op=ALU.add), R=[tm, h_], W=[xsi[k]])
            for dc in range(8):
                pt = psT[dc % 2]
                for k in range(nb):
                    kb.op("pe", lambda: nc.tensor.transpose(out=pt[:, k * 128:(k + 1) * 128], in_=xsi[k][:, dc * 128:(dc + 1) * 128], identity=G.ident[:]),
                          R=[xsi[k], G.ident], W=[pt])
                if dc % 2:
                    kb.op("act", lambda: nc.scalar.copy(out=xTi[dc][:, 0:TT], in_=pt[:, 0:TT]), R=[pt], W=[xTi[dc]])
                else:
                    kb.op("dve", lambda: nc.vector.tensor_copy(out=xTi[dc][:, 0:TT], in_=pt[:, 0:TT]), R=[pt], W=[xTi[dc]])

        def stage2(idx):
            s, it = tiles[idx]
            TT = min(512, s.N); t0 = it * TT
            xTi = xT[idx % 2]
            for cc in range(24):
                pm = psm[cc % 4]
                for dc in range(8):
                    kb.op("pe", lambda: nc.tensor.matmul(pm[:, 0:TT], lhsT=W[dc][:, cc * 128:(cc + 1) * 128], rhs=xTi[dc][:, 0:TT], start=(dc == 0), stop=(dc == 7)),
                          R=[W[dc], xTi[dc]], W=[pm])
                pg = po[st["gi"] % 3]; j = cc % 4
                if st["ev"] % 2:
                    kb.op("act", lambda: nc.scalar.copy(out=pg[j][:, 0:TT], in_=pm[:, 0:TT]), R=[pm], W=[pg[j]])
                else:
                    kb.op("dve", lambda: nc.vector.tensor_copy(out=pg[j][:, 0:TT], in_=pm[:, 0:TT]), R=[pm], W=[pg[j]])
                st["ev"] += 1
                if j == 3:
                    c0 = (cc - 3) * 128
                    kb.dma("act", s.P[c0:c0 + 512, t0:t0 + TT].rearrange("(k p) t -> p k t", p=128), po_t[st["gi"] % 3][:, :, 0:TT], R=pg)
                    st["gi"] += 1

        stage1(0)
        for i in range(len(tiles)):
            if i + 1 < len(tiles):
                stage1(i + 1)
            stage2(i)


def ph_rglru(G, l, s):
    nc, kb = G.nc, G.kb
    N = s.N; T2 = min(1024, N); nt = N // T2
    NS = 4; LA = 3
    SUB = min(512, T2); nsub = T2 // SUB
    is_ctx = s.idx == 1
    with ExitStack() as es:
        bdt = [[kb.sb(es, f"bd{d}{g}", [128, 4, 128], BF16) for g in range(2)] for d in range(2)]
        for d in range(2):
            for g in range(2):
                kb.dma("pool", bdt[d][g][:], G.bd[l, d, g], W=[bdt[d][g]])
        dg_t = [kb.sbt(es, f"rdg{i}", [128, 4, 128], BF16) for i in range(2)]
        dg = [Tl(t) for t in dg_t]
        pin = [kb.sb(es, f"pin{i}", [128, T2 + 3], BF16) for i in range(2)]
        xcf_t = [kb.sbt(es, f"xcf{i}", [128, N], F32) for i in range(2)]
        xcb_t = [kb.sbt(es, f"xcbf{i}", [128, N], BF16) for i in range(2)]
        xcf = [[Tl(xcf_t[j][:, i * T2:(i + 1) * T2]) for i in range(nt)] for j in range(2)]
        xcb = [[Tl(xcb_t[j][:, i * T2:(i + 1) * T2]) for i in range(nt)] for j in range(2)]
        thr = [kb.sb(es, f"thr{i}", [128, T2], F32) for i in range(NS)]
        thi = [kb.sb(es, f"thi{i}", [128, T2], F32) for i in range(NS)]
        sq = [kb.sb(es, f"sq{i}", [128, T2], F32) for i in range(NS)]
        hh = [kb.sb(es, f"hh{i}", [128, T2], F32) for i in range(NS)]
        hbl = [kb.sb(es, f"hbl{i}", [128, T2], F32) for i in range(NS)]
        st = kb.sb(es, "st", [128, 1], F32)
        psg = [kb.ps(es, f"psg{i}", [128, 512], F32) for i in range(6)]
        hbt = {}
        cnt = dict(pi=0, pn=0)
        items = []
        for ch in range(4):
            for d in (1, 0):
                order = list(range(nt - 1, -1, -1)) if d == 1 else list(range(nt))
                for j, ti in enumerate(order):
                    items.append((ch, d, ti, j == 0))
        conv_done = set()

        def conv(ch):
            cb = G.pp[:, l, PP_RCB + ch: PP_RCB + ch + 1]
            dgt = dg_t[ch % 2]; dgl = dg[ch % 2]
            for k in range(4):
                col = PP_RCW + ch * 4 + k
                kb.op("dve", lambda: nc.vector.tensor_scalar(out=dgt[:, k, :], in0=G.ident[:], scalar1=G.pp[:, l, col:col + 1], scalar2=None, op0=ALU.mult), R=[G.ident, G.pp], W=[dgl])
            for ti in range(nt):
                t0 = ti * T2
                p = pin[cnt["pn"] % 2]; cnt["pn"] += 1
                lo = max(t0 - 2, 0); hi = min(t0 + T2 + 1, N)
                if t0 == 0:
                    kb.op("pool", lambda: nc.gpsimd.memset(p[:, 0:2], 0.0), W=[p])
                if t0 + T2 == N:
                    kb.op("pool", lambda: nc.gpsimd.memset(p[:, T2 + 2:T2 + 3], 0.0), W=[p])
                kb.dma("sp", p[:, lo - (t0 - 2): hi - (t0 - 2)], s.P[2048 + ch * 128: 2048 + (ch + 1) * 128, lo:hi], W=[p])
                xf = xcf[ch % 2][ti]; xb = xcb[ch % 2][ti]
                for sb_ in range(nsub):
                    pc = psg[cnt["pi"] % 6]; cnt["pi"] += 1
                    for k in range(4):
                        kb.op("pe", lambda: nc.tensor.matmul(pc[:, 0:SUB], lhsT=dgt[:, k, :], rhs=p[:, sb_ * SUB + k: sb_ * SUB + k + SUB], start=(k == 0), stop=(k == 3)), R=[dgl, p], W=[pc])
                    sl = slice(sb_ * SUB, (sb_ + 1) * SUB)
                    kb.op("act", lambda: nc.scalar.activation(out=xf[:, sl], in_=pc[:, 0:SUB], func=AF.Identity, bias=cb, scale=1.0), R=[pc, G.pp], W=[xf])
                    kb.op("act", lambda: nc.scalar.activation(out=xb[:, sl], in_=pc[:, 0:SUB], func=AF.Identity, bias=cb, scale=1.0), R=[pc, G.pp], W=[xb])

        def stageA(i):
            ch, d, ti, first = items[i]
            if ch not in conv_done:
                conv(ch); conv_done.add(ch)
            slot = i % NS
            t0 = ti * T2
            x = xcf[ch % 2][ti]; xb = xcb[ch % 2][ti]; tr = thr[slot]; tq = thi[slot]; s2 = sq[slot]; hb = hbl[slot]
            hl = G.hl[:, l, d * 4 + ch: d * 4 + ch + 1]
            hba = G.hrb[:, l, (d * 2 + 0) * 4 + ch: (d * 2 + 0) * 4 + ch + 1]
            hbx = G.hrb[:, l, (d * 2 + 1) * 4 + ch: (d * 2 + 1) * 4 + ch + 1]
            for sb_ in range(nsub):
                sl = slice(sb_ * SUB, (sb_ + 1) * SUB)
                pr = psg[cnt["pi"] % 6]; pq = psg[(cnt["pi"] + 1) % 6]; cnt["pi"] += 2
                kb.op("pe", lambda: nc.tensor.matmul(pr[:, 0:SUB], lhsT=bdt[d][0][:, ch, :], rhs=xb[:, sl], start=True, stop=True), R=[bdt[d][0], xb], W=[pr])
                kb.op("pe", lambda: nc.tensor.matmul(pq[:, 0:SUB], lhsT=bdt[d][1][:, ch, :], rhs=xb[:, sl], start=True, stop=True), R=[bdt[d][1], xb], W=[pq])
                kb.op("act", lambda: nc.scalar.activation(out=tr[:, sl], in_=pr[:, 0:SUB], func=AF.Tanh, bias=hba, scale=0.5), R=[pr, G.hrb], W=[tr])
                kb.op("act", lambda: nc.scalar.activation(out=tq[:, sl], in_=pq[:, 0:SUB], func=AF.Tanh, bias=hbx, scale=0.5), R=[pq, G.hrb], W=[tq])
            kb.op("act", lambda: nc.scalar.activation(out=tr[:], in_=tr[:], func=AF.Exp, bias=hl, scale=hl), R=[tr, G.hl], W=[tr])
            kb.op("pool", lambda: nc.gpsimd.tensor_tensor(out=s2[:], in0=tr[:], in1=tr[:], op=ALU.mult), R=[tr], W=[s2])

        def stageA2(i):
            ch, d, ti, first = items[i]
            slot = i % NS
            x = xcf[ch % 2][ti]; tq = thi[slot]; s2 = sq[slot]
            kb.op("act", lambda: nc.scalar.activation(out=s2[:], in_=s2[:], func=AF.Sqrt, bias=1.0, scale=-1.0), R=[s2], W=[s2])
            kb.op("dve", lambda: nc.vector.scalar_tensor_tensor(out=tq[:], in0=tq[:], scalar=1.0, in1=x[:], op0=ALU.add, op1=ALU.mult), R=[tq, x], W=[tq])
            kb.op("dve", lambda: nc.vector.scalar_tensor_tensor(out=tq[:], in0=tq[:], scalar=0.5, in1=s2[:], op0=ALU.mult, op1=ALU.mult), R=[tq, s2], W=[tq])

        def stageB(i):
            ch, d, ti, first = items[i]
            slot = i % NS
            t0 = ti * T2
            tr = thr[slot]; tq = thi[slot]; h = hh[slot]; hb = hbl[slot]
            if d == 0:
                kb.dma("sp", hb[:], s.HB[ch * 128:(ch + 1) * 128, t0:t0 + T2], R=[hbt[(ch, ti)]], W=[hb])
            if first:
                init = 0.0 if is_ctx else G.h0[:, ch, (1 if d == 1 else 0):(2 if d == 1 else 1)]
                rinit = [] if is_ctx else [G.h0]
            else:
                init = st[:, 0:1]; rinit = [st]
            if d == 1:
                kb.op("dve", lambda: nc.vector.tensor_tensor_scan(out=h[:, ::-1], data0=tr[:, ::-1], data1=tq[:, ::-1], initial=init, op0=ALU.mult, op1=ALU.add),
                      R=[tr, tq] + rinit, W=[h])
                kb.op("dve", lambda: nc.vector.tensor_copy(out=st[:], in_=h[:, 0:1]), R=[h], W=[st])
                hbt[(ch, ti)] = Tl(None)
                kb.dma("sp", s.HB[ch * 128:(ch + 1) * 128, t0:t0 + T2], h[:], R=[h], W=[hbt[(ch, ti)]])
                if is_ctx and ti == 0:
                    kb.op("dve", lambda: nc.vector.tensor_copy(out=G.h0[:, ch, 1:2], in_=h[:, 0:1]), R=[h], W=[G.h0])
            else:
                kb.op("dve", lambda: nc.vector.tensor_tensor_scan(out=h[:], data0=tr[:], data1=tq[:], initial=init, op0=ALU.mult, op1=ALU.add),
                      R=[tr, tq] + rinit, W=[h])
                kb.op("dve", lambda: nc.vector.tensor_copy(out=st[:], in_=h[:, T2 - 1:T2]), R=[h], W=[st])
                if is_ctx and ti == nt - 1:
                    kb.op("dve", lambda: nc.vector.tensor_copy(out=G.h0[:, ch, 0:1], in_=h[:, T2 - 1:T2]), R=[h], W=[G.h0])
                kb.op("pool", lambda: nc.gpsimd.tensor_tensor(out=hb[:], in0=hb[:], in1=h[:], op=ALU.add), R=[hb, h], W=[hb])
                kb.dma("sp", s.YR[ch * 128:(ch + 1) * 128, t0:t0 + T2], hb[:], R=[hb])

        import os
        if os.environ.get("RG_NOLOOK"):
            for i in range(len(items)):
                stageA(i)
                stageA2(i)
                stageB(i)
        else:
            n = len(items)
            groups = [list(range(g, min(g + 2, n))) for g in range(0, n, 2)]
            for i in groups[0]:
                stageA(i)
            for gi, grp in enumerate(groups):
                if gi + 1 < len(groups):
                    for i in groups[gi + 1]:
                        stageA(i)
                for i in grp:
                    stageA2(i)
                for i in grp:
                    stageB(i)


def filter_mlp(G, l, L, es, h3T):
    nc, kb = G.nc, G.kb
    SUB = min(512, L)
    w1 = kb.sb(es, "fw1", [33, 64], F32); w2 = kb.sb(es, "fw2", [64, 64], F32); w3 = kb.sb(es, "fw3", [64, 64], F32)
    kb.dma("sp", w1[:], G.f_w1[l], W=[w1]); kb.dma("sp", w2[:], G.f_w2[l], W=[w2]); kb.dma("sp", w3[:], G.f_w3[l], W=[w3])
    zt = [kb.sb(es, f"zt{i}", [33, SUB], F32) for i in range(2)]
    pre = [kb.sb(es, f"fpre{i}", [64, SUB], F32) for i in range(2)]
    t1 = [kb.sb(es, f"ft1{i}", [64, SUB], F32) for i in range(2)]
    hA = [kb.sb(es, f"fhA{i}", [64, SUB], F32) for i in range(2)]
    pf = [kb.ps(es, f"pf{i}", [128, 512], F32) for i in range(2)]
    cnt = 0
    for j in range(L // SUB):
        z = zt[j % 2]
        kb.dma("sp", z[:], G.zT[L][:, j * SUB:(j + 1) * SUB], W=[z])
        src = z; wts = (w1, w2, w3); kdim = (33, 64, 64)
        for li in range(3):
            p = pf[cnt % 2]; pr = pre[cnt % 2]; tt = t1[cnt % 2]; cnt += 1
            kb.op("pe", lambda: nc.tensor.matmul(p[0:64, 0:SUB], lhsT=wts[li][0:kdim[li], :], rhs=src[0:kdim[li], 0:SUB], start=True, stop=True), R=[wts[li], src], W=[p])
            fr = G.fpp[:, l, 3 + li:4 + li]; frb = G.frb[:, l, li:li + 1]
            kb.op("dve", lambda: nc.vector.tensor_scalar(out=pr[:], in0=p[0:64, 0:SUB], scalar1=fr, scalar2=frb, op0=ALU.mult, op1=ALU.add), R=[p, G.fpp, G.frb], W=[pr])
            kb.op("dve", lambda: nc.vector.tensor_scalar(out=tt[:], in0=pr[:], scalar1=1.0 / TWO_PI, scalar2=MAGIC, op0=ALU.mult, op1=ALU.add), R=[pr], W=[tt])
            kb.op("dve", lambda: nc.vector.tensor_scalar(out=tt[:], in0=tt[:], scalar1=-MAGIC, scalar2=-TWO_PI, op0=ALU.add, op1=ALU.mult), R=[tt], W=[tt])
            kb.op("pool", lambda: nc.gpsimd.tensor_tensor(out=pr[:], in0=pr[:], in1=tt[:], op=ALU.add), R=[pr, tt], W=[pr])
            if li < 2:
                h = hA[li]
                kb.op("act", lambda: nc.scalar.activation(out=h[:], in_=pr[:], func=AF.Sin), R=[pr], W=[h])
                src = h
            else:
                kb.op("act", lambda: nc.scalar.activation(out=h3T[:, j * SUB:(j + 1) * SUB], in_=pr[:], func=AF.Sin), R=[pr], W=[h3T])


class FFT:
    def __init__(self, G, es, with_ga=False):
        nc, kb = G.nc, G.kb
        self.G = G
        self.FA_t = kb.sbt(es, "FA", [128, 16384], BF16)
        self.FA = Tl(self.FA_t)
        if with_ga:
            self.GA = kb.sb(es, "GA", [128, 8192], BF16)
            kb.dma("sp", self.GA[:], G.GA_d, W=[self.GA])
        self.F128 = kb.sb(es, "F128", [128, 384], BF16)
        kb.dma("sp", self.F128[:], G.F128_d, W=[self.F128])
        self.UA_t = kb.sbt(es, "UA", [128, 16384], BF16)
        self.C_t = kb.sbt(es, "C", [128, 16384], BF16)
        self.C2_t = kb.sbt(es, "C2", [128, 16384], BF16)
        self.psT = [kb.ps(es, f"fpsT{i}", [128, 1024], BF16) for i in range(2)]
        self.psA = [kb.ps(es, f"fpsA{i}", [128, 512], F32) for i in range(2)]
        self.psU = [kb.ps(es, f"fpsU{i}", [128, 512], F32) for i in range(4)]
        self.ev = 0
        self.fa_is = None

    def load_FA(self, which):
        G = self.G
        if self.fa_is == which:
            return
        if which == "FA":
            G.kb.dma("sp", self.FA_t[:, 0:8192], G.FA_d[:, 0:8192], W=[self.FA])
            G.kb.dma("sp", self.FA_t[:, 8192:16384], G.FA_d[:, 8192:16384], W=[self.FA])
        self.fa_is = which

    def evac(self, out, in_, R, W):
        nc, kb = self.G.nc, self.G.kb
        self.ev += 1
        if self.ev % 2:
            kb.op("act", lambda: nc.scalar.copy(out=out, in_=in_), R=R, W=W)
        else:
            kb.op("dve", lambda: nc.vector.tensor_copy(out=out, in_=in_), R=R, W=W)

    def forward(self, SRC, SRC_t, nA, consume):
        G = self.G; nc, kb = G.nc, G.kb
        ident = G.ident
        UA = [Tl(self.UA_t[:, g * 1024:(g + 1) * 1024]) for g in range(16)]
        C = [Tl(self.C_t[:, g * 512:(g + 1) * 512]) for g in range(32)]
        C2 = [Tl(None) for g in range(16)]
        self.load_FA("FA")
        W = nA * 128
        for g in range(16):
            pt = self.psT[g % 2]
            for j in range(8):
                b = 8 * g + j
                kb.op("pe", lambda: nc.tensor.transpose(out=pt[0:nA, j * 128:(j + 1) * 128], in_=SRC_t[:, b:W:128], identity=ident[:]), R=[SRC, ident], W=[pt])
            self.evac(UA[g][0:nA, :], pt[0:nA, :], [pt], [UA[g]])
        for g in range(32):
            pa = self.psA[g % 2]
            for j in range(4):
                b = 4 * g + j
                kb.op("pe", lambda: nc.tensor.matmul(pa[:, j * 128:(j + 1) * 128], lhsT=self.FA_t[0:nA, b * 128:(b + 1) * 128], rhs=self.UA_t[0:nA, b * 128:(b + 1) * 128], start=True, stop=True),
                      R=[self.FA, UA[b // 8]], W=[pa])
            self.evac(C[g][:, :], pa[:, :], [pa], [C[g]])
        for g in range(16):
            pt = self.psT[g % 2]
            for j in range(8):
                c = 8 * g + j
                kb.op("pe", lambda: nc.tensor.transpose(out=pt[:, j * 128:(j + 1) * 128], in_=self.C_t[:, c:16384:128], identity=ident[:]), R=C + [ident], W=[pt])
            self.evac(self.C2_t[:, g * 1024:(g + 1) * 1024], pt[:, :], [pt], [C2[g]])
        C2k = self.C2_t[:, :].rearrange("p (c k) -> p c k", k=128)
        Cc = self.F128[:, 0:128]; Ss = self.F128[:, 128:256]; nSs = self.F128[:, 256:384]
        for g in range(16):
            k0 = 4 * g
            Cr = C2k[:, :, k0:k0 + 4]; Ci = C2k[:, :, 64 + k0:64 + k0 + 4]
            pr = self.psU[(2 * g) % 4]; pi = self.psU[(2 * g + 1) % 4]
            kb.op("pe", lambda: nc.tensor.matmul(pr[:, :], lhsT=Cc, rhs=Cr, start=True, stop=False), R=C2 + [self.F128], W=[pr])
            kb.op("pe", lambda: nc.tensor.matmul(pr[:, :], lhsT=Ss, rhs=Ci, start=False, stop=True), R=C2 + [self.F128], W=[pr])
            kb.op("pe", lambda: nc.tensor.matmul(pi[:, :], lhsT=Cc, rhs=Ci, start=True, stop=False), R=C2 + [self.F128], W=[pi])
            kb.op("pe", lambda: nc.tensor.matmul(pi[:, :], lhsT=nSs, rhs=Cr, start=False, stop=True), R=C2 + [self.F128], W=[pi])
            consume(g, pr, pi)


def ph_filter_lat(G, l):
    nc, kb = G.nc, G.kb
    L = NL
    with ExitStack() as es:
        h3T = kb.sb(es, "h3T", [64, L], BF16)
        with ExitStack() as es2:
            filter_mlp(G, l, L, es2, h3T)
        kb.barrier()
        F = FFT(G, es)
        w4b = kb.sb(es, "w4b", [64, 1024], BF16)
        kb.dma("pool", w4b[:], G.f_w4[l], W=[w4b])
        KT_t = kb.sbt(es, "KT", [128, 2 * L], BF16)
        KT = Tl(KT_t)
        tg = [kb.sb(es, f"tg{i}", [128, 512], F32) for i in range(2)]
        dec = [kb.sb(es, f"dec{i}", [128, 512], F32) for i in range(2)]
        absb = kb.sb(es, "absb", [128, 4096], BF16)
        l1p = kb.sb(es, "l1p", [128, 4], F32)
        kst = [kb.sb(es, f"kst{i}", [128, 2, 512], F32) for i in range(2)]
        pk = F.psA
        for ch in range(4):
            kb.op("pool", lambda: nc.gpsimd.memset(KT_t[:, L:L + 1], 0.0), W=[KT])
            for j in range(L // 512):
                t0 = j * 512
                tgt = tg[j % 2]; dc = dec[j % 2]
                kb.dma("sp", tgt[:], G.tg[L][:, t0:t0 + 512], W=[tgt])
                kb.op("act", lambda: nc.scalar.activation(out=dc[:], in_=tgt[:], func=AF.Exp, scale=G.ndel[:, ch:ch + 1]), R=[tgt, G.ndel], W=[dc])
                for half in range(2):
                    p = pk[half]
                    c0 = half * 512 + ch * 128
                    kb.op("pe", lambda: nc.tensor.matmul(p[:, :], lhsT=w4b[:, c0:c0 + 128], rhs=h3T[:, t0:t0 + 512], start=True, stop=True), R=[w4b, h3T], W=[p])
                    if half == 0:
                        kb.op("dve", lambda: nc.vector.tensor_tensor(out=KT_t[:, t0:t0 + 512], in0=p[:, :], in1=dc[:], op=ALU.mult), R=[p, dc], W=[KT])
                    elif t0 == 0:
                        kb.op("dve", lambda: nc.vector.scalar_tensor_tensor(out=KT_t[:, 2 * L - 1:2 * L - 512:-1], in0=p[:, 1:512], scalar=-1.0, in1=dc[:, 1:512], op0=ALU.mult, op1=ALU.mult),
                              R=[p, dc], W=[KT])
                    else:
                        kb.op("dve", lambda: nc.vector.scalar_tensor_tensor(out=KT_t[:, 2 * L - t0:2 * L - t0 - 512:-1], in0=p[:, :], scalar=-1.0, in1=dc[:], op0=ALU.mult, op1=ALU.mult),
                              R=[p, dc], W=[KT])
            for q in range(4):
                kb.op("act", lambda: nc.scalar.activation(out=absb[:], in_=KT_t[:, q * 4096:(q + 1) * 4096], func=AF.Abs), R=[KT], W=[absb])
                kb.op("dve", lambda: nc.vector.reduce_sum(out=l1p[:, q:q + 1], in_=absb[:], axis=AX.X), R=[absb], W=[l1p])
            kb.op("dve", lambda: nc.vector.reduce_sum(out=G.rl1[:, ch:ch + 1], in_=l1p[:], axis=AX.X), R=[l1p], W=[G.rl1])
            kb.op("dve", lambda: nc.vector.reciprocal(out=G.rl1[:, ch:ch + 1], in_=G.rl1[:, ch:ch + 1]), R=[G.rl1], W=[G.rl1])

            def consume(g, pr, pi):
                ks = kst[g % 2]
                kb.op("act", lambda: nc.scalar.copy(out=ks[:, 0, :], in_=pr[:, :]), R=[pr], W=[ks])
                kb.op("dve", lambda: nc.vector.tensor_copy(out=ks[:, 1, :], in_=pi[:, :]), R=[pi], W=[ks])
                kb.dma("sp", G.KH[ch, :, g, :, :], ks[:], R=[ks])

            F.forward(KT, KT_t, 128, consume)


def conv3(G, l, col, p, out, T):
    nc, kb = G.nc, G.kb
    w = lambda k: G.pp[:, l, PP_HCW + col * 3 + k: PP_HCW + col * 3 + k + 1]
    b = G.pp[:, l, PP_HCB + col: PP_HCB + col + 1]
    kb.op("dve", lambda: nc.vector.tensor_scalar(out=out[:, 0:T], in0=p[:, 0:T], scalar1=w(0), scalar2=b, op0=ALU.mult, op1=ALU.add), R=[p, G.pp], W=[out])
    for k in (1, 2):
        kb.op("dve", lambda: nc.vector.scalar_tensor_tensor(out=out[:, 0:T], in0=p[:, k:k + T], scalar=w(k), in1=out[:, 0:T], op0=ALU.mult, op1=ALU.add), R=[p, out, G.pp], W=[out])


def load_halo1(G, s, p, row0, t0, T):
    nc, kb = G.nc, G.kb
    N = s.N
    lo = max(t0 - 1, 0); hi = min(t0 + T + 1, N)
    if t0 == 0:
        kb.op("pool", lambda: nc.gpsimd.memset(p[:, 0:1], 0.0), W=[p])
    if t0 + T == N:
        kb.op("pool", lambda: nc.gpsimd.memset(p[:, T + 1:T + 2], 0.0), W=[p])
    kb.dma("sp", p[:, lo - (t0 - 1): hi - (t0 - 1)], s.P[row0:row0 + 128, lo:hi], W=[p])


def ph_hyena_lat(G, l):
    nc, kb = G.nc, G.kb
    s = G.lat
    L = NL; T = 512
    with ExitStack() as es:
        F = FFT(G, es, with_ga=True)
        U_t = kb.sbt(es, "U", [128, L], BF16)
        U = Tl(U_t)
        pt_ = [kb.sb(es, f"hp{i}", [128, T + 2], BF16) for i in range(4)]
        cf = [kb.sb(es, f"hc{i}", [128, T], F32) for i in range(4)]
        Kt = [kb.sb(es, f"Kt{i}", [128, 2, 512], F32) for i in range(2)]
        tm = [kb.sb(es, f"sm{i}", [128, 512], F32) for i in range(4)]
        Yb = [kb.sb(es, f"Yb{i}", [128, 2, 512], BF16) for i in range(2)]
        yo = [kb.sb(es, f"yo{i}", [128, T], F32) for i in range(2)]
        Cc = F.F128[:, 0:128]; Ss = F.F128[:, 128:256]; nSs = F.F128[:, 256:384]
        YT = F.UA_t[:, :].bitcast(F32)
        YTv = YT.rearrange("p (a b) -> p b a", b=128)
        dg_t = kb.sbt(es, "dg", [128, 9, 128], BF16)
        dg = Tl(dg_t)

        def conv_pe(i, ch, p, ps):
            for k in range(3):
                kb.op("pe", lambda: nc.tensor.matmul(ps[:, 0:T], lhsT=dg_t[:, i * 3 + k, :], rhs=p[:, k:k + T], start=(k == 0), stop=(k == 2)), R=[dg, p], W=[ps])

        for ch in range(4):
            for i in range(3):
                for k in range(3):
                    col = PP_HCW + (4 * i + ch) * 3 + k
                    kb.op("dve", lambda: nc.vector.tensor_scalar(out=dg_t[:, i * 3 + k, :], in0=G.ident[:], scalar1=G.pp[:, l, col:col + 1], scalar2=None, op0=ALU.mult), R=[G.ident, G.pp], W=[dg])
            for j in range(L // T):
                t0 = j * T
                p1 = pt_[(2 * j) % 4]; p2 = pt_[(2 * j + 1) % 4]; c1 = cf[j % 4]
                pa = F.psU[(2 * j) % 4]; pb = F.psU[(2 * j + 1) % 4]
                load_halo1(G, s, p1, 512 + ch * 128, t0, T)
                load_halo1(G, s, p2, 1024 + ch * 128, t0, T)
                conv_pe(1, ch, p1, pa)
                conv_pe(2, ch, p2, pb)
                kb.op("act", lambda: nc.scalar.activation(out=c1[:], in_=pa[:, 0:T], func=AF.Identity, bias=G.pp[:, l, PP_HCB + 4 + ch:PP_HCB + 5 + ch], scale=1.0), R=[pa, G.pp], W=[c1])
                kb.op("dve", lambda: nc.vector.scalar_tensor_tensor(out=U_t[:, t0:t0 + T], in0=pb[:, 0:T], scalar=G.pp[:, l, PP_HCB + 8 + ch:PP_HCB + 9 + ch], in1=c1[:], op0=ALU.add, op1=ALU.mult),
                      R=[pb, G.pp, c1], W=[U])
            Dg = [Tl(None) for g in range(16)]

            def consume(g, pr, pi):
                kt = Kt[g % 2]; yb = Yb[g % 2]
                k0 = 4 * g
                kb.dma("sp", kt[:], G.KH[ch, :, g, :, :], W=[kt])
                t1, t2, t3, t4 = tm
                kb.op("dve", lambda: nc.vector.tensor_tensor(out=t1[:], in0=pr[:, :], in1=kt[:, 0, :], op=ALU.mult), R=[pr, kt], W=[t1])
                kb.op("dve", lambda: nc.vector.tensor_tensor(out=t2[:], in0=pi[:, :], in1=kt[:, 1, :], op=ALU.mult), R=[pi, kt], W=[t2])
                kb.op("pool", lambda: nc.gpsimd.tensor_tensor(out=yb[:, 0, :], in0=t1[:], in1=t2[:], op=ALU.subtract), R=[t1, t2], W=[yb])
                kb.op("dve", lambda: nc.vector.tensor_tensor(out=t3[:], in0=pr[:, :], in1=kt[:, 1, :], op=ALU.mult), R=[pr, kt], W=[t3])
                kb.op("dve", lambda: nc.vector.tensor_tensor(out=t4[:], in0=pi[:, :], in1=kt[:, 0, :], op=ALU.mult), R=[pi, kt], W=[t4])
                kb.op("pool", lambda: nc.gpsimd.tensor_tensor(out=yb[:, 1, :], in0=t3[:], in1=t4[:], op=ALU.add), R=[t3, t4], W=[yb])
                pd0 = F.psA[0]; pd1 = F.psA[1]
                kb.op("pe", lambda: nc.tensor.matmul(pd0[:, :], lhsT=Cc, rhs=yb[:, 0, :], start=True, stop=False), R=[yb, F.F128], W=[pd0])
                kb.op("pe", lambda: nc.tensor.matmul(pd0[:, :], lhsT=nSs, rhs=yb[:, 1, :], start=False, stop=True), R=[yb, F.F128], W=[pd0])
                kb.op("pe", lambda: nc.tensor.matmul(pd1[:, :], lhsT=Ss, rhs=yb[:, 0, :], start=True, stop=False), R=[yb, F.F128], W=[pd1])
                kb.op("pe", lambda: nc.tensor.matmul(pd1[:, :], lhsT=Cc, rhs=yb[:, 1, :], start=False, stop=True), R=[yb, F.F128], W=[pd1])
                kb.op("act", lambda: nc.scalar.copy(out=F.C_t[:, k0 * 128:k0 * 128 + 512].rearrange("p (k c) -> p k c", c=128), in_=pd0[:, :].rearrange("p (c k) -> p k c", k=4)), R=[pd0], W=[Dg[g]])
                kb.op("act", lambda: nc.scalar.copy(out=F.C_t[:, (64 + k0) * 128:(64 + k0) * 128 + 512].rearrange("p (k c) -> p k c", c=128), in_=pd1[:, :].rearrange("p (c k) -> p k c", k=4)), R=[pd1], W=[Dg[g]])

            F.forward(U, U_t, 64, consume)
            D2 = [Tl(None) for g in range(16)]
            D2b = F.C2_t[:, :].rearrange("p (c b) -> p b c", b=128)
            for g in range(16):
                pt = F.psT[g % 2]
                for j in range(8):
                    c = 8 * g + j
                    kb.op("pe", lambda: nc.tensor.transpose(out=pt[:, j * 128:(j + 1) * 128], in_=F.C_t[:, c:16384:128], identity=G.ident[:]), R=Dg + [G.ident], W=[pt])
                F.evac(F.C2_t[:, g * 1024:(g + 1) * 1024], pt[:, :], [pt], [D2[g]])
            YTt = [Tl(None) for g in range(16)]
            for g in range(16):
                py = F.psU[g % 4]
                for j in range(8):
                    b = 8 * g + j
                    kb.op("pe", lambda: nc.tensor.matmul(py[:, j * 64:(j + 1) * 64], lhsT=D2b[:, b, :], rhs=F.GA[:, b * 64:(b + 1) * 64], start=True, stop=True),
                          R=D2 + [F.GA], W=[py])
                kb.op("act", lambda: nc.scalar.activation(out=YTv[:, 8 * g:8 * g + 8, :], in_=py[:, :].rearrange("p (b a) -> p b a", a=64), func=AF.Copy, scale=G.rl1[:, ch:ch + 1]),
                      R=[py, G.rl1], W=[YTt[g]])
            for j in range(L // T):
                t0 = j * T
                p0 = pt_[j % 4]; y = yo[j % 2]; pc = F.psA[j % 2]
                load_halo1(G, s, p0, ch * 128, t0, T)
                conv_pe(0, ch, p0, pc)
                kb.op("dve", lambda: nc.vector.scalar_tensor_tensor(out=y[:], in0=U_t[:, t0:t0 + T], scalar=G.pp[:, l, PP_HBIAS + ch:PP_HBIAS + ch + 1], in1=YT[:, t0:t0 + T], op0=ALU.mult, op1=ALU.add),
                      R=[U, G.pp] + YTt, W=[y])
                kb.op("dve", lambda: nc.vector.scalar_tensor_tensor(out=y[:], in0=pc[:, 0:T], scalar=G.pp[:, l, PP_HCB + ch:PP_HCB + ch + 1], in1=y[:], op0=ALU.add, op1=ALU.mult),
                      R=[pc, G.pp, y], W=[y])
                kb.dma("act", s.YH[ch * 128:(ch + 1) * 128, t0:t0 + T], y[:], R=[y])
            kb.barrier()


def ph_hyena_ctx(G, l):
    nc, kb = G.nc, G.kb
    s = G.cx; L = NCX
    with ExitStack() as es:
        h3T = kb.sb(es, "h3Tc", [64, L], BF16)
        with ExitStack() as es2:
            filter_mlp(G, l, L, es2, h3T)
        kb.barrier()
        w4b = kb.sb(es, "w4bc", [64, 1024], BF16)
        kb.dma("pool", w4b[:], G.f_w4[l], W=[w4b])
        tg = kb.sb(es, "tgc", [128, L], F32)
        kb.dma("sp", tg[:], G.tg[L], W=[tg])
        Fc = kb.sb(es, "Fc", [128, 4, 512], BF16); Gc = kb.sb(es, "Gc", [128, 4, 256], BF16); idf = kb.sb(es, "idf", [128, 128], F32)
        kb.dma("sp", Fc[:], G.Fc_d.rearrange("p (a b) -> p a b", a=4), W=[Fc])
        kb.dma("sp", Gc[:], G.Gc_d.rearrange("p (a b) -> p a b", a=4), W=[Gc])
        kb.dma("sp", idf[:], G.identf_d, W=[idf])
        dec = kb.sb(es, "decc", [128, L], F32)
        KT = [kb.sb(es, f"KTc{i}", [128, 2 * L], BF16) for i in range(4)]
        absb = kb.sb(es, "absc", [128, 2 * L], BF16)
        rl = kb.sb(es, "rlc", [128, 4], F32)
        pp_ = [kb.sb(es, f"cp{i}", [128, L + 2], BF16) for i in range(3)]
        x0c = [kb.sb(es, f"x0c{i}", [128, L], F32) for i in range(4)]
        cc_ = [kb.sb(es, f"cc{i}", [128, L], F32) for i in range(2)]
        uf = [kb.sb(es, f"uf{i}", [128, L], F32) for i in range(4)]
        ub = [kb.sb(es, f"ub{i}", [128, L], BF16) for i in range(4)]
        utok = kb.sb(es, "utok", [128, 2, 512], BF16)
        ktok = kb.sb(es, "ktok", [128, 4, 512], BF16)
        Kh = kb.sb(es, "Khc", [128, 4, 512], F32)
        Yh = kb.sb(es, "Yhc", [128, 4, 512], BF16)
        tm = [kb.sb(es, f"tmc{i}", [128, 512], F32) for i in range(4)]
        ytok = kb.sb(es, "ytok", [128, 2, 512], F32)
        yo = [kb.sb(es, f"yoc{i}", [128, L], F32) for i in range(2)]
        psb = [kb.ps(es, f"pcb{i}", [128, 512], BF16) for i in range(2)]
        psf = [kb.ps(es, f"pcf{i}", [128, 512], F32) for i in range(4)]
        for ch in range(4):
            kb.op("act", lambda: nc.scalar.activation(out=dec[:], in_=tg[:], func=AF.Exp, scale=G.ndel[:, ch:ch + 1]), R=[tg, G.ndel], W=[dec])
            K_ = KT[ch]
            kb.op("pool", lambda: nc.gpsimd.memset(K_[:, L:L + 1], 0.0), W=[K_])
            for half in range(2):
                p = psf[half]; c0 = half * 512 + ch * 128
                kb.op("pe", lambda: nc.tensor.matmul(p[:, 0:L], lhsT=w4b[:, c0:c0 + 128], rhs=h3T[:, 0:L], start=True, stop=True), R=[w4b, h3T], W=[p])
                if half == 0:
                    kb.op("dve", lambda: nc.vector.tensor_tensor(out=K_[:, 0:L], in0=p[:, 0:L], in1=dec[:], op=ALU.mult), R=[p, dec], W=[K_])
                else:
                    kb.op("dve", lambda: nc.vector.scalar_tensor_tensor(out=K_[:, 2 * L - 1:L:-1], in0=p[:, 1:L], scalar=-1.0, in1=dec[:, 1:L], op0=ALU.mult, op1=ALU.mult), R=[p, dec], W=[K_])
            kb.op("act", lambda: nc.scalar.activation(out=absb[:], in_=K_[:], func=AF.Abs), R=[K_], W=[absb])
            kb.op("dve", lambda: nc.vector.reduce_sum(out=rl[:, ch:ch + 1], in_=absb[:], axis=AX.X), R=[absb], W=[rl])
            load_halo1(G, s, pp_[0], ch * 128, 0, L)
            conv3(G, l, ch, pp_[0], x0c[ch], L)
            for i, row0 in ((1, 512 + ch * 128), (2, 1024 + ch * 128)):
                load_halo1(G, s, pp_[i], row0, 0, L)
                conv3(G, l, 4 * i + ch, pp_[i], cc_[i - 1], L)
            kb.op("dve", lambda: nc.vector.tensor_tensor(out=uf[ch][:], in0=cc_[0][:], in1=cc_[1][:], op=ALU.mult), R=[cc_[0], cc_[1]], W=[uf[ch]])
            kb.op("act", lambda: nc.scalar.copy(out=ub[ch][:], in_=uf[ch][:]), R=[uf[ch]], W=[ub[ch]])
            pt = psb[ch % 2]
            for tb in range(2):
                kb.op("pe", lambda: nc.tensor.transpose(out=pt[:, tb * 128:(tb + 1) * 128], in_=ub[ch][:, tb * 128:(tb + 1) * 128], identity=G.ident[:]), R=[ub[ch], G.ident], W=[pt])
            kb.op("act", lambda: nc.scalar.copy(out=utok[:, :, ch * 128:(ch + 1) * 128], in_=pt[:, 0:256].rearrange("p (a b) -> p a b", a=2)), R=[pt], W=[utok])
            pt2 = psb[(ch + 1) % 2]
            for nb_ in range(4):
                kb.op("pe", lambda: nc.tensor.transpose(out=pt2[:, nb_ * 128:(nb_ + 1) * 128], in_=K_[:, nb_ * 128:(nb_ + 1) * 128], identity=G.ident[:]), R=[K_, G.ident], W=[pt2])
            kb.op("dve", lambda: nc.vector.tensor_copy(out=ktok[:, :, ch * 128:(ch + 1) * 128], in_=pt2[:, :].rearrange("p (a b) -> p a b", a=4)), R=[pt2], W=[ktok])
        kb.op("dve", lambda: nc.vector.reciprocal(out=rl[:], in_=rl[:]), R=[rl], W=[rl])
        for m in range(4):
            p = psf[m]
            for nb_ in range(4):
                kb.op("pe", lambda: nc.tensor.matmul(p[:, :], lhsT=Fc[:, nb_, m * 128:(m + 1) * 128], rhs=ktok[:, nb_, :], start=(nb_ == 0), stop=(nb_ == 3)), R=[Fc, ktok], W=[p])
            if m % 2:
                kb.op("act", lambda: nc.scalar.copy(out=Kh[:, m, :], in_=p[:, :]), R=[p], W=[Kh])
            else:
                kb.op("dve", lambda: nc.vector.tensor_copy(out=Kh[:, m, :], in_=p[:, :]), R=[p], W=[Kh])
        for m in range(4):
            p = psf[m]
            for tb in range(2):
                kb.op("pe", lambda: nc.tensor.matmul(p[:, :], lhsT=Fc[:, tb, m * 128:(m + 1) * 128], rhs=utok[:, tb, :], start=(tb == 0), stop=(tb == 1)), R=[Fc, utok], W=[p])
        for q in range(2):
            pr = psf[q]; pi = psf[2 + q]
            t1, t2, t3, t4 = tm
            kb.op("dve", lambda: nc.vector.tensor_tensor(out=t1[:], in0=pr[:, :], in1=Kh[:, q, :], op=ALU.mult), R=[pr, Kh], W=[t1])
            kb.op("dve", lambda: nc.vector.tensor_tensor(out=t2[:], in0=pi[:, :], in1=Kh[:, 2 + q, :], op=ALU.mult), R=[pi, Kh], W=[t2])
            kb.op("pool", lambda: nc.gpsimd.tensor_tensor(out=Yh[:, q, :], in0=t1[:], in1=t2[:], op=ALU.subtract), R=[t1, t2], W=[Yh])
            kb.op("dve", lambda: nc.vector.tensor_tensor(out=t3[:], in0=pr[:, :], in1=Kh[:, 2 + q, :], op=ALU.mult), R=[pr, Kh], W=[t3])
            kb.op("dve", lambda: nc.vector.tensor_tensor(out=t4[:], in0=pi[:, :], in1=Kh[:, q, :], op=ALU.mult), R=[pi, Kh], W=[t4])
            kb.op("pool", lambda: nc.gpsimd.tensor_tensor(out=Yh[:, 2 + q, :], in0=t3[:], in1=t4[:], op=ALU.add), R=[t3, t4], W=[Yh])
        for tb in range(2):
            p = psf[tb]
            for kc in range(4):
                kb.op("pe", lambda: nc.tensor.matmul(p[:, :], lhsT=Gc[:, kc, tb * 128:(tb + 1) * 128], rhs=Yh[:, kc, :], start=(kc == 0), stop=(kc == 3)), R=[Gc, Yh], W=[p])
            kb.op("act", lambda: nc.scalar.copy(out=ytok[:, tb, :], in_=p[:, :]), R=[p], W=[ytok])
        for ch in range(4):
            p = psf[2 + ch % 2]
            for tb in range(2):
                kb.op("pe", lambda: nc.tensor.matmul(p[:, tb * 128:(tb + 1) * 128], lhsT=ytok[:, tb, ch * 128:(ch + 1) * 128], rhs=idf[:], start=True, stop=True), R=[ytok, idf], W=[p])
            y = yo[ch % 2]
            kb.op("dve", lambda: nc.vector.tensor_scalar(out=y[:], in0=p[:, 0:L], scalar1=rl[:, ch:ch + 1], scalar2=None, op0=ALU.mult), R=[p, rl], W=[y])
            kb.op("dve", lambda: nc.vector.scalar_tensor_tensor(out=y[:], in0=uf[ch][:], scalar=G.pp[:, l, PP_HBIAS + ch:PP_HBIAS + ch + 1], in1=y[:], op0=ALU.mult, op1=ALU.add),
                  R=[uf[ch], G.pp, y], W=[y])
            kb.op("dve", lambda: nc.vector.tensor_tensor(out=y[:], in0=y[:], in1=x0c[ch][:], op=ALU.mult), R=[y, x0c[ch]], W=[y])
            kb.dma("act", s.YH[ch * 128:(ch + 1) * 128, :], y[:], R=[y])


def ph_out(G, l, s, last):
    nc, kb = G.nc, G.kb
    N = s.N; TT = min(512, N); nt = N // TT; nb = TT // 128
    with ExitStack() as es:
        gbc = kb.sb(es, "gbc", [128, D], F32)
        kb.dma("sp", gbc[:], G.modd[l, s.idx, 2].partition_broadcast(128), W=[gbc])
        Wo_t = kb.sbt(es, "Wo", [128, 8, D], BF16)
        Wo = [Tl(Wo_t[:, k, :]) for k in range(8)]
        wtmp = [kb.sb(es, f"wtmp{i}", [128, D], F32) for i in range(2)]
        for k in range(8):
            wt = wtmp[k % 2]
            kb.dma("sp", wt[:], G.w_out[l, k * 128:(k + 1) * 128, :], W=[wt])
            kb.op("pool", lambda: nc.gpsimd.tensor_tensor(out=Wo[k][:, :], in0=wt[:], in1=gbc[:], op=ALU.mult), R=[wt, gbc], W=[Wo[k]])
        if last:
            fg = kb.sb(es, "fg", [128, D], F32)
            kb.dma("sp", fg[:], G.final_g.partition_broadcast(128), W=[fg])
            junk = kb.sb(es, "ojunk", [128, D], BF16)
            fss = [kb.sb(es, f"fss{i}", [128, 4], F32) for i in range(2)]
            xo = [kb.sb(es, f"xo{i}", [128, D], F32) for i in range(2)]
        yin = [kb.sb(es, f"yin{i}", [128, 8, TT], F32) for i in range(2)]
        zin = [kb.sb(es, f"zin{i}", [128, 8, TT], BF16) for i in range(2)]
        sqb = [kb.sb(es, f"sqb{i}", [128, TT], BF16) for i in range(2)]
        rstd = [kb.sb(es, f"rstd{i}", [128, 2, TT], F32) for i in range(2)]
        sz = [kb.sb(es, f"sz{i}", [128, TT], F32) for i in range(2)]
        t1 = [kb.sb(es, f"ot1{i}", [128, TT], F32) for i in range(2)]
        ym_t = [kb.sbt(es, f"ym{i}", [128, 8, TT], BF16) for i in range(2)]
        ym = [[Tl(ym_t[i][:, k, :]) for k in range(8)] for i in range(2)]
        xin = [kb.sb(es, f"oxin{i}", [128, 4, D], F32) for i in range(2)]
        pms = [kb.ps(es, f"pms{i}", [128, 512], F32) for i in range(2)]
        po = [kb.ps(es, f"po{i}", [128, 512], F32) for i in range(4)]
        st = dict(qi=0, pi=0)

        def stage1(it):
            t0 = it * TT
            yi = yin[it % 2]; zi = zin[it % 2]; rs = rstd[it % 2]; ymi = ym[it % 2]; xi = xin[it % 2]
            kb.dma("sp", yi[:, 0:4, :], s.YH[:, t0:t0 + TT].rearrange("(k p) t -> p k t", p=128), W=[yi])
            kb.dma("sp", yi[:, 4:8, :], s.YR[:, t0:t0 + TT].rearrange("(k p) t -> p k t", p=128), W=[yi])
            kb.dma("sp", zi[:, 0:4, :], s.P[1536:2048, t0:t0 + TT].rearrange("(k p) t -> p k t", p=128), W=[zi])
            kb.dma("sp", zi[:, 4:8, :], s.P[2560:3072, t0:t0 + TT].rearrange("(k p) t -> p k t", p=128), W=[zi])
            kb.dma("sp", xi[:, 0:nb, :], s.X[t0:t0 + TT, :].rearrange("(k p) d -> p k d", p=128), W=[xi])
            for grp in range(2):
                pm = pms[grp]
                for k in range(4):
                    sq = sqb[st["qi"] % 2]; st["qi"] += 1
                    kb.op("act", lambda: nc.scalar.activation(out=sq[:], in_=yi[:, grp * 4 + k, :], func=AF.Square), R=[yi], W=[sq])
                    kb.op("pe", lambda: nc.tensor.matmul(pm[:, 0:TT], lhsT=G.ones[:], rhs=sq[:], start=(k == 0), stop=(k == 3)), R=[G.ones, sq], W=[pm])
                kb.op("act", lambda: nc.scalar.activation(out=rs[:, grp, :], in_=pm[:, 0:TT], func=AF.Sqrt, bias=EPS, scale=1.0 / DH), R=[pm], W=[rs])
            kb.op("dve", lambda: nc.vector.reciprocal(out=rs[:], in_=rs[:]), R=[rs], W=[rs])
            for k8 in range(8):
                grp = k8 // 4
                z = sz[k8 % 2]; tt = t1[k8 % 2]
                gcol = (PP_GNH if grp == 0 else PP_GNR) + (k8 % 4)
                kb.op("act", lambda: nc.scalar.activation(out=z[:], in_=zi[:, k8, :], func=AF.Silu), R=[zi], W=[z])
                kb.op("dve", lambda: nc.vector.scalar_tensor_tensor(out=tt[:], in0=yi[:, k8, :], scalar=G.pp[:, l, gcol:gcol + 1], in1=rs[:, grp, :], op0=ALU.mult, op1=ALU.mult),
                      R=[yi, G.pp, rs], W=[tt])
                kb.op("pool", lambda: nc.gpsimd.tensor_tensor(out=ymi[k8][:, :], in0=tt[:], in1=z[:], op=ALU.mult), R=[tt, z], W=[ymi[k8]])

        def stage2(it):
            t0 = it * TT
            ymi = ym[it % 2]; xi = xin[it % 2]
            for tb in range(nb):
                for dn in range(2):
                    p = po[st["pi"] % 4]; st["pi"] += 1
                    for k8 in range(8):
                        kb.op("pe", lambda: nc.tensor.matmul(p[:, :], lhsT=ymi[k8][:, tb * 128:(tb + 1) * 128], rhs=Wo[k8][:, dn * 512:(dn + 1) * 512], start=(k8 == 0), stop=(k8 == 7)),
                              R=[ymi[k8], Wo[k8]], W=[p])
                    kb.op("dve", lambda: nc.vector.tensor_tensor(out=xi[:, tb, dn * 512:(dn + 1) * 512], in0=p[:, :], in1=xi[:, tb, dn * 512:(dn + 1) * 512], op=ALU.add), R=[p, xi], W=[xi])
            if not last:
                kb.dma("act", s.X[t0:t0 + TT, :].rearrange("(k p) d -> p k d", p=128), xi[:, 0:nb, :], R=[xi])
            else:
                fs = fss[it % 2]
                kb.op("pool", lambda: nc.gpsimd.memset(fs[:], 0.0), W=[fs])
                for tb in range(nb):
                    kb.op("act", lambda: nc.scalar.activation(out=junk[:], in_=xi[:, tb, :], func=AF.Square, accum_out=fs[:, tb:tb + 1]), R=[xi], W=[junk, fs])
                kb.op("act", lambda: nc.scalar.activation(out=fs[:], in_=fs[:], func=AF.Sqrt, bias=EPS, scale=1.0 / D), R=[fs], W=[fs])
                kb.op("dve", lambda: nc.vector.reciprocal(out=fs[:], in_=fs[:]), R=[fs], W=[fs])
                for tb in range(nb):
                    o = xo[tb % 2]
                    kb.op("dve", lambda: nc.vector.scalar_tensor_tensor(out=o[:], in0=xi[:, tb, :], scalar=fs[:, tb:tb + 1], in1=fg[:], op0=ALU.mult, op1=ALU.mult), R=[xi, fs, fg], W=[o])
                    kb.dma("act", G.out[t0 + tb * 128:t0 + (tb + 1) * 128, :], o[:], R=[o])

        stage1(0)
        for it in range(nt):
            if it + 1 < nt:
                stage1(it + 1)
            stage2(it)


_NC_CACHE = {}


def kernel(**inputs):
    inp = {k: np.asarray(v) for k, v in inputs.items()}
    if "nc" not in _NC_CACHE:
        _NC_CACHE["nc"] = build()
    nc = _NC_CACHE["nc"]
    in_maps = [pack_inputs(inp, b) for b in range(8)]
    res = run_bass_kernel_spmd(nc, in_maps, core_ids=list(range(8)))
    out = np.stack([np.asarray(res.results[b]["out"], dtype=np.float32) for b in range(8)], axis=0)
    return out
```

```python
import math
from contextlib import ExitStack
import numpy as np
import ml_dtypes
import concourse.bass as bass
import concourse.mybir as mybir
from concourse.bass_utils import run_bass_kernel_spmd

F32 = mybir.dt.float32
BF16 = mybir.dt.bfloat16
ALU = mybir.AluOpType
AF = mybir.ActivationFunctionType
AX = mybir.AxisListType

D = 1024
DIN = 3072
NL = 8192
NCX = 256
DH = 512
DEPTH = 4
EPS = 1e-6
MFFT = 16384
MAGIC = 12582912.0
TWO_PI = 2.0 * math.pi


class Tl:
    __slots__ = ("t", "w", "r")

    def __init__(self, t):
        self.t = t
        self.w = None
        self.r = {}

    def __getitem__(self, k):
        return self.t[k]


class KB:
    def __init__(self, nc):
        self.nc = nc
        self.eng = {"pe": nc.tensor, "act": nc.scalar, "dve": nc.vector, "pool": nc.gpsimd, "sp": nc.sync}
        self.psem = {e: nc.alloc_semaphore(f"prog_{e}") for e in ("pe", "act", "dve", "pool")}
        self.pcnt = {e: 0 for e in self.psem}
        self.seen = {e: {} for e in self.eng}
        self.dq = {}
        for q, n in (("sp", 12), ("act", 6), ("pool", 6), ("dve", 2)):
            self.dq[q] = dict(sems=[nc.alloc_semaphore(f"dq_{q}{i}") for i in range(n)], cnt=[0] * n, idx=0)
        self.nins = 0

    def _wait(self, e, tok):
        key, val, sem = tok
        if self.seen[e].get(key, 0) >= val:
            return
        self.eng[e].wait_ge(sem, val)
        self.seen[e][key] = val
        self.nins += 1

    def _deps(self, e, R, W, dma=False):
        me = None if dma else "c:" + e
        for b in list(R) + list(W):
            t = b.w
            if t is not None and not (e == "pe" and t[0] == me):
                self._wait(e, t)
        for b in W:
            for t in b.r.values():
                if t[0] != me:
                    self._wait(e, t)

    def op(self, e, fn, R=(), W=()):
        self._deps(e, R, W)
        ins = fn()
        self.pcnt[e] += 1
        ins.then_inc(self.psem[e], 1)
        self.nins += 1
        tok = ("c:" + e, self.pcnt[e], self.psem[e])
        for b in W:
            b.w = tok
            b.r = {}
        for b in R:
            if b.w is not tok:
                b.r[tok[0]] = tok
        return tok

    def dma(self, q, out, in_, R=(), W=()):
        d = self.dq[q]
        e = q
        self._deps(e, R, W, True)
        i = d["idx"]
        d["idx"] = (i + 1) % len(d["sems"])
        sem = d["sems"][i]
        key = f"d:{q}:{i}"
        if d["cnt"][i] > 0:
            self._wait(e, (key, d["cnt"][i], sem))
        ins = self.eng[e].dma_start(out=out, in_=in_)
        d["cnt"][i] += 16
        ins.then_inc(sem, 16)
        self.nins += 1
        tok = (key, d["cnt"][i], sem)
        for b in W:
            b.w = tok
            b.r = {}
        for b in R:
            b.r[key] = tok
        return tok

    def barrier(self):
        toks = [("c:" + e, self.pcnt[e], self.psem[e]) for e in self.psem if self.pcnt[e] > 0]
        for q, d in self.dq.items():
            for i, sem in enumerate(d["sems"]):
                if d["cnt"][i] > 0:
                    toks.append((f"d:{q}:{i}", d["cnt"][i], sem))
        for e in self.eng:
            for t in toks:
                if t[0] != "c:" + e:
                    self._wait(e, t)

    def sbt(self, es, name, shape, dt):
        self.uid = getattr(self, "uid", 0) + 1
        return es.enter_context(self.nc.sbuf_tensor(f"s{self.uid}_{name}", list(shape), dt))

    def sb(self, es, name, shape, dt):
        return Tl(self.sbt(es, name, shape, dt))

    def ps(self, es, name, shape, dt):
        self.uid = getattr(self, "uid", 0) + 1
        return Tl(es.enter_context(self.nc.psum_tensor(f"p{self.uid}_{name}", list(shape), dt)))


_CONST_CACHE = {}


def _bf(a):
    return np.ascontiguousarray(a.astype(np.float32)).astype(ml_dtypes.bfloat16)


def host_consts():
    if _CONST_CACHE:
        return _CONST_CACHE
    c = {}
    rows, gw = NL // 64, 64
    r, col = np.meshgrid(np.arange(rows, dtype=np.float32), np.arange(gw, dtype=np.float32), indexing="ij")
    quarter = D // 4
    omega = (1.0 / (10000.0 ** (np.arange(quarter, dtype=np.float32) / np.float32(quarter)))).astype(np.float32)

    def emb(pos):
        ang = pos.reshape(-1, 1).astype(np.float32) * omega[None, :]
        return np.concatenate([np.sin(ang), np.cos(ang)], axis=-1)

    c["pos"] = np.concatenate([emb(r), emb(col)], axis=-1).astype(np.float32)
    c["ident_bf"] = _bf(np.eye(128))
    c["ident_f"] = np.eye(128, dtype=np.float32)
    c["ones_bf"] = _bf(np.ones((128, 128)))
    a = np.arange(128, dtype=np.float64)[:, None, None]
    b = np.arange(128, dtype=np.float64)[None, :, None]
    k1 = np.arange(64, dtype=np.float64)[None, None, :]
    th = TWO_PI * (128 * a + b) * (k1 + 0.5) / MFFT
    c["FA"] = _bf(np.concatenate([np.cos(th), -np.sin(th)], axis=-1).reshape(128, 128 * 128))
    a64 = np.arange(64, dtype=np.float64)[None, None, :]
    bb = np.arange(128, dtype=np.float64)[None, :, None]
    kk = np.arange(64, dtype=np.float64)[:, None, None]
    th2 = TWO_PI * (128 * a64 + bb) * (kk + 0.5) / MFFT
    c["GA"] = _bf(np.concatenate([(2.0 / MFFT) * np.cos(th2), -(2.0 / MFFT) * np.sin(th2)], axis=0).reshape(128, 128 * 64))
    ph = TWO_PI * np.outer(np.arange(128.0), np.arange(128.0)) / 128.0
    c["F128"] = _bf(np.stack([np.cos(ph), np.sin(ph), -np.sin(ph)], axis=1).reshape(128, 3 * 128))

    def feats(L):
        t = np.linspace(0.0, 1.0, L, dtype=np.float32)[:, None]
        w = (2.0 * math.pi * np.arange(L, dtype=np.float32)[:, None] / L).astype(np.float32)
        f = np.linspace(1e-4, 15, 16, dtype=np.float32)[None, :]
        z = np.concatenate([t, np.cos(f * w), -np.sin(f * w)], axis=-1).astype(np.float32)
        return np.ascontiguousarray(z.T), t[:, 0]

    c["zT_l"], tl = feats(NL)
    c["zT_c"], tcx = feats(NCX)
    c["tg_l"] = np.ascontiguousarray(np.broadcast_to(tl[None, :], (128, NL))).astype(np.float32)
    c["tg_c"] = np.ascontiguousarray(np.broadcast_to(tcx[None, :], (128, NCX))).astype(np.float32)
    max_decay = math.log(1e-2) / 0.3
    min_decay = math.log(1e-2) / 1.5
    deltas = np.abs(np.linspace(min_decay, max_decay, DH, dtype=np.float32))
    c["ndel"] = np.ascontiguousarray((-deltas).reshape(4, 128).T).astype(np.float32)
    n = np.arange(512, dtype=np.float64)[:, None]
    k = np.arange(256, dtype=np.float64)[None, :]
    th = TWO_PI * n * (k + 0.5) / 512.0
    Fc = np.concatenate([np.cos(th), -np.sin(th)], axis=1)
    c["Fc"] = _bf(Fc.reshape(4, 128, 512).transpose(1, 0, 2).reshape(128, 4 * 512))
    t = np.arange(256, dtype=np.float64)[None, :]
    kk_ = np.arange(256, dtype=np.float64)[:, None]
    th2 = TWO_PI * t * (kk_ + 0.5) / 512.0
    Gc = np.concatenate([(2.0 / 512) * np.cos(th2), -(2.0 / 512) * np.sin(th2)], axis=0)
    c["Gc"] = _bf(Gc.reshape(4, 128, 256).transpose(1, 0, 2).reshape(128, 4 * 256))
    _CONST_CACHE.update(c)
    return c


PP_HCW = 0
PP_HCB = 36
PP_HBIAS = 48
PP_GNH = 52
PP_GNR = 56
PP_RCW = 60
PP_RCB = 76
PP_RB = 80
PP_LAM = 96
NPP = 104


class NS:
    pass


def _chunked(v, nchunk):
    return np.ascontiguousarray(v.reshape(nchunk, 128).T)


def pack_inputs(inp, b):
    C = host_consts()
    f32 = np.float32
    m = {}
    m["x"] = np.ascontiguousarray(inp["x"][b], dtype=f32)
    m["ctx"] = np.ascontiguousarray(inp["ctx"][b], dtype=f32)
    cs = np.stack([inp["c"][b], inp["c_ctx"]], axis=-1).astype(f32)
    m["cs"] = np.ascontiguousarray(cs.reshape(8, 128, 2).transpose(1, 0, 2))
    for k in ("w_mod", "b_mod", "norm_g", "w_in", "w_out", "final_g", "f_w1", "f_w2", "f_w3", "f_w4"):
        m[k] = np.ascontiguousarray(inp[k], dtype=f32)
    pp = np.zeros((DEPTH, 128, NPP), f32)
    bd = np.zeros((DEPTH, 2, 2, 4, 128, 128), f32)
    fp = np.zeros((DEPTH, 64, 6), f32)
    for l in range(DEPTH):
        hcw = inp["hy_conv_w"][l]
        pp[l, :, PP_HCW:PP_HCW + 36] = hcw.T.reshape(12, 128, 3).transpose(1, 0, 2).reshape(128, 36)
        pp[l, :, PP_HCB:PP_HCB + 12] = _chunked(inp["hy_conv_b"][l], 12)
        pp[l, :, PP_HBIAS:PP_HBIAS + 4] = _chunked(inp["hy_bias"][l], 4)
        pp[l, :, PP_GNH:PP_GNH + 4] = _chunked(inp["br_norm_h"][l], 4)
        pp[l, :, PP_GNR:PP_GNR + 4] = _chunked(inp["br_norm_r"][l], 4)
        rcw = inp["rg_conv_w"][l]
        pp[l, :, PP_RCW:PP_RCW + 16] = rcw.T.reshape(4, 128, 4).transpose(1, 0, 2).reshape(128, 16)
        pp[l, :, PP_RCB:PP_RCB + 4] = _chunked(inp["rg_conv_b"][l], 4)
        for d in range(2):
            for g, (wk, bk) in enumerate((("rg_wa", "rg_ba"), ("rg_wx", "rg_bx"))):
                pp[l, :, PP_RB + (d * 2 + g) * 4: PP_RB + (d * 2 + g) * 4 + 4] = _chunked(inp[bk][l, d], 4)
                w = inp[wk][l, d]
                for ch in range(4):
                    for hh in range(2):
                        bd[l, d, g, ch, hh * 64:(hh + 1) * 64, hh * 64:(hh + 1) * 64] = w[ch * 2 + hh]
            pp[l, :, PP_LAM + d * 4: PP_LAM + d * 4 + 4] = _chunked(inp["rg_lam"][l, d], 4)
        fp[l, :, 0] = inp["f_b1"][l]
        fp[l, :, 1] = inp["f_b2"][l]
        fp[l, :, 2] = inp["f_b3"][l]
        fp[l, :, 3:6] = inp["f_freq"][l].T
    m["pp"] = np.ascontiguousarray(pp.transpose(1, 0, 2))
    m["bd"] = np.ascontiguousarray(bd.transpose(0, 1, 2, 4, 3, 5))
    m["fp"] = np.ascontiguousarray(fp.transpose(1, 0, 2))
    for k in ("pos", "ident_bf", "ident_f", "ones_bf", "FA", "GA", "F128", "Fc", "Gc", "zT_l", "zT_c", "tg_l", "tg_c", "ndel"):
        m[k] = C[k]
    return m


def build(dbg=(), phases=None, depth=DEPTH):
    nc = bass.Bass("TRN2", target_bir_lowering=False)
    kb = KB(nc)
    G = NS()
    G.nc = nc
    G.kb = kb

    def inp(name, shape, dt=F32):
        return nc.dram_tensor(name, list(shape), dt, kind="ExternalInput").ap()

    def scr(name, shape, dt):
        kind = "ExternalOutput" if name in dbg else "Internal"
        return nc.dram_tensor(name, list(shape), dt, kind=kind).ap()

    G.x = inp("x", [NL, D]); G.ctx = inp("ctx", [NCX, D]); G.cs = inp("cs", [128, 8, 2]); G.pos = inp("pos", [NL, D])
    G.w_mod = inp("w_mod", [DEPTH, D, 3 * D]); G.b_mod = inp("b_mod", [DEPTH, 3 * D]); G.norm_g = inp("norm_g", [DEPTH, D])
    G.w_in = inp("w_in", [DEPTH, D, DIN]); G.w_out = inp("w_out", [DEPTH, D, D]); G.final_g = inp("final_g", [D])
    G.f_w1 = inp("f_w1", [DEPTH, 33, 64]); G.f_w2 = inp("f_w2", [DEPTH, 64, 64]); G.f_w3 = inp("f_w3", [DEPTH, 64, 64])
    G.f_w4 = inp("f_w4", [DEPTH, 64, 1024])
    G.pp_d = inp("pp", [128, DEPTH, NPP]); G.bd = inp("bd", [DEPTH, 2, 2, 128, 4, 128]); G.fp_d = inp("fp", [64, DEPTH, 6])
    G.ident_d = inp("ident_bf", [128, 128], BF16); G.ones_d = inp("ones_bf", [128, 128], BF16)
    G.FA_d = inp("FA", [128, 128 * 128], BF16); G.GA_d = inp("GA", [128, 128 * 64], BF16); G.F128_d = inp("F128", [128, 384], BF16)
    G.zT = {NL: inp("zT_l", [33, NL]), NCX: inp("zT_c", [33, NCX])}
    G.tg = {NL: inp("tg_l", [128, NL]), NCX: inp("tg_c", [128, NCX])}
    G.ndel_d = inp("ndel", [128, 4])
    G.identf_d = inp("ident_f", [128, 128]); G.Fc_d = inp("Fc", [128, 2048], BF16); G.Gc_d = inp("Gc", [128, 1024], BF16)
    G.out = nc.dram_tensor("out", [NL, D], F32, kind="ExternalOutput").ap()

    G.modd = scr("modd", [DEPTH, 2, 3, D], F32)
    G.lat = NS(); G.cx = NS()
    for s, N, nm, idx in ((G.lat, NL, "l", 0), (G.cx, NCX, "c", 1)):
        s.N = N; s.idx = idx; s.nm = nm
        s.X = scr("X" + nm, [N, D], F32)
        s.P = scr("P" + nm, [DIN, N], BF16)
        s.HB = scr("HB" + nm, [DH, N], F32)
        s.YR = scr("YR" + nm, [DH, N], F32)
        s.YH = scr("YH" + nm, [DH, N], F32)
    G.KH = scr("KH", [4, 128, 16, 2, 512], F32)
    G.dbgc = scr("DBGC", [4, 128, 3, 512], F32) if "DBGC" in dbg else None
    G.UT = scr("UT", [DH, NL], BF16)

    with ExitStack() as es0:
        G.ident = kb.sb(es0, "ident", [128, 128], BF16)
        G.ones = kb.sb(es0, "ones", [128, 128], BF16)
        G.pp = kb.sb(es0, "pp", [128, DEPTH, NPP], F32)
        G.hl = kb.sb(es0, "hl", [128, DEPTH, 8], F32)
        G.hrb = kb.sb(es0, "hrb", [128, DEPTH, 16], F32)
        G.h0 = kb.sb(es0, "h0", [128, 4, 2], F32)
        G.rl1 = kb.sb(es0, "rl1", [128, 4], F32)
        G.ndel = kb.sb(es0, "ndel", [128, 4], F32)
        G.fpp = kb.sb(es0, "fpp", [64, DEPTH, 6], F32)
        G.frb = kb.sb(es0, "frb", [64, DEPTH, 3], F32)
        kb.dma("sp", G.ident[:], G.ident_d, W=[G.ident])
        kb.dma("sp", G.ones[:], G.ones_d, W=[G.ones])
        kb.dma("sp", G.pp[:], G.pp_d, W=[G.pp])
        kb.dma("sp", G.ndel[:], G.ndel_d, W=[G.ndel])
        kb.dma("sp", G.fpp[:], G.fp_d, W=[G.fpp])

        P = phases
        ph_prologue(G)
        kb.barrier()
        esW = None; preW = None
        for l in range(depth):
            last = l == DEPTH - 1
            if P is None or "inproj" in P:
                ph_inproj(G, l, preW)
                kb.barrier()
                if esW is not None:
                    esW.close(); esW = None; preW = None
            if P is None or "rglru" in P:
                ph_rglru(G, l, G.cx)
                kb.barrier()
                ph_rglru(G, l, G.lat)
                kb.barrier()
            if P is None or "hyena" in P or "hy_f" in P:
                ph_filter_lat(G, l)
                kb.barrier()
            if P is None or "hyena" in P or "hy_l" in P:
                ph_hyena_lat(G, l)
                kb.barrier()
            if (P is None or "hyena" in P or "hy_c" in P) and not last:
                ph_hyena_ctx(G, l)
                kb.barrier()
            if P is None or "out" in P:
                if not last:
                    ph_out(G, l, G.cx, False)
                    kb.barrier()
                if P is None and not last and l + 1 < depth:
                    esW = ExitStack()
                    preW = load_w_in(G, l + 1, esW)
                ph_out(G, l, G.lat, last)
                kb.barrier()
        kb.barrier()
    return nc


def ph_prologue(G):
    nc, kb = G.nc, G.kb
    with ExitStack() as es:
        t8 = kb.sb(es, "t8", [128, DEPTH, 8], F32)
        kb.op("act", lambda: nc.scalar.activation(out=t8[:], in_=G.pp[:, :, PP_LAM:PP_LAM + 8], func=AF.Exp, scale=-1.0), R=[G.pp], W=[t8])
        t8b = kb.sb(es, "t8b", [128, DEPTH, 8], F32)
        t8c = kb.sb(es, "t8c", [128, DEPTH, 8], F32)
        kb.op("dve", lambda: nc.vector.tensor_scalar(out=t8b[:], in0=t8[:], scalar1=2.0, scalar2=None, op0=ALU.add), R=[t8], W=[t8b])
        kb.op("dve", lambda: nc.vector.reciprocal(out=t8b[:], in_=t8b[:]), R=[t8b], W=[t8b])
        kb.op("dve", lambda: nc.vector.tensor_tensor(out=t8[:], in0=t8[:], in1=t8b[:], op=ALU.mult), R=[t8, t8b], W=[t8])
        kb.op("dve", lambda: nc.vector.tensor_tensor(out=t8b[:], in0=t8[:], in1=t8[:], op=ALU.mult), R=[t8], W=[t8b])
        kb.op("dve", lambda: nc.vector.tensor_scalar(out=t8c[:], in0=t8b[:], scalar1=0.2, scalar2=1.0 / 3.0, op0=ALU.mult, op1=ALU.add), R=[t8b], W=[t8c])
        kb.op("dve", lambda: nc.vector.tensor_tensor(out=t8c[:], in0=t8c[:], in1=t8b[:], op=ALU.mult), R=[t8c, t8b], W=[t8c])
        kb.op("dve", lambda: nc.vector.scalar_tensor_tensor(out=G.hl[:], in0=t8c[:], scalar=1.0, in1=t8[:], op0=ALU.add, op1=ALU.mult), R=[t8c, t8], W=[G.hl])
        kb.op("dve", lambda: nc.vector.tensor_scalar(out=G.hl[:], in0=G.hl[:], scalar1=-8.0, scalar2=None, op0=ALU.mult), R=[G.hl], W=[G.hl])
        kb.op("dve", lambda: nc.vector.tensor_scalar(out=G.hrb[:], in0=G.pp[:, :, PP_RB:PP_RB + 16], scalar1=0.5, scalar2=None, op0=ALU.mult), R=[G.pp], W=[G.hrb])
        kb.op("dve", lambda: nc.vector.tensor_tensor(out=G.frb[:], in0=G.fpp[:, :, 0:3], in1=G.fpp[:, :, 3:6], op=ALU.mult), R=[G.fpp], W=[G.frb])
        cs = kb.sb(es, "cs", [128, 8, 2], F32)
        S = kb.sb(es, "S", [128, 8, 2], BF16)
        kb.dma("sp", cs[:], G.cs, W=[cs])
        kb.op("act", lambda: nc.scalar.activation(out=S[:], in_=cs[:], func=AF.Silu), R=[cs], W=[S])
        wm = [kb.sb(es, f"wm{i}", [128, 8, 512], BF16) for i in range(2)]
        bm = kb.sb(es, "bm", [2, 3 * D], F32)
        ng = kb.sb(es, "ng", [2, D], F32)
        mods = kb.sb(es, "mods", [2, 3 * D], F32)
        pp_ = [kb.ps(es, f"pp{i}", [128, 512], F32) for i in range(2)]
        modd_t = Tl(G.modd)
        it = 0
        for l in range(DEPTH):
            kb.dma("sp", bm[:], G.b_mod[l].partition_broadcast(2), W=[bm])
            kb.dma("sp", ng[:], G.norm_g[l].partition_broadcast(2), W=[ng])
            for n in range(6):
                w = wm[it % 2]; p = pp_[it % 2]; it += 1
                kb.dma("pool", w[:], G.w_mod[l][:, n * 512:(n + 1) * 512].rearrange("(k p) n -> p k n", p=128), W=[w])
                for kc in range(8):
                    kb.op("pe", lambda: nc.tensor.matmul(p[0:2, :], lhsT=S[:, kc, :], rhs=w[:, kc, :], start=(kc == 0), stop=(kc == 7)),
                          R=[S, w], W=[p])
                kb.op("dve", lambda: nc.vector.tensor_tensor(out=mods[:, n * 512:(n + 1) * 512], in0=p[0:2, :], in1=bm[:, n * 512:(n + 1) * 512], op=ALU.add),
                      R=[p, bm], W=[mods])
            kb.op("dve", lambda: nc.vector.scalar_tensor_tensor(out=mods[:, D:2 * D], in0=mods[:, D:2 * D], scalar=1.0, in1=ng[:], op0=ALU.add, op1=ALU.mult),
                  R=[mods, ng], W=[mods])
            kb.dma("sp", G.modd[l].rearrange("s r d -> s (r d)"), mods[:], R=[mods], W=[modd_t])


def load_w_in(G, l, es):
    kb = G.kb
    W = [kb.sb(es, f"W{i}", [128, DIN], BF16) for i in range(8)]
    for i in range(8):
        kb.dma("pool", W[i][:], G.w_in[l, i * 128:(i + 1) * 128, :], W=[W[i]])
    return W


def ph_inproj(G, l, W=None):
    nc, kb = G.nc, G.kb
    with ExitStack() as es:
        if W is None:
            W = load_w_in(G, l, es)
        gsc = {}; sh = {}
        for s in (G.cx, G.lat):
            gsc[s.idx] = kb.sb(es, f"gsc{s.idx}", [128, D], F32)
            sh[s.idx] = kb.sb(es, f"sh{s.idx}", [128, D], F32)
            kb.dma("sp", sh[s.idx][:], G.modd[l, s.idx, 0].partition_broadcast(128), W=[sh[s.idx]])
            kb.dma("sp", gsc[s.idx][:], G.modd[l, s.idx, 1].partition_broadcast(128), W=[gsc[s.idx]])
        xin = [kb.sb(es, f"xin{i}", [128, 4, D], F32) for i in range(2)]
        ptile = [kb.sb(es, f"ptile{i}", [128, 4, D], F32) for i in range(2)] if l == 0 else None
        junk = kb.sb(es, "junk", [128, D], BF16)
        ss = [kb.sb(es, f"ss{i}", [128, 4], F32) for i in range(2)]
        tmp = [kb.sb(es, f"tmp{i}", [128, D], F32) for i in range(2)]
        xs_t = [kb.sbt(es, f"xs{i}", [128, 4, D], BF16) for i in range(2)]
        xs = [[Tl(xs_t[i][:, k, :]) for k in range(4)] for i in range(2)]
        xT_t = [kb.sbt(es, f"xT{i}", [128, 8, 512], BF16) for i in range(2)]
        xT = [[Tl(xT_t[i][:, dc, :]) for dc in range(8)] for i in range(2)]
        po_t = [kb.sbt(es, f"po{i}", [128, 4, 512], BF16) for i in range(3)]
        po = [[Tl(po_t[i][:, j, :]) for j in range(4)] for i in range(3)]
        psT = [kb.ps(es, f"psT{i}", [128, 512], BF16) for i in range(2)]
        psm = [kb.ps(es, f"psm{i}", [128, 512], F32) for i in range(4)]
        tiles = [(s, it) for s in (G.cx, G.lat) for it in range(s.N // min(512, s.N))]
        st = dict(ev=0, gi=0, tk=0)

        def stage1(idx):
            s, it = tiles[idx]
            TT = min(512, s.N); nb = TT // 128; t0 = it * TT
            xi = xin[idx % 2]; sq = ss[idx % 2]; xsi = xs[idx % 2]; xTi = xT[idx % 2]
            g_, h_ = gsc[s.idx], sh[s.idx]
            src = s.X if l > 0 else (G.x if s.idx == 0 else G.ctx)
            kb.dma("sp", xi[:, 0:nb, :], src[t0:t0 + TT, :].rearrange("(k p) d -> p k d", p=128), W=[xi])
            if l == 0:
                if s.idx == 0:
                    pz = ptile[idx % 2]
                    kb.dma("sp", pz[:, 0:nb, :], G.pos[t0:t0 + TT, :].rearrange("(k p) d -> p k d", p=128), W=[pz])
                    kb.op("pool", lambda: nc.gpsimd.tensor_tensor(out=xi[:, 0:nb, :], in0=xi[:, 0:nb, :], in1=pz[:, 0:nb, :], op=ALU.add), R=[xi, pz], W=[xi])
                kb.dma("sp", s.X[t0:t0 + TT, :].rearrange("(k p) d -> p k d", p=128), xi[:, 0:nb, :], R=[xi])
            kb.op("pool", lambda: nc.gpsimd.memset(sq[:], 0.0), W=[sq])
            for k in range(nb):
                kb.op("act", lambda: nc.scalar.activation(out=junk[:], in_=xi[:, k, :], func=AF.Square, accum_out=sq[:, k:k + 1]),
                      R=[xi], W=[junk, sq])
            kb.op("act", lambda: nc.scalar.activation(out=sq[:], in_=sq[:], func=AF.Sqrt, bias=EPS, scale=1.0 / D), R=[sq], W=[sq])
            kb.op("dve", lambda: nc.vector.reciprocal(out=sq[:], in_=sq[:]), R=[sq], W=[sq])
            for k in range(nb):
                tm = tmp[st["tk"] % 2]; st["tk"] += 1
                kb.op("dve", lambda: nc.vector.scalar_tensor_tensor(out=tm[:], in0=xi[:, k, :], scalar=sq[:, k:k + 1], in1=g_[:], op0=ALU.mult, op1=ALU.mult),
                      R=[xi, sq, g_], W=[tm])
                kb.op("pool", lambda: nc.gpsimd.tensor_tensor(out=xsi[k][:], in0=tm[:], in1=h_[:], op=ALU.add), R=[tm, h_], W=[xsi[k]])
            for dc in range(8):
                pt = psT[dc % 2]
                for k in range(nb):
                    kb.op("pe", lambda: nc.tensor.transpose(out=pt[:, k * 128:(k + 1) * 128], in_=xsi[k][:, dc * 128:(dc + 1) * 128], identity=G.ident[:]),
                          R=[xsi[k], G.ident], W=[pt])
                if dc % 2:
                    kb.op("act", lambda: nc.scalar.copy(out=xTi[dc][:, 0:TT], in_=pt[:, 0:TT]), R=[pt], W=[xTi[dc]])
                else:
                    kb.op("dve", lambda: nc.vector.tensor_copy(out=xTi[dc][:, 0:TT], in_=pt[:, 0:TT]), R=[pt], W=[xTi[dc]])

        def stage2(idx):
            s, it = tiles[idx]
            TT = min(512, s.N); t0 = it * TT
            xTi = xT[idx % 2]
            for cc in range(24):
                pm = psm[cc % 4]
                for dc in range(8):
                    kb.op("pe", lambda: nc.tensor.matmul(pm[:, 0:TT], lhsT=W[dc][:, cc * 128:(cc + 1) * 128], rhs=xTi[dc][:, 0:TT], start=(dc == 0), stop=(dc == 7)),
                          R=[W[dc], xTi[dc]], W=[pm])
                pg = po[st["gi"] % 3]; j = cc % 4
                if st["ev"] % 2:
                    kb.op("act", lambda: nc.scalar.copy(out=pg[j][:, 0:TT], in_=pm[:, 0:TT]), R=[pm], W=[pg[j]])
                else:
                    kb.op("dve", lambda: nc.vector.tensor_copy(out=pg[j][:, 0:TT], in_=pm[:, 0:TT]), R=[pm], W=[pg[j]])
                st["ev"] += 1
                if j == 3:
                    c0 = (cc - 3) * 128
                    kb.dma("act", s.P[c0:c0 + 512, t0:t0 + TT].rearrange("(k p) t -> p k t", p=128), po_t[st["gi"] % 3][:, :, 0:TT], R=pg)
                    st["gi"] += 1

        stage1(0)
        for i in range(len(tiles)):
            if i + 1 < len(tiles):
                stage1(i + 1)
            stage2(i)


def ph_rglru(G, l, s):
    nc, kb = G.nc, G.kb
    N = s.N; T2 = min(1024, N); nt = N // T2
    NS = 4; LA = 3
    SUB = min(512, T2); nsub = T2 // SUB
    is_ctx = s.idx == 1
    with ExitStack() as es:
        bdt = [[kb.sb(es, f"bd{d}{g}", [128, 4, 128], BF16) for g in range(2)] for d in range(2)]
        for d in range(2):
            for g in range(2):
                kb.dma("pool", bdt[d][g][:], G.bd[l, d, g], W=[bdt[d][g]])
        dg_t = [kb.sbt(es, f"rdg{i}", [128, 4, 128], BF16) for i in range(2)]
        dg = [Tl(t) for t in dg_t]
        pin = [kb.sb(es, f"pin{i}", [128, T2 + 3], BF16) for i in range(2)]
        xcf_t = [kb.sbt(es, f"xcf{i}", [128, N], F32) for i in range(2)]
        xcb_t = [kb.sbt(es, f"xcbf{i}", [128, N], BF16) for i in range(2)]
        xcf = [[Tl(xcf_t[j][:, i * T2:(i + 1) * T2]) for i in range(nt)] for j in range(2)]
        xcb = [[Tl(xcb_t[j][:, i * T2:(i + 1) * T2]) for i in range(nt)] for j in range(2)]
        thr = [kb.sb(es, f"thr{i}", [128, T2], F32) for i in range(NS)]
        thi = [kb.sb(es, f"thi{i}", [128, T2], F32) for i in range(NS)]
        sq = [kb.sb(es, f"sq{i}", [128, T2], F32) for i in range(NS)]
        hh = [kb.sb(es, f"hh{i}", [128, T2], F32) for i in range(NS)]
        hbl = [kb.sb(es, f"hbl{i}", [128, T2], F32) for i in range(NS)]
        st = kb.sb(es, "st", [128, 1], F32)
        psg = [kb.ps(es, f"psg{i}", [128, 512], F32) for i in range(6)]
        hbt = {}
        cnt = dict(pi=0, pn=0)
        items = []
        for ch in range(4):
            for d in (1, 0):
                order = list(range(nt - 1, -1, -1)) if d == 1 else list(range(nt))
                for j, ti in enumerate(order):
                    items.append((ch, d, ti, j == 0))
        conv_done = set()

        def conv(ch):
            cb = G.pp[:, l, PP_RCB + ch: PP_RCB + ch + 1]
            dgt = dg_t[ch % 2]; dgl = dg[ch % 2]
            for k in range(4):
                col = PP_RCW + ch * 4 + k
                kb.op("dve", lambda: nc.vector.tensor_scalar(out=dgt[:, k, :], in0=G.ident[:], scalar1=G.pp[:, l, col:col + 1], scalar2=None, op0=ALU.mult), R=[G.ident, G.pp], W=[dgl])
            for ti in range(nt):
                t0 = ti * T2
                p = pin[cnt["pn"] % 2]; cnt["pn"] += 1
                lo = max(t0 - 2, 0); hi = min(t0 + T2 + 1, N)
                if t0 == 0:
                    kb.op("pool", lambda: nc.gpsimd.memset(p[:, 0:2], 0.0), W=[p])
                if t0 + T2 == N:
                    kb.op("pool", lambda: nc.gpsimd.memset(p[:, T2 + 2:T2 + 3], 0.0), W=[p])
                kb.dma("sp", p[:, lo - (t0 - 2): hi - (t0 - 2)], s.P[2048 + ch * 128: 2048 + (ch + 1) * 128, lo:hi], W=[p])
                xf = xcf[ch % 2][ti]; xb = xcb[ch % 2][ti]
                for sb_ in range(nsub):
                    pc = psg[cnt["pi"] % 6]; cnt["pi"] += 1
                    for k in range(4):
                        kb.op("pe", lambda: nc.tensor.matmul(pc[:, 0:SUB], lhsT=dgt[:, k, :], rhs=p[:, sb_ * SUB + k: sb_ * SUB + k + SUB], start=(k == 0), stop=(k == 3)), R=[dgl, p], W=[pc])
                    sl = slice(sb_ * SUB, (sb_ + 1) * SUB)
                    kb.op("act", lambda: nc.scalar.activation(out=xf[:, sl], in_=pc[:, 0:SUB], func=AF.Identity, bias=cb, scale=1.0), R=[pc, G.pp], W=[xf])
                    kb.op("act", lambda: nc.scalar.activation(out=xb[:, sl], in_=pc[:, 0:SUB], func=AF.Identity, bias=cb, scale=1.0), R=[pc, G.pp], W=[xb])

        def stageA(i):
            ch, d, ti, first = items[i]
            if ch not in conv_done:
                conv(ch); conv_done.add(ch)
            slot = i % NS
            t0 = ti * T2
            x = xcf[ch % 2][ti]; xb = xcb[ch % 2][ti]; tr = thr[slot]; tq = thi[slot]; s2 = sq[slot]; hb = hbl[slot]
            hl = G.hl[:, l, d * 4 + ch: d * 4 + ch + 1]
            hba = G.hrb[:, l, (d * 2 + 0) * 4 + ch: (d * 2 + 0) * 4 + ch + 1]
            hbx = G.hrb[:, l, (d * 2 + 1) * 4 + ch: (d * 2 + 1) * 4 + ch + 1]
            for sb_ in range(nsub):
                sl = slice(sb_ * SUB, (sb_ + 1) * SUB)
                pr = psg[cnt["pi"] % 6]; pq = psg[(cnt["pi"] + 1) % 6]; cnt["pi"] += 2
                kb.op("pe", lambda: nc.tensor.matmul(pr[:, 0:SUB], lhsT=bdt[d][0][:, ch, :], rhs=xb[:, sl], start=True, stop=True), R=[bdt[d][0], xb], W=[pr])
                kb.op("pe", lambda: nc.tensor.matmul(pq[:, 0:SUB], lhsT=bdt[d][1][:, ch, :], rhs=xb[:, sl], start=True, stop=True), R=[bdt[d][1], xb], W=[pq])
                kb.op("act", lambda: nc.scalar.activation(out=tr[:, sl], in_=pr[:, 0:SUB], func=AF.Tanh, bias=hba, scale=0.5), R=[pr, G.hrb], W=[tr])
                kb.op("act", lambda: nc.scalar.activation(out=tq[:, sl], in_=pq[:, 0:SUB], func=AF.Tanh, bias=hbx, scale=0.5), R=[pq, G.hrb], W=[tq])
            kb.op("act", lambda: nc.scalar.activation(out=tr[:], in_=tr[:], func=AF.Exp, bias=hl, scale=hl), R=[tr, G.hl], W=[tr])
            kb.op("pool", lambda: nc.gpsimd.tensor_tensor(out=s2[:], in0=tr[:], in1=tr[:], op=ALU.mult), R=[tr], W=[s2])

        def stageA2(i):
            ch, d, ti, first = items[i]
            slot = i % NS
            x = xcf[ch % 2][ti]; tq = thi[slot]; s2 = sq[slot]
            kb.op("act", lambda: nc.scalar.activation(out=s2[:], in_=s2[:], func=AF.Sqrt, bias=1.0, scale=-1.0), R=[s2], W=[s2])
            kb.op("dve", lambda: nc.vector.scalar_tensor_tensor(out=tq[:], in0=tq[:], scalar=1.0, in1=x[:], op0=ALU.add, op1=ALU.mult), R=[tq, x], W=[tq])
            kb.op("dve", lambda: nc.vector.scalar_tensor_tensor(out=tq[:], in0=tq[:], scalar=0.5, in1=s2[:], op0=ALU.mult, op1=ALU.mult), R=[tq, s2], W=[tq])

        def stageB(i):
            ch, d, ti, first = items[i]
            slot = i % NS
            t0 = ti * T2
            tr = thr[slot]; tq = thi[slot]; h = hh[slot]; hb = hbl[slot]
            if d == 0:
                kb.dma("sp", hb[:], s.HB[ch * 128:(ch + 1) * 128, t0:t0 + T2], R=[hbt[(ch, ti)]], W=[hb])
            if first:
                init = 0.0 if is_ctx else G.h0[:, ch, (1 if d == 1 else 0):(2 if d == 1 else 1)]
                rinit = [] if is_ctx else [G.h0]
            else:
                init = st[:, 0:1]; rinit = [st]
            if d == 1:
                kb.op("dve", lambda: nc.vector.tensor_tensor_scan(out=h[:, ::-1], data0=tr[:, ::-1], data1=tq[:, ::-1], initial=init, op0=ALU.mult, op1=ALU.add),
                      R=[tr, tq] + rinit, W=[h])
                kb.op("dve", lambda: nc.vector.tensor_copy(out=st[:], in_=h[:, 0:1]), R=[h], W=[st])
                hbt[(ch, ti)] = Tl(None)
                kb.dma("sp", s.HB[ch * 128:(ch + 1) * 128, t0:t0 + T2], h[:], R=[h], W=[hbt[(ch, ti)]])
                if is_ctx and ti == 0:
                    kb.op("dve", lambda: nc.vector.tensor_copy(out=G.h0[:, ch, 1:2], in_=h[:, 0:1]), R=[h], W=[G.h0])
            else:
                kb.op("dve", lambda: nc.vector.tensor_tensor_scan(out=h[:], data0=tr[:], data1=tq[:], initial=init, op0=ALU.mult, op1=ALU.add),
                      R=[tr, tq] + rinit, W=[h])
                kb.op("dve", lambda: nc.vector.tensor_copy(out=st[:], in_=h[:, T2 - 1:T2]), R=[h], W=[st])
                if is_ctx and ti == nt - 1:
                    kb.op("dve", lambda: nc.vector.tensor_copy(out=G.h0[:, ch, 0:1], in_=h[:, T2 - 1:T2]), R=[h], W=[G.h0])
                kb.op("pool", lambda: nc.gpsimd.tensor_tensor(out=hb[:], in0=hb[:], in1=h[:], op=ALU.add), R=[hb, h], W=[hb])
                kb.dma("sp", s.YR[ch * 128:(ch + 1) * 128, t0:t0 + T2], hb[:], R=[hb])

        import os
        if os.environ.get("RG_NOLOOK"):
            for i in range(len(items)):
                stageA(i)
                stageA2(i)
                stageB(i)
        else:
            n = len(items)
            groups = [list(range(g, min(g + 2, n))) for g in range(0, n, 2)]
            for i in groups[0]:
                stageA(i)
            for gi, grp in enumerate(groups):
                if gi + 1 < len(groups):
                    for i in groups[gi + 1]:
                        stageA(i)
                for i in grp:
                    stageA2(i)
                for i in grp:
                    stageB(i)


def filter_mlp(G, l, L, es, h3T):
    nc, kb = G.nc, G.kb
    SUB = min(512, L)
    w1 = kb.sb(es, "fw1", [33, 64], F32); w2 = kb.sb(es, "fw2", [64, 64], F32); w3 = kb.sb(es, "fw3", [64, 64], F32)
    kb.dma("sp", w1[:], G.f_w1[l], W=[w1]); kb.dma("sp", w2[:], G.f_w2[l], W=[w2]); kb.dma("sp", w3[:], G.f_w3[l], W=[w3])
    zt = [kb.sb(es, f"zt{i}", [33, SUB], F32) for i in range(2)]
    pre = [kb.sb(es, f"fpre{i}", [64, SUB], F32) for i in range(2)]
    t1 = [kb.sb(es, f"ft1{i}", [64, SUB], F32) for i in range(2)]
    hA = [kb.sb(es, f"fhA{i}", [64, SUB], F32) for i in range(2)]
    pf = [kb.ps(es, f"pf{i}", [128, 512], F32) for i in range(2)]
    cnt = 0
    for j in range(L // SUB):
        z = zt[j % 2]
        kb.dma("sp", z[:], G.zT[L][:, j * SUB:(j + 1) * SUB], W=[z])
        src = z; wts = (w1, w2, w3); kdim = (33, 64, 64)
        for li in range(3):
            p = pf[cnt % 2]; pr = pre[cnt % 2]; tt = t1[cnt % 2]; cnt += 1
            kb.op("pe", lambda: nc.tensor.matmul(p[0:64, 0:SUB], lhsT=wts[li][0:kdim[li], :], rhs=src[0:kdim[li], 0:SUB], start=True, stop=True), R=[wts[li], src], W=[p])
            fr = G.fpp[:, l, 3 + li:4 + li]; frb = G.frb[:, l, li:li + 1]
            kb.op("dve", lambda: nc.vector.tensor_scalar(out=pr[:], in0=p[0:64, 0:SUB], scalar1=fr, scalar2=frb, op0=ALU.mult, op1=ALU.add), R=[p, G.fpp, G.frb], W=[pr])
            kb.op("dve", lambda: nc.vector.tensor_scalar(out=tt[:], in0=pr[:], scalar1=1.0 / TWO_PI, scalar2=MAGIC, op0=ALU.mult, op1=ALU.add), R=[pr], W=[tt])
            kb.op("dve", lambda: nc.vector.tensor_scalar(out=tt[:], in0=tt[:], scalar1=-MAGIC, scalar2=-TWO_PI, op0=ALU.add, op1=ALU.mult), R=[tt], W=[tt])
            kb.op("pool", lambda: nc.gpsimd.tensor_tensor(out=pr[:], in0=pr[:], in1=tt[:], op=ALU.add), R=[pr, tt], W=[pr])
            if li < 2:
                h = hA[li]
                kb.op("act", lambda: nc.scalar.activation(out=h[:], in_=pr[:], func=AF.Sin), R=[pr], W=[h])
                src = h
            else:
                kb.op("act", lambda: nc.scalar.activation(out=h3T[:, j * SUB:(j + 1) * SUB], in_=pr[:], func=AF.Sin), R=[pr], W=[h3T])


class FFT:
    def __init__(self, G, es, with_ga=False):
        nc, kb = G.nc, G.kb
        self.G = G
        self.FA_t = kb.sbt(es, "FA", [128, 16384], BF16)
        self.FA = Tl(self.FA_t)
        if with_ga:
            self.GA = kb.sb(es, "GA", [128, 8192], BF16)
            kb.dma("sp", self.GA[:], G.GA_d, W=[self.GA])
        self.F128 = kb.sb(es, "F128", [128, 384], BF16)
        kb.dma("sp", self.F128[:], G.F128_d, W=[self.F128])
        self.UA_t = kb.sbt(es, "UA", [128, 16384], BF16)
        self.C_t = kb.sbt(es, "C", [128, 16384], BF16)
        self.C2_t = kb.sbt(es, "C2", [128, 16384], BF16)
        self.psT = [kb.ps(es, f"fpsT{i}", [128, 1024], BF16) for i in range(2)]
        self.psA = [kb.ps(es, f"fpsA{i}", [128, 512], F32) for i in range(2)]
        self.psU = [kb.ps(es, f"fpsU{i}", [128, 512], F32) for i in range(4)]
        self.ev = 0
        self.fa_is = None

    def load_FA(self, which):
        G = self.G
        if self.fa_is == which:
            return
        if which == "FA":
            G.kb.dma("sp", self.FA_t[:, 0:8192], G.FA_d[:, 0:8192], W=[self.FA])
            G.kb.dma("sp", self.FA_t[:, 8192:16384], G.FA_d[:, 8192:16384], W=[self.FA])
        self.fa_is = which

    def evac(self, out, in_, R, W):
        nc, kb = self.G.nc, self.G.kb
        self.ev += 1
        if self.ev % 2:
            kb.op("act", lambda: nc.scalar.copy(out=out, in_=in_), R=R, W=W)
        else:
            kb.op("dve", lambda: nc.vector.tensor_copy(out=out, in_=in_), R=R, W=W)

    def forward(self, SRC, SRC_t, nA, consume):
        G = self.G; nc, kb = G.nc, G.kb
        ident = G.ident
        UA = [Tl(self.UA_t[:, g * 1024:(g + 1) * 1024]) for g in range(16)]
        C = [Tl(self.C_t[:, g * 512:(g + 1) * 512]) for g in range(32)]
        C2 = [Tl(None) for g in range(16)]
        self.load_FA("FA")
        W = nA * 128
        for g in range(16):
            pt = self.psT[g % 2]
            for j in range(8):
                b = 8 * g + j
                kb.op("pe", lambda: nc.tensor.transpose(out=pt[0:nA, j * 128:(j + 1) * 128], in_=SRC_t[:, b:W:128], identity=ident[:]), R=[SRC, ident], W=[pt])
            self.evac(UA[g][0:nA, :], pt[0:nA, :], [pt], [UA[g]])
        pool6 = self.psA + self.psU
        for g in range(32):
            pa = pool6[g % 6]
            for j in range(4):
                b = 4 * g + j
                kb.op("pe", lambda: nc.tensor.matmul(pa[:, j * 128:(j + 1) * 128], lhsT=self.FA_t[0:nA, b * 128:(b + 1) * 128], rhs=self.UA_t[0:nA, b * 128:(b + 1) * 128], start=True, stop=True),
                      R=[self.FA, UA[b // 8]], W=[pa])
            self.evac(C[g][:, :], pa[:, :], [pa], [C[g]])
        for g in range(16):
            pt = self.psT[g % 2]
            for j in range(8):
                c = 8 * g + j
                kb.op("pe", lambda: nc.tensor.transpose(out=pt[:, j * 128:(j + 1) * 128], in_=self.C_t[:, c:16384:128], identity=ident[:]), R=C + [ident], W=[pt])
            self.evac(self.C2_t[:, g * 1024:(g + 1) * 1024], pt[:, :], [pt], [C2[g]])
        C2k = self.C2_t[:, :].rearrange("p (c k) -> p c k", k=128)
        Cc = self.F128[:, 0:128]; Ss = self.F128[:, 128:256]; nSs = self.F128[:, 256:384]
        for g in range(16):
            k0 = 4 * g
            Cr = C2k[:, :, k0:k0 + 4]; Ci = C2k[:, :, 64 + k0:64 + k0 + 4]
            pr = self.psU[(2 * g) % 4]; pi = self.psU[(2 * g + 1) % 4]
            kb.op("pe", lambda: nc.tensor.matmul(pr[:, :], lhsT=Cc, rhs=Cr, start=True, stop=False), R=C2 + [self.F128], W=[pr])
            kb.op("pe", lambda: nc.tensor.matmul(pr[:, :], lhsT=Ss, rhs=Ci, start=False, stop=True), R=C2 + [self.F128], W=[pr])
            kb.op("pe", lambda: nc.tensor.matmul(pi[:, :], lhsT=Cc, rhs=Ci, start=True, stop=False), R=C2 + [self.F128], W=[pi])
            kb.op("pe", lambda: nc.tensor.matmul(pi[:, :], lhsT=nSs, rhs=Cr, start=False, stop=True), R=C2 + [self.F128], W=[pi])
            consume(g, pr, pi)


def ph_filter_lat(G, l):
    nc, kb = G.nc, G.kb
    L = NL
    with ExitStack() as es:
        h3T = kb.sb(es, "h3T", [64, L], BF16)
        with ExitStack() as es2:
            filter_mlp(G, l, L, es2, h3T)
        kb.barrier()
        F = FFT(G, es)
        w4b = kb.sb(es, "w4b", [64, 1024], BF16)
        kb.dma("pool", w4b[:], G.f_w4[l], W=[w4b])
        KT_t = kb.sbt(es, "KT", [128, 2 * L], BF16)
        KT = Tl(KT_t)
        tg = [kb.sb(es, f"tg{i}", [128, 512], F32) for i in range(2)]
        dec = [kb.sb(es, f"dec{i}", [128, 512], F32) for i in range(2)]
        absb = kb.sb(es, "absb", [128, 4096], BF16)
        l1p = kb.sb(es, "l1p", [128, 4], F32)
        kst = [kb.sb(es, f"kst{i}", [128, 2, 512], F32) for i in range(2)]
        pk = F.psA
        for ch in range(4):
            kb.op("pool", lambda: nc.gpsimd.memset(KT_t[:, L:L + 1], 0.0), W=[KT])
            for j in range(L // 512):
                t0 = j * 512
                tgt = tg[j % 2]; dc = dec[j % 2]
                kb.dma("sp", tgt[:], G.tg[L][:, t0:t0 + 512], W=[tgt])
                kb.op("act", lambda: nc.scalar.activation(out=dc[:], in_=tgt[:], func=AF.Exp, scale=G.ndel[:, ch:ch + 1]), R=[tgt, G.ndel], W=[dc])
                for half in range(2):
                    p = pk[half]
                    c0 = half * 512 + ch * 128
                    kb.op("pe", lambda: nc.tensor.matmul(p[:, :], lhsT=w4b[:, c0:c0 + 128], rhs=h3T[:, t0:t0 + 512], start=True, stop=True), R=[w4b, h3T], W=[p])
                    if half == 0:
                        kb.op("dve", lambda: nc.vector.tensor_tensor(out=KT_t[:, t0:t0 + 512], in0=p[:, :], in1=dc[:], op=ALU.mult), R=[p, dc], W=[KT])
                    elif t0 == 0:
                        kb.op("dve", lambda: nc.vector.scalar_tensor_tensor(out=KT_t[:, 2 * L - 1:2 * L - 512:-1], in0=p[:, 1:512], scalar=-1.0, in1=dc[:, 1:512], op0=ALU.mult, op1=ALU.mult),
                              R=[p, dc], W=[KT])
                    else:
                        kb.op("dve", lambda: nc.vector.scalar_tensor_tensor(out=KT_t[:, 2 * L - t0:2 * L - t0 - 512:-1], in0=p[:, :], scalar=-1.0, in1=dc[:], op0=ALU.mult, op1=ALU.mult),
                              R=[p, dc], W=[KT])
            for q in range(4):
                kb.op("act", lambda: nc.scalar.activation(out=absb[:], in_=KT_t[:, q * 4096:(q + 1) * 4096], func=AF.Abs), R=[KT], W=[absb])
                kb.op("dve", lambda: nc.vector.reduce_sum(out=l1p[:, q:q + 1], in_=absb[:], axis=AX.X), R=[absb], W=[l1p])
            kb.op("dve", lambda: nc.vector.reduce_sum(out=G.rl1[:, ch:ch + 1], in_=l1p[:], axis=AX.X), R=[l1p], W=[G.rl1])
            kb.op("dve", lambda: nc.vector.reciprocal(out=G.rl1[:, ch:ch + 1], in_=G.rl1[:, ch:ch + 1]), R=[G.rl1], W=[G.rl1])

            def consume(g, pr, pi):
                ks = kst[g % 2]
                kb.op("act", lambda: nc.scalar.copy(out=ks[:, 0, :], in_=pr[:, :]), R=[pr], W=[ks])
                kb.op("dve", lambda: nc.vector.tensor_copy(out=ks[:, 1, :], in_=pi[:, :]), R=[pi], W=[ks])
                kb.dma("sp", G.KH[ch, :, g, :, :], ks[:], R=[ks])

            F.forward(KT, KT_t, 128, consume)


def conv3(G, l, col, p, out, T):
    nc, kb = G.nc, G.kb
    w = lambda k: G.pp[:, l, PP_HCW + col * 3 + k: PP_HCW + col * 3 + k + 1]
    b = G.pp[:, l, PP_HCB + col: PP_HCB + col + 1]
    kb.op("dve", lambda: nc.vector.tensor_scalar(out=out[:, 0:T], in0=p[:, 0:T], scalar1=w(0), scalar2=b, op0=ALU.mult, op1=ALU.add), R=[p, G.pp], W=[out])
    for k in (1, 2):
        kb.op("dve", lambda: nc.vector.scalar_tensor_tensor(out=out[:, 0:T], in0=p[:, k:k + T], scalar=w(k), in1=out[:, 0:T], op0=ALU.mult, op1=ALU.add), R=[p, out, G.pp], W=[out])


def load_halo1(G, s, p, row0, t0, T):
    nc, kb = G.nc, G.kb
    N = s.N
    lo = max(t0 - 1, 0); hi = min(t0 + T + 1, N)
    if t0 == 0:
        kb.op("pool", lambda: nc.gpsimd.memset(p[:, 0:1], 0.0), W=[p])
    if t0 + T == N:
        kb.op("pool", lambda: nc.gpsimd.memset(p[:, T + 1:T + 2], 0.0), W=[p])
    kb.dma("sp", p[:, lo - (t0 - 1): hi - (t0 - 1)], s.P[row0:row0 + 128, lo:hi], W=[p])


def ph_hyena_lat(G, l):
    nc, kb = G.nc, G.kb
    s = G.lat
    L = NL; T = 512
    with ExitStack() as es:
        F = FFT(G, es, with_ga=True)
        U_t = kb.sbt(es, "U", [128, L], BF16)
        U = Tl(U_t)
        pt_ = [kb.sb(es, f"hp{i}", [128, T + 2], BF16) for i in range(4)]
        cf = [kb.sb(es, f"hc{i}", [128, T], F32) for i in range(4)]
        Kt = [kb.sb(es, f"Kt{i}", [128, 2, 512], F32) for i in range(2)]
        tm = [kb.sb(es, f"sm{i}", [128, 512], F32) for i in range(4)]
        Yb = [kb.sb(es, f"Yb{i}", [128, 2, 512], BF16) for i in range(2)]
        yo = [kb.sb(es, f"yo{i}", [128, T], F32) for i in range(2)]
        Cc = F.F128[:, 0:128]; Ss = F.F128[:, 128:256]; nSs = F.F128[:, 256:384]
        YT = F.UA_t[:, :].bitcast(F32)
        YTv = YT.rearrange("p (a b) -> p b a", b=128)
        dg_t = kb.sbt(es, "dg", [128, 9, 128], BF16)
        dg = Tl(dg_t)

        def conv_pe(i, ch, p, ps):
            for k in range(3):
                kb.op("pe", lambda: nc.tensor.matmul(ps[:, 0:T], lhsT=dg_t[:, i * 3 + k, :], rhs=p[:, k:k + T], start=(k == 0), stop=(k == 2)), R=[dg, p], W=[ps])

        for ch in range(4):
            for i in range(3):
                for k in range(3):
                    col = PP_HCW + (4 * i + ch) * 3 + k
                    kb.op("dve", lambda: nc.vector.tensor_scalar(out=dg_t[:, i * 3 + k, :], in0=G.ident[:], scalar1=G.pp[:, l, col:col + 1], scalar2=None, op0=ALU.mult), R=[G.ident, G.pp], W=[dg])
            for j in range(L // T):
                t0 = j * T
                p1 = pt_[(2 * j) % 4]; p2 = pt_[(2 * j + 1) % 4]; c1 = cf[j % 4]
                pa = F.psU[(2 * j) % 4]; pb = F.psU[(2 * j + 1) % 4]
                load_halo1(G, s, p1, 512 + ch * 128, t0, T)
                load_halo1(G, s, p2, 1024 + ch * 128, t0, T)
                conv_pe(1, ch, p1, pa)
                conv_pe(2, ch, p2, pb)
                kb.op("act", lambda: nc.scalar.activation(out=c1[:], in_=pa[:, 0:T], func=AF.Identity, bias=G.pp[:, l, PP_HCB + 4 + ch:PP_HCB + 5 + ch], scale=1.0), R=[pa, G.pp], W=[c1])
                kb.op("dve", lambda: nc.vector.scalar_tensor_tensor(out=U_t[:, t0:t0 + T], in0=pb[:, 0:T], scalar=G.pp[:, l, PP_HCB + 8 + ch:PP_HCB + 9 + ch], in1=c1[:], op0=ALU.add, op1=ALU.mult),
                      R=[pb, G.pp, c1], W=[U])
            Dg = [Tl(None) for g in range(16)]

            def consume(g, pr, pi):
                kt = Kt[g % 2]; yb = Yb[g % 2]
                k0 = 4 * g
                kb.dma("sp", kt[:], G.KH[ch, :, g, :, :], W=[kt])
                t1, t2, t3, t4 = tm
                kb.op("dve", lambda: nc.vector.tensor_tensor(out=t1[:], in0=pr[:, :], in1=kt[:, 0, :], op=ALU.mult), R=[pr, kt], W=[t1])
                kb.op("dve", lambda: nc.vector.tensor_tensor(out=t2[:], in0=pi[:, :], in1=kt[:, 1, :], op=ALU.mult), R=[pi, kt], W=[t2])
                kb.op("pool", lambda: nc.gpsimd.tensor_tensor(out=yb[:, 0, :], in0=t1[:], in1=t2[:], op=ALU.subtract), R=[t1, t2], W=[yb])
                kb.op("dve", lambda: nc.vector.tensor_tensor(out=t3[:], in0=pr[:, :], in1=kt[:, 1, :], op=ALU.mult), R=[pr, kt], W=[t3])
                kb.op("dve", lambda: nc.vector.tensor_tensor(out=t4[:], in0=pi[:, :], in1=kt[:, 0, :], op=ALU.mult), R=[pi, kt], W=[t4])
                kb.op("pool", lambda: nc.gpsimd.tensor_tensor(out=yb[:, 1, :], in0=t3[:], in1=t4[:], op=ALU.add), R=[t3, t4], W=[yb])
                pd0 = F.psA[0]; pd1 = F.psA[1]
                kb.op("pe", lambda: nc.tensor.matmul(pd0[:, :], lhsT=Cc, rhs=yb[:, 0, :], start=True, stop=False), R=[yb, F.F128], W=[pd0])
                kb.op("pe", lambda: nc.tensor.matmul(pd0[:, :], lhsT=nSs, rhs=yb[:, 1, :], start=False, stop=True), R=[yb, F.F128], W=[pd0])
                kb.op("pe", lambda: nc.tensor.matmul(pd1[:, :], lhsT=Ss, rhs=yb[:, 0, :], start=True, stop=False), R=[yb, F.F128], W=[pd1])
                kb.op("pe", lambda: nc.tensor.matmul(pd1[:, :], lhsT=Cc, rhs=yb[:, 1, :], start=False, stop=True), R=[yb, F.F128], W=[pd1])
                kb.op("act", lambda: nc.scalar.copy(out=F.C_t[:, k0 * 128:k0 * 128 + 512].rearrange("p (k c) -> p k c", c=128), in_=pd0[:, :].rearrange("p (c k) -> p k c", k=4)), R=[pd0], W=[Dg[g]])
                kb.op("act", lambda: nc.scalar.copy(out=F.C_t[:, (64 + k0) * 128:(64 + k0) * 128 + 512].rearrange("p (k c) -> p k c", c=128), in_=pd1[:, :].rearrange("p (c k) -> p k c", k=4)), R=[pd1], W=[Dg[g]])

            F.forward(U, U_t, 64, consume)
            D2 = [Tl(None) for g in range(16)]
            D2b = F.C2_t[:, :].rearrange("p (c b) -> p b c", b=128)
            for g in range(16):
                pt = F.psT[g % 2]
                for j in range(8):
                    c = 8 * g + j
                    kb.op("pe", lambda: nc.tensor.transpose(out=pt[:, j * 128:(j + 1) * 128], in_=F.C_t[:, c:16384:128], identity=G.ident[:]), R=Dg + [G.ident], W=[pt])
                F.evac(F.C2_t[:, g * 1024:(g + 1) * 1024], pt[:, :], [pt], [D2[g]])
            YTt = [Tl(None) for g in range(16)]
            for g in range(16):
                py = F.psU[g % 4]
                for j in range(8):
                    b = 8 * g + j
                    kb.op("pe", lambda: nc.tensor.matmul(py[:, j * 64:(j + 1) * 64], lhsT=D2b[:, b, :], rhs=F.GA[:, b * 64:(b + 1) * 64], start=True, stop=True),
                          R=D2 + [F.GA], W=[py])
                kb.op("act", lambda: nc.scalar.activation(out=YTv[:, 8 * g:8 * g + 8, :], in_=py[:, :].rearrange("p (b a) -> p b a", a=64), func=AF.Copy, scale=G.rl1[:, ch:ch + 1]),
                      R=[py, G.rl1], W=[YTt[g]])
            for j in range(L // T):
                t0 = j * T
                p0 = pt_[j % 4]; y = yo[j % 2]; pc = F.psA[j % 2]
                load_halo1(G, s, p0, ch * 128, t0, T)
                conv_pe(0, ch, p0, pc)
                kb.op("dve", lambda: nc.vector.scalar_tensor_tensor(out=y[:], in0=U_t[:, t0:t0 + T], scalar=G.pp[:, l, PP_HBIAS + ch:PP_HBIAS + ch + 1], in1=YT[:, t0:t0 + T], op0=ALU.mult, op1=ALU.add),
                      R=[U, G.pp] + YTt, W=[y])
                kb.op("dve", lambda: nc.vector.scalar_tensor_tensor(out=y[:], in0=pc[:, 0:T], scalar=G.pp[:, l, PP_HCB + ch:PP_HCB + ch + 1], in1=y[:], op0=ALU.add, op1=ALU.mult),
                      R=[pc, G.pp, y], W=[y])
                kb.dma("act", s.YH[ch * 128:(ch + 1) * 128, t0:t0 + T], y[:], R=[y])
            kb.barrier()


def ph_hyena_ctx(G, l):
    nc, kb = G.nc, G.kb
    s = G.cx; L = NCX
    with ExitStack() as es:
        h3T = kb.sb(es, "h3Tc", [64, L], BF16)
        with ExitStack() as es2:
            filter_mlp(G, l, L, es2, h3T)
        kb.barrier()
        w4b = kb.sb(es, "w4bc", [64, 1024], BF16)
        kb.dma("pool", w4b[:], G.f_w4[l], W=[w4b])
        tg = kb.sb(es, "tgc", [128, L], F32)
        kb.dma("sp", tg[:], G.tg[L], W=[tg])
        Fc = kb.sb(es, "Fc", [128, 4, 512], BF16); Gc = kb.sb(es, "Gc", [128, 4, 256], BF16); idf = kb.sb(es, "idf", [128, 128], F32)
        kb.dma("sp", Fc[:], G.Fc_d.rearrange("p (a b) -> p a b", a=4), W=[Fc])
        kb.dma("sp", Gc[:], G.Gc_d.rearrange("p (a b) -> p a b", a=4), W=[Gc])
        kb.dma("sp", idf[:], G.identf_d, W=[idf])
        dec = kb.sb(es, "decc", [128, L], F32)
        KT = [kb.sb(es, f"KTc{i}", [128, 2 * L], BF16) for i in range(4)]
        absb = kb.sb(es, "absc", [128, 2 * L], BF16)
        rl = kb.sb(es, "rlc", [128, 4], F32)
        pp_ = [kb.sb(es, f"cp{i}", [128, L + 2], BF16) for i in range(3)]
        x0c = [kb.sb(es, f"x0c{i}", [128, L], F32) for i in range(4)]
        cc_ = [kb.sb(es, f"cc{i}", [128, L], F32) for i in range(2)]
        uf = [kb.sb(es, f"uf{i}", [128, L], F32) for i in range(4)]
        ub = [kb.sb(es, f"ub{i}", [128, L], BF16) for i in range(4)]
        utok = kb.sb(es, "utok", [128, 2, 512], BF16)
        ktok = kb.sb(es, "ktok", [128, 4, 512], BF16)
        Kh = kb.sb(es, "Khc", [128, 4, 512], F32)
        Yh = kb.sb(es, "Yhc", [128, 4, 512], BF16)
        tm = [kb.sb(es, f"tmc{i}", [128, 512], F32) for i in range(4)]
        ytok = kb.sb(es, "ytok", [128, 2, 512], F32)
        yo = [kb.sb(es, f"yoc{i}", [128, L], F32) for i in range(2)]
        psb = [kb.ps(es, f"pcb{i}", [128, 512], BF16) for i in range(2)]
        psf = [kb.ps(es, f"pcf{i}", [128, 512], F32) for i in range(4)]
        for ch in range(4):
            kb.op("act", lambda: nc.scalar.activation(out=dec[:], in_=tg[:], func=AF.Exp, scale=G.ndel[:, ch:ch + 1]), R=[tg, G.ndel], W=[dec])
            K_ = KT[ch]
            kb.op("pool", lambda: nc.gpsimd.memset(K_[:, L:L + 1], 0.0), W=[K_])
            for half in range(2):
                p = psf[half]; c0 = half * 512 + ch * 128
                kb.op("pe", lambda: nc.tensor.matmul(p[:, 0:L], lhsT=w4b[:, c0:c0 + 128], rhs=h3T[:, 0:L], start=True, stop=True), R=[w4b, h3T], W=[p])
                if half == 0:
                    kb.op("dve", lambda: nc.vector.tensor_tensor(out=K_[:, 0:L], in0=p[:, 0:L], in1=dec[:], op=ALU.mult), R=[p, dec], W=[K_])
                else:
                    kb.op("dve", lambda: nc.vector.scalar_tensor_tensor(out=K_[:, 2 * L - 1:L:-1], in0=p[:, 1:L], scalar=-1.0, in1=dec[:, 1:L], op0=ALU.mult, op1=ALU.mult), R=[p, dec], W=[K_])
            kb.op("act", lambda: nc.scalar.activation(out=absb[:], in_=K_[:], func=AF.Abs), R=[K_], W=[absb])
            kb.op("dve", lambda: nc.vector.reduce_sum(out=rl[:, ch:ch + 1], in_=absb[:], axis=AX.X), R=[absb], W=[rl])
            load_halo1(G, s, pp_[0], ch * 128, 0, L)
            conv3(G, l, ch, pp_[0], x0c[ch], L)
            for i, row0 in ((1, 512 + ch * 128), (2, 1024 + ch * 128)):
                load_halo1(G, s, pp_[i], row0, 0, L)
                conv3(G, l, 4 * i + ch, pp_[i], cc_[i - 1], L)
            kb.op("dve", lambda: nc.vector.tensor_tensor(out=uf[ch][:], in0=cc_[0][:], in1=cc_[1][:], op=ALU.mult), R=[cc_[0], cc_[1]], W=[uf[ch]])
            kb.op("act", lambda: nc.scalar.copy(out=ub[ch][:], in_=uf[ch][:]), R=[uf[ch]], W=[ub[ch]])
            pt = psb[ch % 2]
            for tb in range(2):
                kb.op("pe", lambda: nc.tensor.transpose(out=pt[:, tb * 128:(tb + 1) * 128], in_=ub[ch][:, tb * 128:(tb + 1) * 128], identity=G.ident[:]), R=[ub[ch], G.ident], W=[pt])
            kb.op("act", lambda: nc.scalar.copy(out=utok[:, :, ch * 128:(ch + 1) * 128], in_=pt[:, 0:256].rearrange("p (a b) -> p a b", a=2)), R=[pt], W=[utok])
            pt2 = psb[(ch + 1) % 2]
            for nb_ in range(4):
                kb.op("pe", lambda: nc.tensor.transpose(out=pt2[:, nb_ * 128:(nb_ + 1) * 128], in_=K_[:, nb_ * 128:(nb_ + 1) * 128], identity=G.ident[:]), R=[K_, G.ident], W=[pt2])
            kb.op("dve", lambda: nc.vector.tensor_copy(out=ktok[:, :, ch * 128:(ch + 1) * 128], in_=pt2[:, :].rearrange("p (a b) -> p a b", a=4)), R=[pt2], W=[ktok])
        kb.op("dve", lambda: nc.vector.reciprocal(out=rl[:], in_=rl[:]), R=[rl], W=[rl])
        for m in range(4):
            p = psf[m]
            for nb_ in range(4):
                kb.op("pe", lambda: nc.tensor.matmul(p[:, :], lhsT=Fc[:, nb_, m * 128:(m + 1) * 128], rhs=ktok[:, nb_, :], start=(nb_ == 0), stop=(nb_ == 3)), R=[Fc, ktok], W=[p])
            if m % 2:
                kb.op("act", lambda: nc.scalar.copy(out=Kh[:, m, :], in_=p[:, :]), R=[p], W=[Kh])
            else:
                kb.op("dve", lambda: nc.vector.tensor_copy(out=Kh[:, m, :], in_=p[:, :]), R=[p], W=[Kh])
        for m in range(4):
            p = psf[m]
            for tb in range(2):
                kb.op("pe", lambda: nc.tensor.matmul(p[:, :], lhsT=Fc[:, tb, m * 128:(m + 1) * 128], rhs=utok[:, tb, :], start=(tb == 0), stop=(tb == 1)), R=[Fc, utok], W=[p])
        for q in range(2):
            pr = psf[q]; pi = psf[2 + q]
            t1, t2, t3, t4 = tm
            kb.op("dve", lambda: nc.vector.tensor_tensor(out=t1[:], in0=pr[:, :], in1=Kh[:, q, :], op=ALU.mult), R=[pr, Kh], W=[t1])
            kb.op("dve", lambda: nc.vector.tensor_tensor(out=t2[:], in0=pi[:, :], in1=Kh[:, 2 + q, :], op=ALU.mult), R=[pi, Kh], W=[t2])
            kb.op("pool", lambda: nc.gpsimd.tensor_tensor(out=Yh[:, q, :], in0=t1[:], in1=t2[:], op=ALU.subtract), R=[t1, t2], W=[Yh])
            kb.op("dve", lambda: nc.vector.tensor_tensor(out=t3[:], in0=pr[:, :], in1=Kh[:, 2 + q, :], op=ALU.mult), R=[pr, Kh], W=[t3])
            kb.op("dve", lambda: nc.vector.tensor_tensor(out=t4[:], in0=pi[:, :], in1=Kh[:, q, :], op=ALU.mult), R=[pi, Kh], W=[t4])
            kb.op("pool", lambda: nc.gpsimd.tensor_tensor(out=Yh[:, 2 + q, :], in0=t3[:], in1=t4[:], op=ALU.add), R=[t3, t4], W=[Yh])
        for tb in range(2):
            p = psf[tb]
            for kc in range(4):
                kb.op("pe", lambda: nc.tensor.matmul(p[:, :], lhsT=Gc[:, kc, tb * 128:(tb + 1) * 128], rhs=Yh[:, kc, :], start=(kc == 0), stop=(kc == 3)), R=[Gc, Yh], W=[p])
            kb.op("act", lambda: nc.scalar.copy(out=ytok[:, tb, :], in_=p[:, :]), R=[p], W=[ytok])
        for ch in range(4):
            p = psf[2 + ch % 2]
            for tb in range(2):
                kb.op("pe", lambda: nc.tensor.matmul(p[:, tb * 128:(tb + 1) * 128], lhsT=ytok[:, tb, ch * 128:(ch + 1) * 128], rhs=idf[:], start=True, stop=True), R=[ytok, idf], W=[p])
            y = yo[ch % 2]
            kb.op("dve", lambda: nc.vector.tensor_scalar(out=y[:], in0=p[:, 0:L], scalar1=rl[:, ch:ch + 1], scalar2=None, op0=ALU.mult), R=[p, rl], W=[y])
            kb.op("dve", lambda: nc.vector.scalar_tensor_tensor(out=y[:], in0=uf[ch][:], scalar=G.pp[:, l, PP_HBIAS + ch:PP_HBIAS + ch + 1], in1=y[:], op0=ALU.mult, op1=ALU.add),
                  R=[uf[ch], G.pp, y], W=[y])
            kb.op("dve", lambda: nc.vector.tensor_tensor(out=y[:], in0=y[:], in1=x0c[ch][:], op=ALU.mult), R=[y, x0c[ch]], W=[y])
            kb.dma("act", s.YH[ch * 128:(ch + 1) * 128, :], y[:], R=[y])


def ph_out(G, l, s, last):
    nc, kb = G.nc, G.kb
    N = s.N; TT = min(512, N); nt = N // TT; nb = TT // 128
    with ExitStack() as es:
        gbc = kb.sb(es, "gbc", [128, D], F32)
        kb.dma("sp", gbc[:], G.modd[l, s.idx, 2].partition_broadcast(128), W=[gbc])
        Wo_t = kb.sbt(es, "Wo", [128, 8, D], BF16)
        Wo = [Tl(Wo_t[:, k, :]) for k in range(8)]
        wtmp = [kb.sb(es, f"wtmp{i}", [128, D], F32) for i in range(2)]
        for k in range(8):
            wt = wtmp[k % 2]
            kb.dma("sp", wt[:], G.w_out[l, k * 128:(k + 1) * 128, :], W=[wt])
            kb.op("pool", lambda: nc.gpsimd.tensor_tensor(out=Wo[k][:, :], in0=wt[:], in1=gbc[:], op=ALU.mult), R=[wt, gbc], W=[Wo[k]])
        if last:
            fg = kb.sb(es, "fg", [128, D], F32)
            kb.dma("sp", fg[:], G.final_g.partition_broadcast(128), W=[fg])
            junk = kb.sb(es, "ojunk", [128, D], BF16)
            fss = [kb.sb(es, f"fss{i}", [128, 4], F32) for i in range(2)]
            xo = [kb.sb(es, f"xo{i}", [128, D], F32) for i in range(2)]
        yin = [kb.sb(es, f"yin{i}", [128, 8, TT], F32) for i in range(2)]
        zin = [kb.sb(es, f"zin{i}", [128, 8, TT], BF16) for i in range(2)]
        sqb = [kb.sb(es, f"sqb{i}", [128, TT], BF16) for i in range(2)]
        rstd = [kb.sb(es, f"rstd{i}", [128, 2, TT], F32) for i in range(2)]
        sz = [kb.sb(es, f"sz{i}", [128, TT], F32) for i in range(2)]
        t1 = [kb.sb(es, f"ot1{i}", [128, TT], F32) for i in range(2)]
        ym_t = [kb.sbt(es, f"ym{i}", [128, 8, TT], BF16) for i in range(2)]
        ym = [[Tl(ym_t[i][:, k, :]) for k in range(8)] for i in range(2)]
        xin = [kb.sb(es, f"oxin{i}", [128, 4, D], F32) for i in range(2)]
        pms = [kb.ps(es, f"pms{i}", [128, 512], F32) for i in range(2)]
        po = [kb.ps(es, f"po{i}", [128, 512], F32) for i in range(4)]
        st = dict(qi=0, pi=0)

        def stage1(it):
            t0 = it * TT
            yi = yin[it % 2]; zi = zin[it % 2]; rs = rstd[it % 2]; ymi = ym[it % 2]; xi = xin[it % 2]
            kb.dma("sp", yi[:, 0:4, :], s.YH[:, t0:t0 + TT].rearrange("(k p) t -> p k t", p=128), W=[yi])
            kb.dma("sp", yi[:, 4:8, :], s.YR[:, t0:t0 + TT].rearrange("(k p) t -> p k t", p=128), W=[yi])
            kb.dma("sp", zi[:, 0:4, :], s.P[1536:2048, t0:t0 + TT].rearrange("(k p) t -> p k t", p=128), W=[zi])
            kb.dma("sp", zi[:, 4:8, :], s.P[2560:3072, t0:t0 + TT].rearrange("(k p) t -> p k t", p=128), W=[zi])
            kb.dma("sp", xi[:, 0:nb, :], s.X[t0:t0 + TT, :].rearrange("(k p) d -> p k d", p=128), W=[xi])
            for grp in range(2):
                pm = pms[grp]
                for k in range(4):
                    sq = sqb[st["qi"] % 2]; st["qi"] += 1
                    kb.op("act", lambda: nc.scalar.activation(out=sq[:], in_=yi[:, grp * 4 + k, :], func=AF.Square), R=[yi], W=[sq])
                    kb.op("pe", lambda: nc.tensor.matmul(pm[:, 0:TT], lhsT=G.ones[:], rhs=sq[:], start=(k == 0), stop=(k == 3)), R=[G.ones, sq], W=[pm])
                kb.op("act", lambda: nc.scalar.activation(out=rs[:, grp, :], in_=pm[:, 0:TT], func=AF.Sqrt, bias=EPS, scale=1.0 / DH), R=[pm], W=[rs])
            kb.op("dve", lambda: nc.vector.reciprocal(out=rs[:], in_=rs[:]), R=[rs], W=[rs])
            for k8 in range(8):
                grp = k8 // 4
                z = sz[k8 % 2]; tt = t1[k8 % 2]
                gcol = (PP_GNH if grp == 0 else PP_GNR) + (k8 % 4)
                kb.op("act", lambda: nc.scalar.activation(out=z[:], in_=zi[:, k8, :], func=AF.Silu), R=[zi], W=[z])
                kb.op("dve", lambda: nc.vector.scalar_tensor_tensor(out=tt[:], in0=yi[:, k8, :], scalar=G.pp[:, l, gcol:gcol + 1], in1=rs[:, grp, :], op0=ALU.mult, op1=ALU.mult),
                      R=[yi, G.pp, rs], W=[tt])
                kb.op("pool", lambda: nc.gpsimd.tensor_tensor(out=ymi[k8][:, :], in0=tt[:], in1=z[:], op=ALU.mult), R=[tt, z], W=[ymi[k8]])

        def stage2(it):
            t0 = it * TT
            ymi = ym[it % 2]; xi = xin[it % 2]
            for tb in range(nb):
                for dn in range(2):
                    p = po[st["pi"] % 4]; st["pi"] += 1
                    for k8 in range(8):
                        kb.op("pe", lambda: nc.tensor.matmul(p[:, :], lhsT=ymi[k8][:, tb * 128:(tb + 1) * 128], rhs=Wo[k8][:, dn * 512:(dn + 1) * 512], start=(k8 == 0), stop=(k8 == 7)),
                              R=[ymi[k8], Wo[k8]], W=[p])
                    kb.op("dve", lambda: nc.vector.tensor_tensor(out=xi[:, tb, dn * 512:(dn + 1) * 512], in0=p[:, :], in1=xi[:, tb, dn * 512:(dn + 1) * 512], op=ALU.add), R=[p, xi], W=[xi])
            if not last:
                kb.dma("act", s.X[t0:t0 + TT, :].rearrange("(k p) d -> p k d", p=128), xi[:, 0:nb, :], R=[xi])
            else:
                fs = fss[it % 2]
                kb.op("pool", lambda: nc.gpsimd.memset(fs[:], 0.0), W=[fs])
                for tb in range(nb):
                    kb.op("act", lambda: nc.scalar.activation(out=junk[:], in_=xi[:, tb, :], func=AF.Square, accum_out=fs[:, tb:tb + 1]), R=[xi], W=[junk, fs])
                kb.op("act", lambda: nc.scalar.activation(out=fs[:], in_=fs[:], func=AF.Sqrt, bias=EPS, scale=1.0 / D), R=[fs], W=[fs])
                kb.op("dve", lambda: nc.vector.reciprocal(out=fs[:], in_=fs[:]), R=[fs], W=[fs])
                for tb in range(nb):
                    o = xo[tb % 2]
                    kb.op("dve", lambda: nc.vector.scalar_tensor_tensor(out=o[:], in0=xi[:, tb, :], scalar=fs[:, tb:tb + 1], in1=fg[:], op0=ALU.mult, op1=ALU.mult), R=[xi, fs, fg], W=[o])
                    kb.dma("act", G.out[t0 + tb * 128:t0 + (tb + 1) * 128, :], o[:], R=[o])

        stage1(0)
        for it in range(nt):
            if it + 1 < nt:
                stage1(it + 1)
            stage2(it)


_NC_CACHE = {}


def kernel(**inputs):
    inp = {k: np.asarray(v) for k, v in inputs.items()}
    if "nc" not in _NC_CACHE:
        _NC_CACHE["nc"] = build()
    nc = _NC_CACHE["nc"]
    in_maps = [pack_inputs(inp, b) for b in range(8)]
    res = run_bass_kernel_spmd(nc, in_maps, core_ids=list(range(8)))
    out = np.stack([np.asarray(res.results[b]["out"], dtype=np.float32) for b in range(8)], axis=0)
    return out
```
